# Optimizing a Trainium2 kernel written in Bass

```python
import math
import jax
import jax.numpy as jnp
from jax import lax
import numpy as np

D_MODEL = 1024
BATCH = 2
SEQ = 8192
DEPTH = 4
DEC_BATCH = 128
DEC_SEQ = 8
PAST_LEN = 2048
PAGE_SIZE = 128

N_MIXERS = 3
N_A_LAYERS = (DEPTH + 2) // 3
N_B_LAYERS = (DEPTH + 1) // 3
N_C_LAYERS = DEPTH // 3
DEEPNORM_ALPHA = (2 * DEPTH) ** 0.25
DEEPNORM_BETA = (8 * DEPTH) ** -0.25
LN_EPS = 1e-5
RMS_EPS = 1e-5
FFN_HALF = 0.5

D_FF = 2816

CMLP_CHUNK = 128
CMLP_DV = D_MODEL
CMLP_GROUPS = 8
CMLP_GDIM = CMLP_DV // CMLP_GROUPS

MOBA_HEAD_DIM = 128
MOBA_HEADS = D_MODEL // MOBA_HEAD_DIM
MOBA_BLOCK = 256
MOBA_TOPK = 3
MOBA_QBLOCK = 64
ROPE_THETA = 10000.0

SSM_D_INNER = 2 * D_MODEL
SSM_HEAD_DIM = 64
SSM_HEADS = SSM_D_INNER // SSM_HEAD_DIM
SSM_GROUPS = 8
SSM_HPG = SSM_HEADS // SSM_GROUPS
SSM_D_STATE = 128
SSM_CONV = 4
SSM_CONV_DIM = SSM_D_INNER + 2 * SSM_GROUPS * SSM_D_STATE
SSM_IN_DIM = SSM_D_INNER + SSM_CONV_DIM + SSM_HEADS
SSM_CHUNK = 128

kernel_name = "hybrid_gmlp_moba_ssd_macaron_deepnorm_step"


def layer_norm(x, g, b):
    xf = x.astype(jnp.float32)
    mu = jnp.mean(xf, axis=-1, keepdims=True)
    var = jnp.mean(jnp.square(xf - mu), axis=-1, keepdims=True)
    return ((xf - mu) * lax.rsqrt(var + LN_EPS) * g + b).astype(x.dtype)


def swiglu(x, w_in, w_out):
    gate, up = jnp.split(x @ w_in, 2, axis=-1)
    return (jax.nn.silu(gate) * up) @ w_out


def ffn_sublayer(h, w_in, w_out, g, b):
    return layer_norm(DEEPNORM_ALPHA * h + FFN_HALF * swiglu(h, w_in, w_out), g, b)


def rope(x, pos):
    half = MOBA_HEAD_DIM // 2
    inv = ROPE_THETA ** (-jnp.arange(half, dtype=jnp.float32) / half)
    ang = pos.astype(jnp.float32)[:, None] * inv[None, :]
    cos = jnp.cos(ang)[:, None, :]
    sin = jnp.sin(ang)[:, None, :]
    xf = x.astype(jnp.float32)
    x1, x2 = xf[..., :half], xf[..., half:]
    return jnp.concatenate([x1 * cos - x2 * sin, x2 * cos + x1 * sin], axis=-1).astype(x.dtype)


def chunk_mlp_mixer(x, w_in, ln_g, ln_b, w_s, b_s, w_out):
    bn, t, _ = x.shape
    u, v = jnp.split(jax.nn.gelu(x @ w_in, approximate=False), 2, axis=-1)
    v = layer_norm(v, ln_g, ln_b)
    lc = min(t, CMLP_CHUNK)
    causal = jnp.tril(jnp.ones((lc, lc), dtype=bool))
    w = jnp.where(causal, w_s[:, :lc, :lc], 0)
    vc = v.reshape(bn, t // lc, lc, CMLP_GROUPS, CMLP_GDIM)
    s = jnp.einsum("gts,bcsgd->bctgd", w, vc) + b_s[:, :lc].T[None, None, :, :, None]
    return (u * s.reshape(bn, t, CMLP_DV)) @ w_out, v


def moba_qkv(x, pos, w_qkv):
    bn, t, _ = x.shape
    qkv = (x @ w_qkv).reshape(bn, t, 3, MOBA_HEADS, MOBA_HEAD_DIM)
    return rope(qkv[:, :, 0], pos), rope(qkv[:, :, 1], pos), qkv[:, :, 2]


def moba_sequence(q, k, v, q_offset):
    tq = q.shape[0]
    n_keys = k.shape[0]
    nb = -(-n_keys // MOBA_BLOCK)
    pad = nb * MOBA_BLOCK - n_keys
    kb = jnp.pad(k, ((0, pad), (0, 0), (0, 0))).reshape(nb, MOBA_BLOCK, MOBA_HEADS, MOBA_HEAD_DIM).transpose(2, 0, 1, 3)
    vb = jnp.pad(v, ((0, pad), (0, 0), (0, 0))).reshape(nb, MOBA_BLOCK, MOBA_HEADS, MOBA_HEAD_DIM).transpose(2, 0, 1, 3)
    kmean = jnp.mean(kb.astype(jnp.float32), axis=2)
    topk = min(MOBA_TOPK, nb)
    qb = math.gcd(tq, MOBA_QBLOCK)
    scale = MOBA_HEAD_DIM ** -0.5
    head_ix = jnp.arange(MOBA_HEADS)[None, :, None]
    blk_ix = jnp.arange(nb)
    row_ix = jnp.arange(MOBA_BLOCK)

    def query_block(i):
        start = q_offset + i * qb
        q_blk = lax.dynamic_slice_in_dim(q, i * qb, qb, axis=0)
        qpos = start + jnp.arange(qb)
        own = start // MOBA_BLOCK
        gate = jnp.einsum("thd,hnd->thn", q_blk.astype(jnp.float32), kmean)
        gate = jnp.where(blk_ix < own, gate, -jnp.inf)
        gval, gidx = lax.top_k(gate, topk)
        sel_ok = jnp.isfinite(gval)
        k_own = lax.dynamic_index_in_dim(kb, own, axis=1, keepdims=False)
        v_own = lax.dynamic_index_in_dim(vb, own, axis=1, keepdims=False)
        s_own = jnp.einsum("thd,hrd->thr", q_blk, k_own).astype(jnp.float32) * scale
        s_own = jnp.where((own * MOBA_BLOCK + row_ix)[None, None, :] <= qpos[:, None, None], s_own, -jnp.inf)
        k_sel = kb[head_ix, gidx]
        v_sel = vb[head_ix, gidx]
        s_sel = jnp.einsum("thd,thjrd->thjr", q_blk, k_sel).astype(jnp.float32) * scale
        s_sel = jnp.where(sel_ok[..., None], s_sel, -jnp.inf)
        scores = jnp.concatenate([s_own, s_sel.reshape(qb, MOBA_HEADS, topk * MOBA_BLOCK)], axis=-1)
        p = jax.nn.softmax(scores, axis=-1).astype(v.dtype)
        p_own = p[..., :MOBA_BLOCK]
        p_sel = p[..., MOBA_BLOCK:].reshape(qb, MOBA_HEADS, topk, MOBA_BLOCK)
        return jnp.einsum("thr,hrd->thd", p_own, v_own) + jnp.einsum("thjr,thjrd->thd", p_sel, v_sel)

    out = lax.map(query_block, jnp.arange(tq // qb))
    return out.reshape(tq, MOBA_HEADS, MOBA_HEAD_DIM)


def moba_prompt_attend(q, k, v):
    return lax.map(lambda a: moba_sequence(a[0], a[1], a[2], 0), (q, k, v))


def moba_sample_attend(q, k_new, v_new, ck, cv, page_table):
    def one_sequence(a):
        q_s, k_s, v_s, pages = a
        k_past = ck[pages].reshape(-1, MOBA_HEADS, MOBA_HEAD_DIM)
        v_past = cv[pages].reshape(-1, MOBA_HEADS, MOBA_HEAD_DIM)
        return moba_sequence(q_s, jnp.concatenate([k_past, k_s], axis=0),
                             jnp.concatenate([v_past, v_s], axis=0), k_past.shape[0])
    return lax.map(one_sequence, (q, k_new, v_new, page_table))


def segsum(a):
    t = a.shape[-1]
    x = jnp.broadcast_to(a[..., :, None], a.shape + (t,))
    x = jnp.where(jnp.tril(jnp.ones((t, t), dtype=bool), -1), x, 0.0)
    cs = jnp.cumsum(x, axis=-2)
    return jnp.where(jnp.tril(jnp.ones((t, t), dtype=bool)), cs, -jnp.inf)


def ssd_scan(x, dt, a, bm, cm, init_state):
    bn, t = x.shape[:2]
    lc = math.gcd(t, SSM_CHUNK)
    nc = t // lc
    f32 = jnp.float32
    xdt = (x.astype(f32) * dt[..., None]).reshape(bn, nc, lc, SSM_GROUPS, SSM_HPG, SSM_HEAD_DIM)
    adt = (dt * a).reshape(bn, nc, lc, SSM_GROUPS, SSM_HPG).transpose(0, 1, 3, 4, 2)
    bc = bm.astype(f32).reshape(bn, nc, lc, SSM_GROUPS, SSM_D_STATE)
    cc = cm.astype(f32).reshape(bn, nc, lc, SSM_GROUPS, SSM_D_STATE)
    a_cs = jnp.cumsum(adt, axis=-1)
    lmat = jnp.exp(segsum(adt))
    cb = jnp.einsum("bclgn,bcsgn->bcgls", cc, bc)
    y_diag = jnp.einsum("bcgls,bcgels,bcsgep->bclgep", cb, lmat, xdt)
    decay_states = jnp.exp(a_cs[..., -1:] - a_cs)
    states = jnp.einsum("bclgn,bcgel,bclgep->bcgepn", bc, decay_states, xdt)
    init = init_state.astype(f32).reshape(bn, 1, SSM_GROUPS, SSM_HPG, SSM_HEAD_DIM, SSM_D_STATE)
    states = jnp.concatenate([init, states], axis=1)
    chunk_tot = jnp.pad(a_cs[..., -1], ((0, 0), (1, 0), (0, 0), (0, 0))).transpose(0, 2, 3, 1)
    decay_chunk = jnp.exp(segsum(chunk_tot))
    new_states = jnp.einsum("bgezc,bcgepn->bzgepn", decay_chunk, states)
    prev_states, final = new_states[:, :-1], new_states[:, -1]
    y_off = jnp.einsum("bclgn,bcgepn,bcgel->bclgep", cc, prev_states, jnp.exp(a_cs))
    y = (y_diag + y_off).reshape(bn, t, SSM_HEADS, SSM_HEAD_DIM)
    return y, final.reshape(bn, SSM_HEADS, SSM_HEAD_DIM, SSM_D_STATE)


def ssm_mixer(x, conv_state, ssm_state, w_in, w_conv, b_conv, dt_bias, a_log, d_skip, norm_g, w_out):
    bn, t, _ = x.shape
    zxbcdt = x @ w_in
    z = zxbcdt[..., :SSM_D_INNER]
    xbc_raw = zxbcdt[..., SSM_D_INNER:SSM_D_INNER + SSM_CONV_DIM]
    dt_raw = zxbcdt[..., SSM_D_INNER + SSM_CONV_DIM:]
    xp = jnp.concatenate([conv_state.astype(xbc_raw.dtype), xbc_raw], axis=1)
    conv = lax.conv_general_dilated(xp, w_conv[:, None, :].astype(xp.dtype), window_strides=(1,), padding="VALID",
                                    dimension_numbers=("NWC", "WIO", "NWC"), feature_group_count=SSM_CONV_DIM)
    xbc = jax.nn.silu(conv + b_conv)
    new_conv = xp[:, -(SSM_CONV - 1):]
    xs = xbc[..., :SSM_D_INNER].reshape(bn, t, SSM_HEADS, SSM_HEAD_DIM)
    bm = xbc[..., SSM_D_INNER:SSM_D_INNER + SSM_GROUPS * SSM_D_STATE].reshape(bn, t, SSM_GROUPS, SSM_D_STATE)
    cm = xbc[..., SSM_D_INNER + SSM_GROUPS * SSM_D_STATE:].reshape(bn, t, SSM_GROUPS, SSM_D_STATE)
    dt = jax.nn.softplus(dt_raw.astype(jnp.float32) + dt_bias.astype(jnp.float32))
    a = -jnp.exp(a_log.astype(jnp.float32))
    y, new_ssm = ssd_scan(xs, dt, a, bm, cm, ssm_state)
    y = y + xs.astype(jnp.float32) * d_skip.astype(jnp.float32)[:, None]
    y = y.reshape(bn, t, SSM_D_INNER) * jax.nn.silu(z.astype(jnp.float32))
    yg = y.reshape(bn, t, SSM_GROUPS, SSM_D_INNER // SSM_GROUPS)
    yg = yg * lax.rsqrt(jnp.mean(yg * yg, axis=-1, keepdims=True) + RMS_EPS)
    y = (yg.reshape(bn, t, SSM_D_INNER) * norm_g).astype(x.dtype)
    return y @ w_out, new_conv, new_ssm.astype(ssm_state.dtype)


def setup_inputs(seed: int = 0) -> dict:
    key = jax.random.key(seed)
    ks = jax.random.split(key, 32)
    f32 = jnp.float32

    def nrm(k, shape, s):
        return jax.random.normal(k, shape, f32) * s

    n_pages = PAST_LEN // PAGE_SIZE
    n_used = DEC_BATCH * n_pages
    n_pool = n_used + (n_used + 3) // 4
    page_table = jax.random.permutation(ks[0], n_pool)[:n_used].reshape(DEC_BATCH, n_pages).astype(jnp.int32)
    dt0 = jnp.exp(jax.random.uniform(ks[1], (N_C_LAYERS, SSM_HEADS), f32, math.log(1e-3), math.log(1e-1)))
    return {
        "x_prompt": nrm(ks[2], (BATCH, SEQ, D_MODEL), 1.0),
        "x_sample": nrm(ks[3], (DEC_BATCH, DEC_SEQ, D_MODEL), 1.0),
        "cache_k": nrm(ks[4], (N_B_LAYERS, n_pool, PAGE_SIZE, MOBA_HEADS, MOBA_HEAD_DIM), 1.0),
        "cache_v": nrm(ks[5], (N_B_LAYERS, n_pool, PAGE_SIZE, MOBA_HEADS, MOBA_HEAD_DIM), 1.0),
        "state_conv": nrm(ks[6], (N_C_LAYERS, DEC_BATCH, SSM_CONV - 1, SSM_CONV_DIM), 1.0),
        "state_ssm": nrm(ks[7], (N_C_LAYERS, DEC_BATCH, SSM_HEADS, SSM_HEAD_DIM, SSM_D_STATE), 0.1),
        "page_table": page_table,
        "ln_g": 1.0 + nrm(ks[8], (DEPTH, 3, D_MODEL), 0.02),
        "ln_b": nrm(ks[9], (DEPTH, 3, D_MODEL), 0.02),
        "ffn_w_in": nrm(ks[10], (DEPTH, 2, D_MODEL, 2 * D_FF), D_MODEL ** -0.5),
        "ffn_w_out": nrm(ks[11], (DEPTH, 2, D_FF, D_MODEL), DEEPNORM_BETA * D_FF ** -0.5),
        "cmlp_w_in": nrm(ks[12], (N_A_LAYERS, D_MODEL, 2 * CMLP_DV), D_MODEL ** -0.5),
        "cmlp_ln_g": 1.0 + nrm(ks[13], (N_A_LAYERS, CMLP_DV), 0.02),
        "cmlp_ln_b": nrm(ks[14], (N_A_LAYERS, CMLP_DV), 0.02),
        "cmlp_w_s": nrm(ks[15], (N_A_LAYERS, CMLP_GROUPS, CMLP_CHUNK, CMLP_CHUNK), CMLP_CHUNK ** -0.5),
        "cmlp_b_s": 1.0 + nrm(ks[16], (N_A_LAYERS, CMLP_GROUPS, CMLP_CHUNK), 0.02),
        "cmlp_w_out": nrm(ks[17], (N_A_LAYERS, CMLP_DV, D_MODEL), DEEPNORM_BETA * CMLP_DV ** -0.5),
        "moba_w_qkv": nrm(ks[18], (N_B_LAYERS, D_MODEL, 3 * MOBA_HEADS * MOBA_HEAD_DIM), D_MODEL ** -0.5),
        "moba_w_out": nrm(ks[19], (N_B_LAYERS, MOBA_HEADS * MOBA_HEAD_DIM, D_MODEL), DEEPNORM_BETA * (MOBA_HEADS * MOBA_HEAD_DIM) ** -0.5),
        "ssm_w_in": nrm(ks[20], (N_C_LAYERS, D_MODEL, SSM_IN_DIM), D_MODEL ** -0.5),
        "ssm_w_conv": nrm(ks[21], (N_C_LAYERS, SSM_CONV, SSM_CONV_DIM), SSM_CONV ** -0.5),
        "ssm_b_conv": nrm(ks[22], (N_C_LAYERS, SSM_CONV_DIM), 0.02),
        "ssm_dt_bias": dt0 + jnp.log(-jnp.expm1(-dt0)),
        "ssm_a_log": jnp.log(jax.random.uniform(ks[23], (N_C_LAYERS, SSM_HEADS), f32, 1.0, 16.0)),
        "ssm_d": 1.0 + nrm(ks[24], (N_C_LAYERS, SSM_HEADS), 0.02),
        "ssm_norm_g": 1.0 + nrm(ks[25], (N_C_LAYERS, SSM_D_INNER), 0.02),
        "ssm_w_out": nrm(ks[26], (N_C_LAYERS, SSM_D_INNER, D_MODEL), DEEPNORM_BETA * SSM_D_INNER ** -0.5),
    }


def reference(x_prompt, x_sample, cache_k, cache_v, state_conv, state_ssm, page_table,
              ln_g, ln_b, ffn_w_in, ffn_w_out,
              cmlp_w_in, cmlp_ln_g, cmlp_ln_b, cmlp_w_s, cmlp_b_s, cmlp_w_out,
              moba_w_qkv, moba_w_out,
              ssm_w_in, ssm_w_conv, ssm_b_conv, ssm_dt_bias, ssm_a_log, ssm_d, ssm_norm_g, ssm_w_out):
    n_prompt, t_prompt, _ = x_prompt.shape
    n_sample, t_sample, _ = x_sample.shape
    past_len = page_table.shape[1] * PAGE_SIZE
    pos_prompt = jnp.arange(t_prompt)
    pos_sample = past_len + jnp.arange(t_sample)
    hp, hs = x_prompt, x_sample
    cmlp_v_sample = []
    k_prompt, v_prompt, k_sample, v_sample = [], [], [], []
    conv_prompt, ssm_prompt, conv_sample, ssm_sample = [], [], [], []
    for i in range(DEPTH):
        hp = ffn_sublayer(hp, ffn_w_in[i, 0], ffn_w_out[i, 0], ln_g[i, 0], ln_b[i, 0])
        hs = ffn_sublayer(hs, ffn_w_in[i, 0], ffn_w_out[i, 0], ln_g[i, 0], ln_b[i, 0])
        kind, j = i % N_MIXERS, i // N_MIXERS
        if kind == 0:
            a_par = (cmlp_w_in[j], cmlp_ln_g[j], cmlp_ln_b[j], cmlp_w_s[j], cmlp_b_s[j], cmlp_w_out[j])
            mix_p, _ = chunk_mlp_mixer(hp, *a_par)
            mix_s, v_rows = chunk_mlp_mixer(hs, *a_par)
            cmlp_v_sample.append(v_rows)
        elif kind == 1:
            q, k, v = moba_qkv(hp, pos_prompt, moba_w_qkv[j])
            mix_p = moba_prompt_attend(q, k, v).reshape(n_prompt, t_prompt, -1) @ moba_w_out[j]
            k_prompt.append(k)
            v_prompt.append(v)
            q, k, v = moba_qkv(hs, pos_sample, moba_w_qkv[j])
            mix_s = moba_sample_attend(q, k, v, cache_k[j], cache_v[j], page_table).reshape(n_sample, t_sample, -1) @ moba_w_out[j]
            k_sample.append(k)
            v_sample.append(v)
        else:
            c_par = (ssm_w_in[j], ssm_w_conv[j], ssm_b_conv[j], ssm_dt_bias[j], ssm_a_log[j], ssm_d[j], ssm_norm_g[j], ssm_w_out[j])
            zero_conv = jnp.zeros((n_prompt, SSM_CONV - 1, SSM_CONV_DIM), hp.dtype)
            zero_ssm = jnp.zeros((n_prompt, SSM_HEADS, SSM_HEAD_DIM, SSM_D_STATE), state_ssm.dtype)
            mix_p, cp, sp = ssm_mixer(hp, zero_conv, zero_ssm, *c_par)
            mix_s, cs, ss = ssm_mixer(hs, state_conv[j], state_ssm[j], *c_par)
            conv_prompt.append(cp)
            ssm_prompt.append(sp)
            conv_sample.append(cs)
            ssm_sample.append(ss)
        hp = layer_norm(DEEPNORM_ALPHA * hp + mix_p, ln_g[i, 1], ln_b[i, 1])
        hs = layer_norm(DEEPNORM_ALPHA * hs + mix_s, ln_g[i, 1], ln_b[i, 1])
        hp = ffn_sublayer(hp, ffn_w_in[i, 1], ffn_w_out[i, 1], ln_g[i, 2], ln_b[i, 2])
        hs = ffn_sublayer(hs, ffn_w_in[i, 1], ffn_w_out[i, 1], ln_g[i, 2], ln_b[i, 2])
    return (hp, hs, jnp.stack(cmlp_v_sample), jnp.stack(k_prompt), jnp.stack(v_prompt), jnp.stack(k_sample), jnp.stack(v_sample), jnp.stack(conv_prompt), jnp.stack(ssm_prompt), jnp.stack(conv_sample), jnp.stack(ssm_sample))
```

```python
import os
import numpy as np
import concourse.bass as bass
import concourse.mybir as mybir
from concourse.bass_utils import run_bass_kernel_spmd

F32 = mybir.dt.float32
BF16 = mybir.dt.bfloat16
I32 = mybir.dt.int32
ALU = mybir.AluOpType
AF = mybir.ActivationFunctionType
AX = mybir.AxisListType

D = 1024
DFF = 2816
NFC = DFF // 128
DEPTH = 4
ALPHA = (2 * DEPTH) ** 0.25
LN_EPS = 1e-5
SEQ = 8192
NSEQ_S = 16
TS = 8


class Sched:
    def __init__(self, nc, stack):
        self.nc = nc
        self.eng = {}
        for name, e in (("pe", nc.tensor), ("act", nc.scalar), ("dve", nc.vector),
                        ("pool", nc.gpsimd), ("sp", nc.sync)):
            sem = stack.enter_context(nc.semaphore("sem_" + name))
            self.eng[name] = dict(name=name, e=e, sem=sem, count=0, waited={}, ops=[])
        self.NDS = 32
        self.dsem = [stack.enter_context(nc.semaphore("dsem%d" % i)) for i in range(self.NDS)]
        self.dval = [0] * self.NDS
        self.dnext2 = [0, 0]
        self.res = {}
        self.n_ins = 0

    def _deps(self, R, W):
        toks = []
        for k in R:
            r = self.res.get(k)
            if r is not None and r["w"] is not None:
                toks.append(r["w"])
        for k in W:
            r = self.res.get(k)
            if r is not None:
                if r["w"] is not None:
                    toks.append(r["w"])
                toks.extend(r["r"].values())
        return toks

    def _wait_list(self, E, toks):
        waits = []
        for (sem, val, owner) in toks:
            if owner == E["name"] and owner == "pe":
                continue
            if E["waited"].get(owner, 0) < val:
                E["waited"][owner] = val
                waits.append((sem, val))
        return waits

    def _mark(self, R, W, tok):
        for k in R:
            r = self.res.setdefault(k, dict(w=None, r={}))
            r["r"][tok[2]] = tok
        for k in W:
            self.res[k] = dict(w=tok, r={})

    def op(self, engname, fn, R=(), W=()):
        E = self.eng[engname]
        if engname in ("act", "dve"):
            W = list(W) + [k for k in R if isinstance(k, str) and len(k) == 2 and k[0] == "P" and k[1].isdigit()]
        waits = self._wait_list(E, self._deps(R, W))
        E["count"] += 1
        tok = (E["sem"], E["count"], engname)
        sem = E["sem"]

        def run(e, waits=waits, fn=fn, sem=sem):
            for (s, v) in waits:
                e.wait_ge(s, v)
            fn(e).then_inc(sem, 1)
        E["ops"].append(run)
        self._mark(R, W, tok)
        self.n_ins += 1 + len(waits)

    def dma(self, qname, out, in_, R=(), W=(), **kw):
        E = self.eng[qname]
        half = self.NDS // 2
        qi = 0 if qname == "sp" else 1
        i = qi * half + self.dnext2[qi]
        self.dnext2[qi] = (self.dnext2[qi] + 1) % half
        toks = self._deps(R, W)
        if self.dval[i] > 0:
            toks.append((self.dsem[i], self.dval[i], "d%d" % i))
        waits = self._wait_list(E, toks)
        self.dval[i] += 16
        tok = (self.dsem[i], self.dval[i], "d%d" % i)
        ds = self.dsem[i]

        def run(e, waits=waits, ds=ds, out=out, in_=in_, kw=kw):
            for (s, v) in waits:
                e.wait_ge(s, v)
            e.dma_start(out=out, in_=in_, **kw).then_inc(ds, 16)
        E["ops"].append(run)
        self._mark(R, W, tok)
        self.n_ins += 1 + len(waits)

    def idma(self, out, in_, off_ap, R=(), W=()):
        E = self.eng["pool"]
        half = self.NDS // 2
        i = half + self.dnext2[1]
        self.dnext2[1] = (self.dnext2[1] + 1) % half
        toks = self._deps(R, W)
        if self.dval[i] > 0:
            toks.append((self.dsem[i], self.dval[i], "d%d" % i))
        waits = self._wait_list(E, toks)
        self.dval[i] += 16
        tok = (self.dsem[i], self.dval[i], "d%d" % i)
        ds = self.dsem[i]

        def run(e, waits=waits, ds=ds):
            for (s_, v) in waits:
                e.wait_ge(s_, v)
            e.indirect_dma_start(out=out, out_offset=None, in_=in_,
                                 in_offset=bass.IndirectOffsetOnAxis(ap=off_ap, axis=0)).then_inc(ds, 16)
        E["ops"].append(run)
        self._mark(R, W, tok)
        self.n_ins += 1 + len(waits)

    def finish(self, final_keys):
        toks = [(self.dsem[i], self.dval[i], "d%d" % i) for i in range(self.NDS) if self.dval[i] > 0]
        E = self.eng["sp"]
        waits = self._wait_list(E, toks)

        def run(e, waits=waits):
            for (s, v) in waits:
                e.wait_ge(s, v)
        E["ops"].append(run)

    def replay(self, block):
        def mk(name):
            ops = self.eng[name]["ops"]

            def f(e):
                for o in ops:
                    o(e)
            return f
        block.tensor(mk("pe"))
        block.scalar(mk("act"))
        block.vector(mk("dve"))
        block.gpsimd(mk("pool"))
        block.sync(mk("sp"))


def build(NT=16, LAYERS=4, SAMPLE=True):
    from contextlib import ExitStack
    nc = bass.Bass("TRN2", target_bir_lowering=False)
    st = ExitStack()

    def din(name, shape, dt=F32):
        return nc.dram_tensor(name, list(shape), dt, kind="ExternalInput").ap()

    def dout(name, shape, dt=F32):
        return nc.dram_tensor(name, list(shape), dt, kind="ExternalOutput").ap()

    xp = din("xp", [SEQ, D])
    xs = din("xs", [128, D])
    ln_g = din("ln_g", [12, D])
    ln_b = din("ln_b", [12, D])
    ffn_w_in = din("ffn_w_in", [8, D, 2 * DFF])
    ffn_w_out = din("ffn_w_out", [8, DFF, D])
    cst = din("cst", [128, 1024])
    cmlp_w_in = din("cmlp_w_in", [2, D, 2 * D])
    cmlp_ln_g = din("cmlp_ln_g", [2, D])
    cmlp_ln_b = din("cmlp_ln_b", [2, D])
    cmlp_w_s = din("cmlp_w_s", [2, 8, 128, 128])
    cmlp_b_s = din("cmlp_b_s", [2, 8, 128])
    cmlp_w_out = din("cmlp_w_out", [2, D, D])
    cv_s = dout("cv_s", [2, 128, D])
    moba_w_qkv = din("moba_w_qkv", [D, 3 * D])
    moba_w_out = din("moba_w_out", [D, D])
    ropec = din("ropec", [SEQ + 128, 128])
    cstb = din("cstb", [128, 6144])
    k_p = dout("k_p", [SEQ, D])
    v_p = dout("v_p", [SEQ, D])
    k_s = dout("k_s", [128, D])
    v_s = dout("v_s", [128, D])
    cache_k = din("cache_k", [2560 * 128, D])
    cache_v = din("cache_v", [2560 * 128, D])
    page_table = din("page_table", [16, 16], I32)
    sel8d = din("sel8d", [8, 1024])
    ssm_w_in = din("ssm_w_in", [D, 6176])
    ssm_w_conv = din("ssm_w_conv", [4, 4096])
    ssm_b_conv = din("ssm_b_conv", [1, 4096])
    ssm_dt_bias = din("ssm_dt_bias", [1, 32])
    ssm_a_log = din("ssm_a_log", [1, 32])
    ssm_d = din("ssm_d", [1, 32])
    ssm_norm_g = din("ssm_norm_g", [1, 2048])
    ssm_w_out = din("ssm_w_out", [2048, D])
    state_conv = din("state_conv", [48, 4096])
    state_ssm = din("state_ssm", [16, 2048, 128])
    cst2 = din("cst2", [128, 768])
    conv_p = dout("conv_p", [3, 4096])
    ssm_p = dout("ssm_p", [2048, 128])
    conv_s = dout("conv_s", [48, 4096])
    ssm_s = dout("ssm_s", [16, 2048, 128])
    KT_hist = nc.dram_tensor("KT_hist", [8, 128, SEQ], BF16).ap()
    V_hist = nc.dram_tensor("V_hist", [SEQ, D], BF16).ap()
    y_p = dout("y_p", [SEQ, D])
    y_s = dout("y_s", [128, D])

    def sb(name, shape, dt):
        return st.enter_context(nc.sbuf_tensor(name, list(shape), dt))

    def ps(name, shape, dt=F32):
        return st.enter_context(nc.psum_tensor(name, list(shape), dt))

    S = Sched(nc, st)
    DBG = os.environ.get("MK_DBG", "")
    MIX = os.environ.get("MK_MIX", "1") == "1"

    hT = sb("hT", [128, 8, 512], F32)
    hb = sb("hb", [128, 8, 512], BF16)
    aT = sb("aT", [128, NFC, 512], BF16)
    wA = sb("wA", [128, 2, 8, 1024], BF16)
    wB = sb("wB", [128, NFC, 1024], BF16)
    xin = sb("xin", [128, 2, D], F32)
    tmp = sb("tmp", [128, 2, 512], F32)
    zb = sb("zb", [128, 2, 512], BF16)
    stat = sb("stat", [128, 4, 512], F32)
    cs = sb("cs", [128, 1024], F32)
    ident_b = sb("ident_b", [128, 128], BF16)
    onesb = sb("onesb", [128, 128], BF16)
    lng = sb("lng", [128, 96], F32)
    lnb = sb("lnb", [128, 96], F32)
    P = [ps("ps%d" % i, [128, 512]) for i in range(8)]

    ident = cs[:, 0:128]

    if os.environ.get("MK_PRECAST", "1") == "1":
        def precast(name, src, rows_per=128):
            shp = list(src.shape)
            dst = nc.dram_tensor(name + "_b", shp, BF16).ap()
            if len(shp) == 2:
                pairs = [(dst, src)]
            else:
                pairs = [(dst[i], src[i]) for i in range(shp[0])]
            for (d2, s2) in pairs:
                nr = d2.shape[0]
                for r in range(0, nr, rows_per):
                    S.dma("pool", d2[r:min(nr, r + rows_per), :], s2[r:min(nr, r + rows_per), :], W=[("wc", name, r, id(d2))])
                    wck.append(("wc", name, r, id(d2)))
            return dst
        wck = []
        ffn_w_in = precast("ffn_w_in", ffn_w_in)
        ffn_w_out = precast("ffn_w_out", ffn_w_out)
        if MIX:
            cmlp_w_in = precast("cmlp_w_in", cmlp_w_in)
            cmlp_w_out = precast("cmlp_w_out", cmlp_w_out)
            moba_w_qkv = precast("moba_w_qkv", moba_w_qkv)
            moba_w_out = precast("moba_w_out", moba_w_out)
            ssm_w_in = precast("ssm_w_in", ssm_w_in)
            ssm_w_out = precast("ssm_w_out", ssm_w_out)
        S.dma("pool", xin[0:1, 0, 0:64], cst[0:1, 0:64], R=wck, W=["xin0"])

    S.dma("sp", cs[:, :], cst[:, :], W=["cs"])
    epsb = sb("epsb", [128, 2], F32)
    if "B" not in DBG:
        S.op("dve", lambda e: e.tensor_copy(out=ident_b[:, :], in_=cs[:, 0:128]), R=["cs"], W=["ident_b"])
        S.op("dve", lambda e: e.memset(onesb[:, :], 1.0 / D), W=["onesb"])
        S.op("dve", lambda e: e.memset(epsb[:, :], LN_EPS), W=["epsb"])

    def load_cols(dst, src_rows, nrows, key):
        S.dma("sp", xin[0:nrows, 0, 0:128], src_rows, R=[], W=["xin0"])
        S.op("pe", lambda e: e.transpose(out=P[7][:, 0:nrows], in_=xin[0:nrows, 0, 0:128],
                                         identity=ident[0:nrows, 0:nrows]),
             R=["xin0", "cs"], W=["P7"])
        S.op("dve", lambda e: e.tensor_copy(out=dst, in_=P[7][:, 0:nrows]), R=["P7"], W=[key])

    if os.environ.get("MK_COLS", "1") == "1":
        load_cols(lng[:, :], ln_g.rearrange("l (c p) -> (l c) p", p=128), 96, "lng")
        load_cols(lnb[:, :], ln_b.rearrange("l (c p) -> (l c) p", p=128), 96, "lnb")

    WTp = sb("WTp", [128, 2, 8, 128], BF16)
    WTs = sb("WTs", [128, 2, 8, 128], BF16)
    statf = stat[:, :, :].rearrange("p a b -> p (a b)")
    cGv = statf[:, 0:D]
    cBv = statf[:, D:2 * D]
    bSs = sb("bSs", [128, 2, 8, 8], F32)
    arena = sb("arena", [128, 9216], BF16)
    vnb = arena[:, 0:4 * D].rearrange("p (a b) -> p a b", a=4)
    st6 = sb("st6", [128, 16], F32)
    Rw = sb("Rw", [128, 8, 8], F32)
    wBf = wB[:, :, :].rearrange("p a b -> p (a b)")
    wBv = wBf[:, 0:8 * 2048].rearrange("p (k n) -> p k n", k=8)
    if MIX:
        for j in range(2):
            S.dma("sp", bSs[:, j, :, :], bass.AP(tensor=cmlp_b_s.tensor, offset=j * 1024, ap=[[0, 128], [128, 8], [1, 8]]),
                  W=["bSs"])
            xv = xin[:, 0, :].rearrange("p (g s) -> p g s", g=8)
            S.dma("sp", xv, cmlp_w_s[j].rearrange("g t s -> t g s"), W=["xin0"])
            for g in range(8):
                S.op("dve", lambda e, g=g: e.tensor_tensor(out=xin[:, 0, g * 128:(g + 1) * 128],
                                                           in0=xin[:, 0, g * 128:(g + 1) * 128], in1=cs[:, 256:384], op=ALU.mult),
                     R=["cs", "xin0"], W=["xin0"])
                pb = P[g % 2]
                S.op("pe", lambda e, g=g, pb=pb: e.transpose(out=pb[:, 0:128], in_=xin[:, 0, g * 128:(g + 1) * 128], identity=ident),
                     R=["xin0", "cs"], W=["P%d" % (g % 2)])
                S.op("dve", lambda e, g=g, pb=pb, j=j: e.tensor_copy(out=WTp[:, j, g, :], in_=pb[:, 0:128]),
                     R=["P%d" % (g % 2)], W=["WTp"])
            for q in range(16):
                S.dma("sp", Rw[q * 8:(q + 1) * 8, :, :], cmlp_w_s[j, :, 0:8, 0:8].rearrange("g t s -> t g s"), W=["Rw"])
            for g in range(8):
                S.op("dve", lambda e, g=g: e.tensor_tensor(
                    out=xin[:, 1, g * 128:(g + 1) * 128].rearrange("p (a b) -> p a b", a=16),
                    in0=Rw[:, g, :].unsqueeze(1).to_broadcast([128, 16, 8]),
                    in1=cs[:, 384:512].rearrange("p (a b) -> p a b", a=16), op=ALU.mult),
                    R=["Rw", "cs"], W=["xin1"])
                pb = P[2 + g % 2]
                S.op("pe", lambda e, g=g, pb=pb: e.transpose(out=pb[:, 0:128], in_=xin[:, 1, g * 128:(g + 1) * 128], identity=ident),
                     R=["xin1", "cs"], W=["P%d" % (2 + g % 2)])
                S.op("dve", lambda e, g=g, pb=pb, j=j: e.tensor_copy(out=WTs[:, j, g, :], in_=pb[:, 0:128]),
                     R=["P%d" % (2 + g % 2)], W=["WTs"])

    SEL = wBf[0:32, 16384:20480]
    CMt = wBf[:, 20480:22528].rearrange("p (a b) -> p a b", a=4)
    ones_b1 = sb("ones_b1", [128, 128], BF16)
    kmT = sb("kmT", [128, 8, 32], F32)
    kmTb = sb("kmTb", [128, 8, 32], BF16)
    rc = sb("rc", [128, 128], F32)
    xpad = sb("xpad", [128, 2, 520], F32)
    PTf = xpad[:, :, 0:512]
    rd = sb("rd", [128, 512], F32)
    gsb = sb("gsb", [128, 8, 32], F32)
    biasq = sb("biasq", [128, 8, 32], F32)
    mx = sb("mx", [128, 8, 8], F32)
    OT = arena[:, 0:4096].rearrange("p (a b) -> p a b", a=8)
    ktt = arena[:, 4096:8192].rearrange("p (a b) -> p a b", a=8)
    PT = arena[:, 8192:9216].rearrange("p (a b) -> p a b", a=2)
    QT = aT[:, 0:8, :]
    biasT = aT[:, 8:16, :]
    wAf = wA[:, :, :, :].rearrange("p a b c -> p (a b c)")
    wBq = wBf[:, 0:8192].rearrange("p (k n) -> p k n", k=8)
    KWA = [[("wA", 0, 0), ("wA", 0, 1)], [("wA", 1, 0), ("wA", 1, 1)]]
    KWB = [("wB", 0), ("wB", 1)]
    SCALE = 128 ** -0.5
    arena_f = arena[:, 4096:8192].bitcast(F32)
    arena_i = arena[:, 8192:9216].bitcast(I32)
    sel8 = arena_f[0:8, 0:1024]
    Pown = arena_f[:, 1024:1152]
    PTs = arena_f[:, 1152:1280].rearrange("p (a b) -> p a b", a=2)
    fin = arena_f[:, 1280:1408].rearrange("p (a b) -> p a b", a=2)
    smal = arena_f[:, 1408:1472]
    bT8 = arena_f[0:8, 1472:1536]
    ptf = arena_f[:, 1536:1552]
    ptb = arena_i[:, 0:16]
    pidx = arena_i[:, 16:32]
    aTf = aT[:, :, :].rearrange("p a b -> p (a b)").bitcast(F32)
    QTf = aTf[:, 0:1024].rearrange("p (a b) -> p a b", a=8)
    KTnf = aTf[:, 1024:2048].rearrange("p (a b) -> p a b", a=8)
    Oown = aTf[:, 2048:3072].rearrange("p (a b) -> p a b", a=8)
    dOwn = aTf[:, 3072:4096].rearrange("p (a b) -> p a b", a=8)
    wBf32 = wBf[:, 0:16384].bitcast(F32)
    Kpg = wBf32.rearrange("p (s j d) -> p s j d", s=2, j=4)
    wAf32 = wAf.bitcast(F32)
    KTp = wAf32[:, 0:2048].rearrange("p (s h k) -> p s h k", s=2, h=8)
    STall = wAf32[:, 2048:3072].rearrange("p (j c) -> p j c", j=16)
    onesf = cs[:, 641:769]
    if MIX:
        S.op("dve", lambda e: e.memset(ones_b1[:, :], 1.0), W=["ones_b1"])

    stateT = sb("stateT", [128, 2048], F32)
    stateTb = sb("stateTb", [128, 2048], BF16)
    halo = sb("halo", [128, 32, 3], F32)
    wcvdt = sb("wcvdt", [128, 8, 32], BF16)
    sm = sb("sm", [128, 8, 32], F32)
    dts = sb("dts", [128, 4, 32], F32)
    prm = sb("prm", [128, 3, 32], F32)
    wcv = sb("wcv", [128, 128], F32)
    bcv = sb("bcv", [128, 32], F32)
    cs2 = sb("cs2", [128, 768], F32)
    SLc, NEGM, UB, SLB, SAME = (cs2[:, i * 128:(i + 1) * 128] for i in range(5))
    seqm, lastm = cs2[:, 640:656], cs2[:, 656:672]
    xtok = wBf[:, 0:8192].rearrange("p (a b) -> p a b", a=4)
    Btok = wBf[:, 8192:12288].rearrange("p (a b) -> p a b", a=4)
    zs = wBf[:, 12288:20480].rearrange("p (a b) -> p a b", a=4)
    BCt = arena[:, 0:8192].rearrange("p (a b) -> p a b", a=16)
    MT = arena[:, 8192:9216].rearrange("p (a b) -> p a b", a=2)
    hbf = hb[:, :, :].rearrange("p a b -> p (a b)")
    xdt, xdtd = hbf[:, 0:2048], hbf[:, 2048:4096]
    ysq = hbf.bitcast(F32)
    yv = xin[:, :, :].rearrange("p a b -> p (a b)")
    ynT = aT[:, 0:16, :]
    xpf = xpad[:, :, :].rearrange("p a b -> p (a b)")
    CBt = xpf[:, 0:1024].rearrange("p (g t) -> p g t", g=8)
    CmT = xpf[:, 520:1032].bitcast(BF16).rearrange("p (g t) -> p g t", g=8) if False else None
    if MIX and (LAYERS >= 3 or os.environ.get("MK_ONLY", "") == "2"):
        S.dma("sp", cs2[:, :], cst2[:, :], W=["cs2"])
        S.dma("sp", prm[:, 0, :], bass.AP(tensor=ssm_dt_bias.tensor, offset=0, ap=[[0, 128], [1, 32]]), W=["prm"])
        S.dma("sp", prm[:, 1, :], bass.AP(tensor=ssm_a_log.tensor, offset=0, ap=[[0, 128], [1, 32]]), W=["prm"])
        S.dma("sp", prm[:, 2, :], bass.AP(tensor=ssm_d.tensor, offset=0, ap=[[0, 128], [1, 32]]), W=["prm"])
        S.op("act", lambda e: e.activation(out=prm[:, 1, :], in_=prm[:, 1, :], func=AF.Exp), R=["prm"], W=["prm"])
        S.op("dve", lambda e: e.tensor_scalar(out=prm[:, 1, :], in0=prm[:, 1, :], scalar1=-1.0, scalar2=None, op0=ALU.mult),
             R=["prm"], W=["prm"])
        load_cols(wcv[:, :], ssm_w_conv.rearrange("k (c p) -> (k c) p", p=128), 128, "wcv")
        load_cols(bcv[:, :], ssm_b_conv.rearrange("o (c p) -> (o c) p", p=128), 32, "bcv")
        S.op("dve", lambda e: e.memset(stateT[:, :], 0.0), W=["stateT"])
        S.op("dve", lambda e: e.memset(stateTb[:, :], 0.0), W=["stateTb"])
        S.op("dve", lambda e: e.memset(halo[:, :, :], 0.0), W=["halo"])

    def load_tile(src, r0, T):
        nsub = T // 128
        for m in range(nsub):
            slot = m % 2
            S.dma("sp", xin[:, slot, :], src[r0 + m * 128: r0 + (m + 1) * 128, :], W=["xin%d" % slot])
            for c in range(8):
                pb = P[c % 2]
                S.op("pe", lambda e, pb=pb, slot=slot, c=c: e.transpose(
                    out=pb[:, 0:128], in_=xin[:, slot, c * 128:(c + 1) * 128], identity=ident),
                    R=["xin%d" % slot, "cs"], W=["P%d" % (c % 2)])
                S.op("dve", lambda e, pb=pb, c=c, m=m: e.tensor_copy(
                    out=hT[:, c, m * 128:(m + 1) * 128], in_=pb[:, 0:128]),
                    R=["P%d" % (c % 2)], W=[("hT", c)])
                if "C" in DBG:
                    S.op("act", lambda e, pb=pb, c=c, m=m: e.activation(
                        out=hb[:, c, m * 128:(m + 1) * 128], in_=pb[:, 0:128], func=AF.Identity),
                        R=["P%d" % (c % 2)], W=[("hb", c)])
                elif "E" not in DBG:
                    S.op("act", lambda e, pb=pb, c=c, m=m: e.activation(
                        out=hb[:, c, m * 128:(m + 1) * 128], in_=hT[:, c, m * 128:(m + 1) * 128], func=AF.Identity),
                        R=[("hT", c)], W=[("hb", c)])
                elif "A" not in DBG:
                    S.op("act", lambda e, pb=pb, c=c, m=m: e.copy(
                        out=hb[:, c, m * 128:(m + 1) * 128], in_=pb[:, 0:128]),
                        R=["P%d" % (c % 2)], W=[("hb", c)])

    def store_tile(dst, r0, T):
        nsub = T // 128
        for m in range(nsub):
            slot = m % 2
            for c in range(8):
                pb = P[c % 2]
                S.op("pe", lambda e, pb=pb, c=c, m=m: e.transpose(
                    out=pb[:, 0:128], in_=hT[:, c, m * 128:(m + 1) * 128], identity=ident),
                    R=[("hT", c), "cs"], W=["P%d" % (c % 2)])
                S.op("dve", lambda e, pb=pb, c=c, slot=slot: e.tensor_copy(
                    out=xin[:, slot, c * 128:(c + 1) * 128], in_=pb[:, 0:128]),
                    R=["P%d" % (c % 2)], W=["xin%d" % slot])
            S.dma("sp", dst[r0 + m * 128: r0 + (m + 1) * 128, :], xin[:, slot, :],
                  R=["xin%d" % slot], W=["out"])

    def layer_norm(li, T):
        for c in range(8):
            s = c % 2
            S.op("act", lambda e, c=c, s=s: e.copy(out=zb[:, s, 0:T], in_=hT[:, c, 0:T]),
                 R=[("hT", c)], W=[("zb", s)])
            S.op("pe", lambda e, c=c, s=s: e.matmul(P[4][:, 0:T], lhsT=onesb[:, :], rhs=zb[:, s, 0:T],
                                                  start=(c == 0), stop=(c == 7)),
                 R=[("zb", s), "onesb"], W=["P4"])
        for c in range(8):
            s = c % 2
            S.op("act", lambda e, c=c, s=s: e.activation(out=zb[:, s, 0:T], in_=hT[:, c, 0:T], func=AF.Square),
                 R=[("hT", c)], W=[("zb", s)])
            S.op("pe", lambda e, c=c, s=s: e.matmul(P[5][:, 0:T], lhsT=onesb[:, :], rhs=zb[:, s, 0:T],
                                                  start=(c == 0), stop=(c == 7)),
                 R=[("zb", s), "onesb"], W=["P5"])
        mean, msq, rstd = stat[:, 0, 0:T], stat[:, 1, 0:T], stat[:, 2, 0:T]
        S.op("dve", lambda e: e.tensor_copy(out=mean, in_=P[4][:, 0:T]), R=["P4"], W=["mean", "cG"])
        S.op("dve", lambda e: e.tensor_tensor(out=msq, in0=mean, in1=mean, op=ALU.mult), R=["mean"], W=["msq", "cG"])
        S.op("dve", lambda e: e.tensor_tensor(out=rstd, in0=P[5][:, 0:T], in1=msq, op=ALU.subtract),
             R=["P5", "msq"], W=["rstd", "cB"])
        S.op("act", lambda e: e.activation(out=rstd, in_=rstd, func=AF.Ln, bias=epsb[:, 0:1]), R=["rstd", "epsb"], W=["rstd"])
        S.op("act", lambda e: e.activation(out=rstd, in_=rstd, func=AF.Exp, scale=-0.5), R=["rstd"], W=["rstd"])
        for c in range(8):
            s = c % 2
            t = tmp[:, s, 0:T]
            S.op("dve", lambda e, c=c, t=t: e.tensor_tensor(out=t, in0=hT[:, c, 0:T], in1=mean, op=ALU.subtract),
                 R=[("hT", c), "mean"], W=[("tmp", s)])
            S.op("dve", lambda e, t=t: e.tensor_tensor(out=t, in0=t, in1=rstd, op=ALU.mult),
                 R=[("tmp", s), "rstd"], W=[("tmp", s)])
            col = li * 8 + c
            S.op("dve", lambda e, c=c, t=t, col=col: e.tensor_scalar(
                out=hT[:, c, 0:T], in0=t, scalar1=lng[:, col:col + 1], scalar2=lnb[:, col:col + 1],
                op0=ALU.mult, op1=ALU.add), R=[("tmp", s), "lng", "lnb"], W=[("hT", c)])
            S.op("act", lambda e, c=c: e.copy(out=hb[:, c, 0:T], in_=hT[:, c, 0:T]),
                 R=[("hT", c)], W=[("hb", c)])

    GROUPS = [(0, 4), (4, 4), (8, 4), (12, 4), (16, 4), (20, 2)]

    def ffn(fi, li, T, prefetch=None):
        w_in = ffn_w_in[fi]
        w_out = ffn_w_out[fi]

        def load_group(gi):
            j0, nj = GROUPS[gi]
            slot = gi % 2
            w = nj * 128
            S.dma("pool", wA[:, slot, :, 0:w],
                  w_in[:, j0 * 128: j0 * 128 + w].rearrange("(k p) n -> p k n", p=128),
                  W=[("wA", slot, 0)])
            S.dma("pool", wA[:, slot, :, 512:512 + w],
                  w_in[:, DFF + j0 * 128: DFF + j0 * 128 + w].rearrange("(k p) n -> p k n", p=128),
                  W=[("wA", slot, 1)])

        load_group(0)
        for gi, (j0, nj) in enumerate(GROUPS):
            if gi + 1 < len(GROUPS):
                load_group(gi + 1)
            if gi == 1:
                for q in range(2):
                    S.dma("pool", wB[:, q * 11:(q + 1) * 11, :],
                          w_out[q * 11 * 128:(q + 1) * 11 * 128, :].rearrange("(j p) n -> p j n", p=128),
                          W=[("wB", q)])
            slot = gi % 2
            for jj in range(nj):
                j = j0 + jj
                pg, pu = P[(j % 2) * 2], P[(j % 2) * 2 + 1]
                kg, ku = "P%d" % ((j % 2) * 2), "P%d" % ((j % 2) * 2 + 1)
                for k in range(8):
                    S.op("pe", lambda e, k=k, jj=jj, pg=pg, slot=slot: e.matmul(
                        pg[:, 0:T], lhsT=wA[:, slot, k, jj * 128:(jj + 1) * 128], rhs=hb[:, k, 0:T],
                        start=(k == 0), stop=(k == 7)), R=[("wA", slot, 0), ("hb", k)], W=[kg])
                for k in range(8):
                    S.op("pe", lambda e, k=k, jj=jj, pu=pu, slot=slot: e.matmul(
                        pu[:, 0:T], lhsT=wA[:, slot, k, 512 + jj * 128:512 + (jj + 1) * 128], rhs=hb[:, k, 0:T],
                        start=(k == 0), stop=(k == 7)), R=[("wA", slot, 1), ("hb", k)], W=[ku])
                s = j % 2
                S.op("act", lambda e, pg=pg, s=s: e.activation(out=tmp[:, s, 0:T], in_=pg[:, 0:T], func=AF.Silu),
                     R=[kg], W=[("tmp", s)])
                S.op("dve", lambda e, pu=pu, s=s, j=j: e.scalar_tensor_tensor(
                    out=aT[:, j, 0:T], in0=tmp[:, s, 0:T], scalar=0.5, in1=pu[:, 0:T],
                    op0=ALU.mult, op1=ALU.mult), R=[("tmp", s), ku], W=[("aT", j)])
        pre = prefetch() if prefetch is not None else None
        for c in range(8):
            po = P[6 + c % 2]
            ko = "P%d" % (6 + c % 2)
            for j in range(NFC):
                S.op("pe", lambda e, j=j, c=c, po=po: e.matmul(
                    po[:, 0:T], lhsT=wB[:, j, c * 128:(c + 1) * 128], rhs=aT[:, j, 0:T],
                    start=(j == 0), stop=(j == NFC - 1)), R=[("wB", j // 11), ("aT", j)], W=[ko])
            S.op("dve", lambda e, c=c, po=po: e.scalar_tensor_tensor(
                out=hT[:, c, 0:T], in0=hT[:, c, 0:T], scalar=ALPHA, in1=po[:, 0:T],
                op0=ALU.mult, op1=ALU.add), R=[ko, ("hT", c)], W=[("hT", c)])
        layer_norm(li, T)
        return pre

    def cmlp(j, li, T, sample, pre=None):
        nsub = T // 128
        if pre is None:
            pre = cmlp_prefetch(j)()
        rk_u, rk_vv = pre
        rk_o = load_w(wBq, cmlp_w_out[j], KWB, "co")
        WK = rk_u
        S.dma("sp", cGv, bass.AP(tensor=cmlp_ln_g.tensor, offset=j * D, ap=[[0, 128], [1, D]]), W=["mean", "msq", "cG"])
        S.dma("sp", cBv, bass.AP(tensor=cmlp_ln_b.tensor, offset=j * D, ap=[[0, 128], [1, D]]), W=["rstd", "cB"])
        for jc in range(8):
            pb, kb = P[jc % 2], "P%d" % (jc % 2)
            for k in range(8):
                S.op("pe", lambda e, k=k, jc=jc, pb=pb: e.matmul(
                    pb[:, 0:T], lhsT=wA[:, 0, k, jc * 128:(jc + 1) * 128], rhs=hb[:, k, 0:T],
                    start=(k == 0), stop=(k == 7)), R=rk_u + [("hb", k)], W=[kb])
            s2 = jc % 2
            S.op("act", lambda e, s2=s2, pb=pb: e.activation(out=tmp[:, s2, 0:T], in_=pb[:, 0:T], func=AF.Gelu),
                 R=[kb], W=[("tmp", s2)])
            S.op("dve", lambda e, jc=jc, s2=s2: e.tensor_copy(out=aT[:, jc, 0:T], in_=tmp[:, s2, 0:T]),
                 R=[("tmp", s2)], W=[("aT", jc)])
        for m in range(nsub):
            for n in range(2):
                pb, kb = P[2 + n], "P%d" % (2 + n)
                for k in range(8):
                    S.op("pe", lambda e, k=k, n=n, m=m, pb=pb: e.matmul(
                        pb[:, 0:512], lhsT=hb[:, k, m * 128:(m + 1) * 128],
                        rhs=wA[:, 1, k, n * 512:(n + 1) * 512],
                        start=(k == 0), stop=(k == 7)), R=rk_vv + [("hb", k)], W=[kb])
                S.op("act", lambda e, n=n, pb=pb: e.activation(out=xin[:, 0, n * 512:(n + 1) * 512], in_=pb[:, 0:512],
                                                              func=AF.Gelu), R=[kb], W=["xin0"])
            for n in range(2):
                S.op("dve", lambda e, n=n: e.bn_stats(out=st6[:, n * 6:(n + 1) * 6], in_=xin[:, 0, n * 512:(n + 1) * 512]),
                     R=["xin0"], W=["st6"])
            S.op("dve", lambda e: e.bn_aggr(out=st6[:, 12:14], in_=st6[:, 0:12]), R=["st6"], W=["st6"])
            S.op("act", lambda e: e.activation(out=st6[:, 14:15], in_=st6[:, 13:14], func=AF.Ln, bias=epsb[:, 0:1]),
                 R=["st6", "epsb"], W=["st6"])
            S.op("act", lambda e: e.activation(out=st6[:, 14:15], in_=st6[:, 14:15], func=AF.Exp, scale=-0.5),
                 R=["st6"], W=["st6"])
            S.op("dve", lambda e: e.tensor_scalar(out=xin[:, 0, :], in0=xin[:, 0, :], scalar1=st6[:, 12:13],
                                                  scalar2=st6[:, 14:15], op0=ALU.subtract, op1=ALU.mult),
                 R=["st6", "xin0"], W=["xin0"])
            S.op("dve", lambda e: e.tensor_tensor(out=xin[:, 0, :], in0=xin[:, 0, :], in1=cGv, op=ALU.mult),
                 R=["cG", "xin0"], W=["xin0"])
            S.op("dve", lambda e: e.tensor_tensor(out=xin[:, 0, :], in0=xin[:, 0, :], in1=cBv, op=ALU.add),
                 R=["cB", "xin0"], W=["xin0"])
            if sample:
                S.dma("sp", cv_s[j], xin[:, 0, :], R=["xin0"], W=["out"])
            S.op("act", lambda e, m=m: e.activation(out=vnb[:, m, :], in_=xin[:, 0, :], func=AF.Identity),
                 R=["xin0"], W=[("vnb", m)])
        WT = WTs if sample else WTp
        bSp = xpad[:, :, :].rearrange("p a b -> p (a b)")[:, 0:1024].rearrange("p (g t) -> p g t", g=8)
        if not sample:
            S.dma("sp", bSp, bass.AP(tensor=cmlp_b_s.tensor, offset=j * 1024, ap=[[0, 128], [128, 8], [1, 128]]),
                  W=["bSp", ("PTf", 0), ("PTf", 1)])
        for g in range(8):
            pb, kb = P[4 + g % 2], "P%d" % (4 + g % 2)
            for m in range(nsub):
                S.op("pe", lambda e, g=g, m=m, pb=pb: e.matmul(
                    pb[:, m * 128:(m + 1) * 128], lhsT=vnb[:, m, g * 128:(g + 1) * 128], rhs=WT[:, j, g, :],
                    start=True, stop=True), R=[("vnb", m), "WTp", "WTs"], W=[kb])
            s2 = g % 2
            bw = 8 if sample else 128
            S.op("dve", lambda e, g=g, pb=pb, s2=s2, bw=bw: e.tensor_tensor(
                out=tmp[:, s2, 0:T].rearrange("p (a b) -> p a b", b=bw),
                in0=pb[:, 0:T].rearrange("p (a b) -> p a b", b=bw),
                in1=(bSs[:, j, g, :] if sample else bSp[:, g, :]).unsqueeze(1).to_broadcast([128, T // bw, bw]), op=ALU.add),
                R=[kb, "bSp", "bSs"], W=[("tmp", s2)])
            S.op("dve", lambda e, g=g, s2=s2: e.tensor_tensor(out=aT[:, 8 + g, 0:T], in0=tmp[:, s2, 0:T], in1=aT[:, g, 0:T],
                                                           op=ALU.mult), R=[("tmp", s2), ("aT", g)], W=[("aT", 8 + g)])
        WK2 = rk_o
        for c in range(8):
            po, ko = P[6 + c % 2], "P%d" % (6 + c % 2)
            for k in range(8):
                S.op("pe", lambda e, k=k, c=c, po=po: e.matmul(
                    po[:, 0:T], lhsT=wBq[:, k, c * 128:(c + 1) * 128], rhs=aT[:, 8 + k, 0:T],
                    start=(k == 0), stop=(k == 7)), R=WK2 + [("aT", 8 + k)], W=[ko])
            S.op("dve", lambda e, c=c, po=po: e.scalar_tensor_tensor(
                out=hT[:, c, 0:T], in0=hT[:, c, 0:T], scalar=ALPHA, in1=po[:, 0:T],
                op0=ALU.mult, op1=ALU.add), R=[ko, ("hT", c)], W=[("hT", c)])
        layer_norm(li, T)

    def load_w(dst, src, canon, tag):
        keys = list(canon)
        for q in range(2):
            kq = ("x", tag, q)
            S.dma("pool", dst[:, q * 4:(q + 1) * 4, :], src[q * 512:(q + 1) * 512, :].rearrange("(k p) n -> p k n", p=128),
                  W=(list(canon) if q == 0 else []) + [kq])
            keys.append(kq)
        return keys

    def cmlp_prefetch(j):
        def f():
            return (load_w(wA[:, 0], cmlp_w_in[j][:, 0:D], KWA[0], "cu"), load_w(wA[:, 1], cmlp_w_in[j][:, D:2 * D], KWA[1], "cv"))
        return f

    def moba_prefetch():
        return (load_w(wA[:, 0], moba_w_qkv[:, 0:D], KWA[0], "mq"), load_w(wA[:, 1], moba_w_qkv[:, D:2 * D], KWA[1], "mk"))

    def ssd_prefetch():
        return [load_w(wA[:, g], ssm_w_in[:, 2048 + g * 1024:2048 + (g + 1) * 1024], KWA[g], "sx%d" % g) for g in range(2)]

    def out_proj(w_src, rhs_chunks, rhs_keys, li, T):
        rk = load_w(wA[:, 0], w_src, KWA[0], "wo")
        for c in range(8):
            po, ko = P[6 + c % 2], "P%d" % (6 + c % 2)
            nk = len(rhs_chunks)
            for k in range(nk):
                S.op("pe", lambda e, k=k, c=c, po=po: e.matmul(
                    po[:, 0:T], lhsT=wA[:, 0, k, c * 128:(c + 1) * 128], rhs=rhs_chunks[k],
                    start=(k == 0), stop=(k == nk - 1)), R=rk + [rhs_keys[k]], W=[ko])
            S.op("dve", lambda e, c=c, po=po: e.scalar_tensor_tensor(
                out=hT[:, c, 0:T], in0=hT[:, c, 0:T], scalar=ALPHA, in1=po[:, 0:T],
                op0=ALU.mult, op1=ALU.add), R=[ko, ("hT", c)], W=[("hT", c)])
        layer_norm(li, T)

    def moba(li, T, r0, sample, pre=None):
        nsub = T // 128
        rk_q, rk_k = pre if pre is not None else moba_prefetch()
        rk_v = load_w(wBq, moba_w_qkv[:, 2 * D:3 * D], KWB, "mv")
        qst, kst = xin[:, 0, :], xin[:, 1, :]
        vst = statf[:, 0:D]
        rsc = statf[:, D:D + 256].rearrange("p (h d) -> p h d", h=4)
        VK = ["mean", "msq", "cG"]
        RK = ["rstd", "cB"]
        kout, vout = (k_s, v_s) if sample else (k_p, v_p)
        ro = 0 if sample else r0
        blk0 = r0 // 256
        for m in range(nsub):
            S.dma("sp", rc[:, :], ropec[(SEQ if sample else r0) + m * 128:(SEQ if sample else r0) + (m + 1) * 128, :], W=["rc"])
            cosb = rc[:, 0:64].unsqueeze(1).to_broadcast([128, 4, 64])
            sinb = rc[:, 64:128].unsqueeze(1).to_broadcast([128, 4, 64])
            for (wv, rk, stage, skey) in ((wA[:, 0], rk_q, qst, "xin0"), (wA[:, 1], rk_k, kst, "xin1")):
                for n in range(2):
                    pb, kb = P[n], "P%d" % n
                    for k in range(8):
                        S.op("pe", lambda e, k=k, n=n, m=m, pb=pb, wv=wv: e.matmul(
                            pb[:, 0:512], lhsT=hb[:, k, m * 128:(m + 1) * 128], rhs=wv[:, k, n * 512:(n + 1) * 512],
                            start=(k == 0), stop=(k == 7)), R=rk + [("hb", k)], W=[kb])
                    psv = pb[:, 0:512].rearrange("p (h t d) -> p h t d", h=4, t=2)
                    dv = stage[:, n * 512:(n + 1) * 512].rearrange("p (h t d) -> p h t d", h=4, t=2)
                    x1, x2, d1, d2 = psv[:, :, 0, :], psv[:, :, 1, :], dv[:, :, 0, :], dv[:, :, 1, :]
                    S.op("dve", lambda e, d1=d1, x1=x1: e.tensor_tensor(out=d1, in0=x1, in1=cosb, op=ALU.mult), R=[kb, "rc"], W=[skey])
                    S.op("dve", lambda e, x2=x2: e.tensor_tensor(out=rsc, in0=x2, in1=sinb, op=ALU.mult), R=[kb, "rc"], W=RK)
                    S.op("dve", lambda e, d1=d1: e.tensor_tensor(out=d1, in0=d1, in1=rsc, op=ALU.subtract), R=RK + [skey], W=[skey])
                    S.op("dve", lambda e, d2=d2, x2=x2: e.tensor_tensor(out=d2, in0=x2, in1=cosb, op=ALU.mult), R=[kb, "rc"], W=[skey])
                    S.op("dve", lambda e, x1=x1: e.tensor_tensor(out=rsc, in0=x1, in1=sinb, op=ALU.mult), R=[kb, "rc"], W=RK)
                    S.op("dve", lambda e, d2=d2: e.tensor_tensor(out=d2, in0=d2, in1=rsc, op=ALU.add), R=RK + [skey], W=[skey])
            for n in range(2):
                pb, kb = P[n], "P%d" % n
                for k in range(8):
                    S.op("pe", lambda e, k=k, n=n, m=m, pb=pb: e.matmul(
                        pb[:, 0:512], lhsT=hb[:, k, m * 128:(m + 1) * 128], rhs=wBq[:, k, n * 512:(n + 1) * 512],
                        start=(k == 0), stop=(k == 7)), R=rk_v + [("hb", k)], W=[kb])
                S.op("act", lambda e, n=n, pb=pb: e.activation(out=vst[:, n * 512:(n + 1) * 512], in_=pb[:, 0:512], func=AF.Identity),
                     R=[kb], W=VK)
            S.dma("sp", kout[ro + m * 128:ro + (m + 1) * 128, :], kst, R=["xin1"], W=["out"])
            S.dma("sp", vout[ro + m * 128:ro + (m + 1) * 128, :], vst, R=VK, W=["out"])
            if not sample:
                S.dma("pool", V_hist[r0 + m * 128:r0 + (m + 1) * 128, :], vst, R=VK, W=[("vhist", m)])
            for (stage, skey, dstT, dkey, scl) in ((qst, "xin0", QTf if sample else QT, "QT", SCALE),
                                                   (kst, "xin1", KTnf if sample else ktt, "ktt", 1.0)):
                for g in range(2):
                    pb, kb = P[2 + g], "P%d" % (2 + g)
                    for hh in range(4):
                        h = g * 4 + hh
                        S.op("pe", lambda e, h=h, hh=hh, pb=pb, stage=stage: e.transpose(
                            out=pb[:, hh * 128:(hh + 1) * 128], in_=stage[:, h * 128:(h + 1) * 128], identity=ident),
                            R=[skey, "cs"], W=[kb])
                    S.op("dve", lambda e, g=g, pb=pb, dstT=dstT, scl=scl, m=m: e.tensor_scalar(
                        out=dstT[:, g * 4:(g + 1) * 4, m * 128:(m + 1) * 128],
                        in0=pb[:, 0:512].rearrange("p (a b) -> p a b", a=4), scalar1=scl, scalar2=None, op0=ALU.mult),
                        R=[kb], W=[dkey])
        if sample:
            S.dma("sp", sel8, sel8d[:, :], R=["ktt"], W=["sel8", "ktt"])
            for h in range(8):
                S.op("pe", lambda e, h=h: e.matmul(P[0][:, 0:128], lhsT=KTnf[:, h, :], rhs=QTf[:, h, :], start=True, stop=True),
                     R=["QT", "ktt"], W=["P0"])
                S.op("dve", lambda e: e.tensor_tensor(out=Pown, in0=P[0][:, 0:128], in1=cs[:, 512:640], op=ALU.add),
                     R=["P0", "cs"], W=["Pown"])
                S.op("act", lambda e: e.activation(out=Pown, in_=Pown, func=AF.Exp), R=["Pown"], W=["Pown"])
                S.op("pe", lambda e, h=h: e.matmul(P[1][:, 0:128], lhsT=vst[:, h * 128:(h + 1) * 128], rhs=Pown,
                                                   start=True, stop=True), R=VK + ["Pown"], W=["P1"])
                S.op("pe", lambda e, h=h: e.matmul(P[2][:, 0:128], lhsT=onesf, rhs=Pown, start=True, stop=True),
                     R=["cs", "Pown"], W=["P2"])
                S.op("dve", lambda e, h=h: e.tensor_copy(out=Oown[:, h, :], in_=P[1][:, 0:128]), R=["P1"], W=["Oown"])
                S.op("dve", lambda e, h=h: e.tensor_copy(out=dOwn[:, h, :], in_=P[2][:, 0:128]), R=["P2"], W=["dOwn"])
            for sq in range(NSEQ_S):
                S.dma("sp", ptb, bass.AP(tensor=page_table.tensor, offset=sq * 16, ap=[[0, 128], [1, 16]]), W=["ptb"])
                S.op("dve", lambda e: e.tensor_copy(out=ptf, in_=ptb), R=["ptb"], W=["ptf"])
                S.op("dve", lambda e: e.tensor_scalar(out=ptf, in0=ptf, scalar1=128.0, scalar2=cs[:, 640:641],
                                                      op0=ALU.mult, op1=ALU.add), R=["ptf", "cs"], W=["ptf"])
                S.op("dve", lambda e: e.tensor_copy(out=pidx, in_=ptf), R=["ptf"], W=["pidx"])
                for j in range(16):
                    sl, jj = (j // 4) % 2, j % 4
                    if jj == 0:
                        for j2 in range(4):
                            first = (sq == 0 and j < 8)
                            S.idma(Kpg[:, sl, j2, :], cache_k[:, :], pidx[:, j + j2:j + j2 + 1], R=["pidx"],
                                   W=(KWB if first else []) + [("Kpg", sl, j2)])
                    ks = j % 2
                    for g in range(2):
                        pb, kb = P[2 + g], "P%d" % (2 + g)
                        for hh in range(4):
                            h = g * 4 + hh
                            S.op("pe", lambda e, h=h, hh=hh, pb=pb, sl=sl, jj=jj: e.transpose(
                                out=pb[:, hh * 128:(hh + 1) * 128], in_=Kpg[:, sl, jj, h * 128:(h + 1) * 128], identity=ident),
                                R=KWB + [("Kpg", sl, jj), "cs"], W=[kb])
                        S.op("dve" if g == 0 else "act", (lambda e, g=g, pb=pb, ks=ks: e.tensor_copy(
                            out=KTp[:, ks, g * 4:(g + 1) * 4, :], in_=pb[:, 0:512].rearrange("p (a b) -> p a b", a=4))) if g == 0 else
                            (lambda e, g=g, pb=pb, ks=ks: e.activation(
                                out=KTp[:, ks, g * 4:(g + 1) * 4, :], in_=pb[:, 0:512].rearrange("p (a b) -> p a b", a=4),
                                func=AF.Identity)), R=[kb], W=KWA[0] + [("KTp", ks, g)] if (sq == 0 and j < 2) else [("KTp", ks, g)])
                    pb, kb = P[j % 2], "P%d" % (j % 2)
                    for h in range(8):
                        S.op("pe", lambda e, h=h, pb=pb, ks=ks, sq=sq: e.matmul(
                            pb[:, h * 8:(h + 1) * 8], lhsT=KTp[:, ks, h, :], rhs=QTf[:, h, sq * 8:(sq + 1) * 8],
                            start=True, stop=True), R=KWA[0] + [("KTp", ks, h // 4), "QT"], W=[kb])
                    S.op("dve", lambda e, j=j, pb=pb: e.tensor_copy(out=STall[:, j, :], in_=pb[:, 0:64]), R=[kb],
                         W=(KWA[0] if (sq == 0 and j == 0) else []) + [("ST", j)])
                for n in range(8):
                    for t2 in range(2):
                        j = 2 * n + t2
                        S.op("pe", lambda e, n=n, j=j, t2=t2: e.matmul(P[6][0:64, n:n + 1], lhsT=STall[:, j, :], rhs=onesf[:, 0:1],
                                                                     start=(t2 == 0), stop=(t2 == 1)),
                             R=KWA[0] + [("ST", j), "cs"], W=["P6"])
                S.op("dve", lambda e: e.tensor_copy(out=smal[0:64, 0:8], in_=P[6][0:64, 0:8]), R=["P6"], W=["smal"])
                S.op("dve", lambda e: e.max(out=smal[0:64, 8:16], in_=smal[0:64, 0:8]), R=["smal"], W=["smal"])
                S.op("dve", lambda e: e.tensor_tensor(out=smal[0:64, 16:24], in0=smal[0:64, 0:8],
                                                      in1=smal[0:64, 10:11].to_broadcast([64, 8]), op=ALU.is_ge),
                     R=["smal"], W=["smal"])
                S.op("dve", lambda e: e.tensor_scalar(out=smal[0:64, 16:24], in0=smal[0:64, 16:24], scalar1=-1.0, scalar2=30000.0,
                                                      op0=ALU.add, op1=ALU.mult), R=["smal"], W=["smal"])
                S.op("pe", lambda e: e.transpose(out=P[7][0:8, 0:64], in_=smal[0:64, 16:24], identity=ident[0:64, 0:64]),
                     R=["smal", "cs"], W=["P7"])
                S.op("dve", lambda e: e.tensor_copy(out=bT8, in_=P[7][0:8, 0:64]), R=["P7"], W=["bT8"])
                for j in range(16):
                    sl, jj = (j // 4) % 2, j % 4
                    n = j // 2
                    if jj == 0:
                        for j2 in range(4):
                            S.idma(Kpg[:, sl, j2, :], cache_v[:, :], pidx[:, j + j2:j + j2 + 1], R=["pidx"], W=[("Kpg", sl, j2)])
                    b2 = j % 2
                    pb, kb = P[b2], "P%d" % b2
                    S.op("pe", lambda e, n=n, pb=pb: e.matmul(pb[:, 0:64], lhsT=sel8[0:8, n * 128:(n + 1) * 128], rhs=bT8[0:8, 0:64],
                                                              start=True, stop=False), R=["sel8", "bT8"], W=[kb])
                    S.op("pe", lambda e, j=j, pb=pb: e.matmul(pb[:, 0:64], lhsT=ident, rhs=STall[:, j, :], start=False, stop=True),
                         R=KWA[0] + [("ST", j), "cs"], W=[kb])
                    S.op("act", lambda e, pb=pb, b2=b2: e.activation(out=PTs[:, b2, :], in_=pb[:, 0:64], func=AF.Exp),
                         R=[kb], W=[("PTs", b2)])
                    if j == 0:
                        S.op("pe", lambda e: e.matmul(P[4][:, 0:64], lhsT=onesf, rhs=cs[:, 769:833], start=True, stop=False),
                             R=["cs"], W=["P4"])
                    for h in range(8):
                        S.op("pe", lambda e, h=h, j=j, b2=b2, sl=sl, jj=jj: e.matmul(
                            P[4][:, h * 8:(h + 1) * 8], lhsT=Kpg[:, sl, jj, h * 128:(h + 1) * 128], rhs=PTs[:, b2, h * 8:(h + 1) * 8],
                            start=False, stop=(j == 15 and h == 7)), R=KWB + [("Kpg", sl, jj), ("PTs", b2)], W=["P4"])
                    S.op("pe", lambda e, j=j, b2=b2: e.matmul(P[5][:, 0:64], lhsT=onesf, rhs=PTs[:, b2, :],
                                                              start=(j == 0), stop=(j == 15)), R=["cs", ("PTs", b2)], W=["P5"])
                S.op("dve", lambda e, sq=sq: e.tensor_tensor(
                    out=fin[:, 0, :].rearrange("p (a b) -> p a b", a=8), in0=P[4][:, 0:64].rearrange("p (a b) -> p a b", a=8),
                    in1=Oown[:, :, sq * 8:(sq + 1) * 8], op=ALU.add), R=["P4", "Oown"], W=["fin0"])
                S.op("dve", lambda e, sq=sq: e.tensor_tensor(
                    out=fin[:, 1, :].rearrange("p (a b) -> p a b", a=8), in0=P[5][:, 0:64].rearrange("p (a b) -> p a b", a=8),
                    in1=dOwn[:, :, sq * 8:(sq + 1) * 8], op=ALU.add), R=["P5", "dOwn"], W=["fin1"])
                S.op("dve", lambda e: e.reciprocal(out=fin[:, 1, :], in_=fin[:, 1, :]), R=["fin1"], W=["fin1"])
                S.op("dve", lambda e, sq=sq: e.tensor_tensor(
                    out=OT[:, :, sq * 8:(sq + 1) * 8], in0=fin[:, 0, :].rearrange("p (a b) -> p a b", a=8),
                    in1=fin[:, 1, :].rearrange("p (a b) -> p a b", a=8), op=ALU.mult),
                    R=["fin0", "fin1"], W=[("OT", h) for h in range(8)])
        else:
            NC = (r0 + T) // 128
            S.dma("pool", SEL, cstb[0:32, 0:4096], W=KWB + ["SEL"])
            S.dma("pool", CMt, cstb[:, 4096:6144].rearrange("p (a b) -> p a b", a=4), W=["CMt"])
            S.dma("sp", KT_hist[:, :, r0:r0 + T].rearrange("h p t -> p h t"), ktt[:, :, 0:T], R=["ktt"], W=["kthist"])
            S.op("dve", lambda e: e.tensor_reduce(out=kmT[:, :, blk0:blk0 + 2],
                                                  in_=ktt[:, :, 0:T].rearrange("p h (b t) -> p h b t", b=2),
                                                  axis=AX.X, op=ALU.add), R=["ktt"], W=["kmT"])
            S.op("dve", lambda e: e.tensor_scalar(out=kmTb[:, :, blk0:blk0 + 2], in0=kmT[:, :, blk0:blk0 + 2],
                                                  scalar1=1.0 / 256, scalar2=None, op0=ALU.mult), R=["kmT"], W=["kmTb"])
            for m in range(nsub):
                own = blk0 + m // 2
                S.op("dve", lambda e: e.memset(biasq[:, :, :], -30000.0), W=["biasq"])
                if own > 0:
                    if own <= 3:
                        S.op("dve", lambda e, own=own: e.memset(biasq[:, :, 0:own], 0.0), W=["biasq"])
                    else:
                        S.op("dve", lambda e: e.memset(gsb[:, :, :], -1e30), W=["gsb"])
                        for h in range(8):
                            S.op("pe", lambda e, h=h, m=m, own=own: e.matmul(
                                P[6][:, h * 32:h * 32 + own], lhsT=QT[:, h, m * 128:(m + 1) * 128], rhs=kmTb[:, h, 0:own],
                                start=True, stop=True), R=["QT", "kmTb"], W=["P6"])
                        S.op("dve", lambda e, own=own: e.tensor_copy(
                            out=gsb[:, :, 0:own], in_=P[6][:, 0:256].rearrange("p (h n) -> p h n", h=8)[:, :, 0:own]),
                            R=["P6"], W=["gsb"])
                        for h in range(8):
                            S.op("dve", lambda e, h=h: e.max(out=mx[:, h, :], in_=gsb[:, h, :]), R=["gsb"], W=["mx"])
                        S.op("dve", lambda e, own=own: e.tensor_tensor(
                            out=biasq[:, :, 0:own], in0=gsb[:, :, 0:own], in1=mx[:, :, 2:3].to_broadcast([128, 8, own]),
                            op=ALU.is_ge), R=["gsb", "mx"], W=["biasq"])
                        S.op("dve", lambda e, own=own: e.tensor_scalar(
                            out=biasq[:, :, 0:own], in0=biasq[:, :, 0:own], scalar1=-1.0, scalar2=30000.0,
                            op0=ALU.add, op1=ALU.mult), R=["biasq"], W=["biasq"])
                S.op("dve", lambda e, own=own: e.memset(biasq[:, :, own:own + 1], 0.0), W=["biasq"])
                for g in range(2):
                    for hh in range(4):
                        h = g * 4 + hh
                        S.op("pe", lambda e, h=h, hh=hh: e.transpose(
                            out=P[7][0:32, hh * 128:(hh + 1) * 128], in_=biasq[:, h, :], identity=ident),
                            R=["biasq", "cs"], W=["P7"])
                    S.op("dve", lambda e, g=g, m=m: e.tensor_copy(
                        out=biasT[0:32, g * 4:(g + 1) * 4, m * 128:(m + 1) * 128],
                        in_=P[7][0:32, 0:512].rearrange("p (a b) -> p a b", a=4)), R=["P7"], W=["biasT"])
            for h in range(8):
                s2 = h % 2
                KTh = wAf[:, s2 * 8192:s2 * 8192 + NC * 128]
                Vh = wBf[:, s2 * 8192:s2 * 8192 + NC * 128].rearrange("p (c d) -> p c d", d=128)
                first = False
                S.dma("sp", KTh, KT_hist[h, :, 0:NC * 128], R=["kthist"], W=(KWA[s2] if h < 2 else []) + [("KTh", s2)])
                S.dma("sp", Vh, V_hist[0:NC * 128, h * 128:(h + 1) * 128].rearrange("(c p) d -> p c d", p=128),
                      R=[("vhist", mm) for mm in range(nsub)], W=(KWB if h < 2 else []) + [("Vh", s2)])
                RKT = KWA[s2] + [("KTh", s2)]
                RV = KWB + [("Vh", s2)]
                for c in range(NC):
                    n = c // 2
                    b2 = c % 2
                    pb, kb = P[b2], "P%d" % b2
                    own_tile = c >= NC - 4
                    S.op("pe", lambda e, c=c, pb=pb, h=h, KTh=KTh: e.matmul(
                        pb[:, 0:T], lhsT=KTh[:, c * 128:(c + 1) * 128], rhs=QT[:, h, 0:T], start=True, stop=False),
                        R=RKT + ["QT"], W=[kb])
                    S.op("pe", lambda e, n=n, pb=pb, h=h, own_tile=own_tile: e.matmul(
                        pb[:, 0:T], lhsT=SEL[0:32, n * 128:(n + 1) * 128], rhs=biasT[0:32, h, 0:T],
                        start=False, stop=(not own_tile)), R=["SEL", "biasT"], W=[kb])
                    if own_tile:
                        cc = c - (NC - 4)
                        S.op("pe", lambda e, cc=cc, pb=pb: e.matmul(
                            pb[:, 0:T], lhsT=ident_b[:, :], rhs=CMt[:, cc, 0:T], start=False, stop=True),
                            R=["ident_b", "CMt"], W=[kb])
                    S.op("act", lambda e, pb=pb, b2=b2: e.activation(out=PTf[:, b2, 0:T], in_=pb[:, 0:T], func=AF.Exp),
                         R=[kb], W=[("PTf", b2)])
                    S.op("dve", lambda e, b2=b2: e.tensor_copy(out=PT[:, b2, 0:T], in_=PTf[:, b2, 0:T]),
                         R=[("PTf", b2)], W=[("PT", b2)])
                    S.op("pe", lambda e, c=c, b2=b2, Vh=Vh: e.matmul(
                        P[4][:, 0:T], lhsT=Vh[:, c, :], rhs=PT[:, b2, 0:T], start=(c == 0), stop=(c == NC - 1)),
                        R=RV + [("PT", b2)], W=["P4"])
                    S.op("pe", lambda e, c=c, b2=b2: e.matmul(
                        P[5][:, 0:T], lhsT=ones_b1[:, :], rhs=PT[:, b2, 0:T], start=(c == 0), stop=(c == NC - 1)),
                        R=[("PT", b2), "ones_b1"], W=["P5"])
                S.op("dve", lambda e: e.reciprocal(out=rd[:, 0:T], in_=P[5][:, 0:T]), R=["P5"], W=["rd"])
                S.op("dve", lambda e, h=h: e.tensor_tensor(out=OT[:, h, 0:T], in0=P[4][:, 0:T], in1=rd[:, 0:T], op=ALU.mult),
                     R=["P4", "rd"], W=[("OT", h)])
        out_proj(moba_w_out, [OT[:, k, 0:T] for k in range(8)], [("OT", k) for k in range(8)], li, T)

    def ssd(li, T, r0, sample, last, pre=None):
        nsub = T // 128
        HK = [("hb", k) for k in range(8)]
        Um, SLm, NGm, ATm = (UB, SLB, cs[:, 512:640], SAME) if sample else (cs[:, 128:256], SLc, NEGM, onesf)
        if sample:
            pass
        for grp in range(4):
            slot = grp % 2
            if pre is not None and grp < 2:
                rk = pre[grp]
            else:
                rk = load_w(wA[:, slot], ssm_w_in[:, 2048 + grp * 1024:2048 + (grp + 1) * 1024], KWA[slot], "sx%d" % slot)
            if sample:
                S.dma("sp", xin[0:48, 0, :], state_conv[:, grp * 1024:(grp + 1) * 1024], W=["xin0"])
            for cc in range(8):
                ch = grp * 8 + cc
                b = ch % 2
                pb, kb = P[b], "P%d" % b
                for k in range(8):
                    S.op("pe", lambda e, k=k, cc=cc, pb=pb, slot=slot: e.matmul(
                        pb[:, 0:T], lhsT=wA[:, slot, k, cc * 128:(cc + 1) * 128], rhs=hb[:, k, 0:T],
                        start=(k == 0), stop=(k == 7)), R=rk + [("hb", k)], W=[kb])
                xk = ("PTf", b)
                if sample:
                    xp3 = xpad[:, b, 0:176].rearrange("p (s t) -> p s t", t=11)
                    S.op("pe", lambda e, cc=cc: e.transpose(out=P[2][:, 0:48], in_=xin[0:48, 0, cc * 128:(cc + 1) * 128],
                                                            identity=ident[0:48, 0:48]), R=["xin0", "cs"], W=["P2"])
                    S.op("dve", lambda e, xp3=xp3: e.tensor_copy(out=xp3[:, :, 0:3], in_=P[2][:, 0:48].rearrange("p (s t) -> p s t", t=3)),
                         R=["P2"], W=[xk])
                    S.op("act", lambda e, xp3=xp3, pb=pb: e.activation(out=xp3[:, :, 3:11], in_=pb[:, 0:128].rearrange("p (s t) -> p s t", t=8),
                                                                       func=AF.Identity), R=[kb], W=[xk])
                    views = [xp3[:, :, k:k + 8] for k in range(4)]
                    acc = rd[:, 0:128].rearrange("p (s t) -> p s t", t=8)
                else:
                    S.op("dve", lambda e, b=b, ch=ch: e.tensor_copy(out=xpad[:, b, 0:3], in_=halo[:, ch, :]), R=["halo"], W=[xk])
                    S.op("act", lambda e, b=b, pb=pb: e.activation(out=xpad[:, b, 3:3 + T], in_=pb[:, 0:T], func=AF.Identity),
                         R=[kb], W=[xk])
                    S.op("dve", lambda e, b=b, ch=ch: e.tensor_copy(out=halo[:, ch, :], in_=xpad[:, b, T:T + 3]), R=[xk], W=["halo"])
                    views = [xpad[:, b, k:k + T] for k in range(4)]
                    acc = rd[:, 0:T]
                S.op("dve", lambda e, acc=acc, v0=views[0], ch=ch: e.tensor_scalar(
                    out=acc, in0=v0, scalar1=wcv[:, ch:ch + 1], scalar2=None, op0=ALU.mult), R=[xk, "wcv"], W=["rd"])
                for k in range(1, 4):
                    S.op("dve", lambda e, acc=acc, vk=views[k], ch=ch, k=k: e.scalar_tensor_tensor(
                        out=acc, in0=vk, scalar=wcv[:, k * 32 + ch:k * 32 + ch + 1], in1=acc, op0=ALU.mult, op1=ALU.add),
                        R=[xk, "wcv", "rd"], W=["rd"])
                S.op("act", lambda e, ch=ch: e.activation(out=rd[:, 0:T], in_=rd[:, 0:T], func=AF.Silu, bias=bcv[:, ch:ch + 1]),
                     R=["rd", "bcv"], W=["rd"])
                if ch >= 16:
                    S.op("dve", lambda e, ch=ch: e.tensor_copy(out=BCt[:, ch - 16, 0:T], in_=rd[:, 0:T]), R=["rd"], W=[("BCt", ch - 16)])
                if ch < 24:
                    dst = xtok if ch < 16 else Btok
                    co = ch if ch < 16 else ch - 16
                    pb2, kb2 = P[2 + ch % 2], "P%d" % (2 + ch % 2)
                    for sub in range(nsub):
                        S.op("pe", lambda e, sub=sub, pb2=pb2: e.transpose(out=pb2[:, sub * 128:(sub + 1) * 128],
                                                                          in_=rd[:, sub * 128:(sub + 1) * 128], identity=ident),
                             R=["rd", "cs"], W=[kb2])
                    S.op("dve", lambda e, dst=dst, co=co, pb2=pb2: e.tensor_copy(
                        out=dst[:, 0:nsub, co * 128:(co + 1) * 128], in_=pb2[:, 0:T].rearrange("p (a b) -> p a b", b=128)),
                        R=[kb2], W=KWB + ["xtok"])
        if sample or last:
            sub = nsub - 1
            for grp in range(4):
                slot = grp % 2
                rk = load_w(wA[:, slot], ssm_w_in[:, 2048 + grp * 1024:2048 + (grp + 1) * 1024], KWA[slot], "sx%d" % slot)
                for n in range(2):
                    pb, kb = P[n], "P%d" % n
                    for k in range(8):
                        S.op("pe", lambda e, k=k, n=n, pb=pb, slot=slot, sub=sub: e.matmul(
                            pb[:, 0:512], lhsT=hb[:, k, sub * 128:(sub + 1) * 128], rhs=wA[:, slot, k, n * 512:(n + 1) * 512],
                            start=(k == 0), stop=(k == 7)), R=rk + [("hb", k)], W=[kb])
                    S.op("act", lambda e, n=n, pb=pb: e.activation(out=xin[:, 1, n * 512:(n + 1) * 512], in_=pb[:, 0:512], func=AF.Identity),
                         R=[kb], W=["xin1"])
                if sample:
                    for sq in range(NSEQ_S):
                        S.dma("sp", conv_s[sq * 3:(sq + 1) * 3, grp * 1024:(grp + 1) * 1024], xin[sq * 8 + 5:sq * 8 + 8, 1, :],
                              R=["xin1"], W=["out"])
                else:
                    S.dma("sp", conv_p[:, grp * 1024:(grp + 1) * 1024], xin[125:128, 1, :], R=["xin1"], W=["out"])
        rkz = [load_w(wA[:, q], ssm_w_in[:, q * 1024:(q + 1) * 1024], KWA[q], "sz%d" % q) for q in range(2)]
        S.dma("pool", wcvdt[:, :, :], ssm_w_in[:, 6144:6176].rearrange("(k p) n -> p k n", p=128), W=["wdt"])
        for sub in range(nsub):
            for q in range(2):
                for n in range(2):
                    pb, kb = P[n], "P%d" % n
                    for k in range(8):
                        S.op("pe", lambda e, k=k, n=n, q=q, pb=pb, sub=sub: e.matmul(
                            pb[:, 0:512], lhsT=hb[:, k, sub * 128:(sub + 1) * 128], rhs=wA[:, q, k, n * 512:(n + 1) * 512],
                            start=(k == 0), stop=(k == 7)), R=rkz[q] + [("hb", k)], W=[kb])
                    S.op("act", lambda e, n=n, pb=pb: e.activation(out=tmp[:, n, :], in_=pb[:, 0:512], func=AF.Silu),
                         R=[kb], W=[("tmp", n)])
                    S.op("dve", lambda e, n=n, q=q, sub=sub: e.tensor_copy(
                        out=zs[:, sub, q * 1024 + n * 512:q * 1024 + (n + 1) * 512], in_=tmp[:, n, :]),
                        R=[("tmp", n)], W=KWB + ["zs"])
            for k in range(8):
                S.op("pe", lambda e, k=k, sub=sub: e.matmul(P[7][:, 0:32], lhsT=hb[:, k, sub * 128:(sub + 1) * 128], rhs=wcvdt[:, k, :],
                                                            start=(k == 0), stop=(k == 7)), R=["wdt", ("hb", k)], W=["P7"])
            S.op("dve", lambda e: e.tensor_tensor(out=sm[:, 5, :], in0=P[7][:, 0:32], in1=prm[:, 0, :], op=ALU.add), R=["P7", "prm"], W=["sm5"])
            S.op("dve", lambda e: e.tensor_scalar(out=sm[:, 6, :], in0=sm[:, 5, :], scalar1=-1.0, scalar2=None, op0=ALU.mult),
                 R=["sm5"], W=["sm6"])
            S.op("dve", lambda e: e.tensor_tensor(out=sm[:, 6, :], in0=sm[:, 6, :], in1=sm[:, 5, :], op=ALU.max),
                 R=["sm5", "sm6"], W=["sm6"])
            S.op("act", lambda e: e.activation(out=sm[:, 6, :], in_=sm[:, 6, :], func=AF.Exp, scale=-1.0), R=["sm6"], W=["sm6"])
            S.op("act", lambda e: e.activation(out=sm[:, 6, :], in_=sm[:, 6, :], func=AF.Ln, bias=cs[:, 641:642]), R=["sm6", "cs"], W=["sm6"])
            S.op("dve", lambda e: e.tensor_scalar(out=sm[:, 5, :], in0=sm[:, 5, :], scalar1=0.0, scalar2=None, op0=ALU.max),
                 R=["sm5"], W=["sm5"])
            S.op("dve", lambda e, sub=sub: e.tensor_tensor(out=dts[:, sub, :], in0=sm[:, 5, :], in1=sm[:, 6, :], op=ALU.add),
                 R=["sm5", "sm6"], W=["dts"])
        ngv = statf[:, 0:2048]
        S.dma("sp", ngv, bass.AP(tensor=ssm_norm_g.tensor, offset=0, ap=[[0, 128], [1, 2048]]), W=["mean", "msq", "rstd", "cG", "cB"])
        HBK = HK
        snat = wAf32[:, 0:2048].rearrange("p (c n) -> p c n", c=16)
        sT = wAf32[:, 2048:4096]
        sTb = wAf[:, 8192:10240]
        CmT = wAf[:, 10240:11264].rearrange("p (g t) -> p g t", g=8)
        xdm = wAf[:, 11264:13312]
        Esq = wAf32[:, 6656:6784]
        etq = wAf32[:, 6784:6816]
        WAK = KWA[0] + KWA[1]

        def ssd_sample_states(csl):
            S.op("dve", lambda e: e.memset(zb[:, 0, :], 0.0), W=[("zb", 0)])
            for sq in range(NSEQ_S):
                S.dma("sp", snat, state_ssm[sq].rearrange("(c p) n -> p c n", p=128), W=(WAK if sq == 0 else []) + ["snat"])
                for c4 in range(4):
                    pb, kb = P[4 + c4], "P%d" % (4 + c4)
                    for cc in range(4):
                        c = c4 * 4 + cc
                        S.op("pe", lambda e, c=c, cc=cc, pb=pb: e.transpose(out=pb[:, cc * 128:(cc + 1) * 128], in_=snat[:, c, :], identity=ident),
                             R=WAK + ["snat", "cs"], W=[kb])
                    S.op("dve", lambda e, c4=c4, pb=pb: e.tensor_copy(out=sT[:, c4 * 512:(c4 + 1) * 512], in_=pb[:, 0:512]), R=[kb], W=["sT"])
                    S.op("act", lambda e, c4=c4: e.activation(out=sTb[:, c4 * 512:(c4 + 1) * 512], in_=sT[:, c4 * 512:(c4 + 1) * 512],
                                                              func=AF.Identity), R=["sT"], W=["sTb"])
                S.op("dve", lambda e: e.memset(CmT, 0.0), W=["CmT"])
                S.op("dve", lambda e, sq=sq: e.tensor_copy(out=CmT[:, :, sq * 8:(sq + 1) * 8], in_=BCt[:, 8:16, sq * 8:(sq + 1) * 8]),
                     R=[("BCt", 8 + g) for g in range(8)], W=["CmT"])
                for g in range(8):
                    if sq == 0 and g % 2 == 0:
                        S.op("pe", lambda e, g=g: e.matmul(P[g // 2][:, 0:512], lhsT=ones_b1[:, :], rhs=zb[:, 0, :], start=True, stop=False),
                             R=[("zb", 0), "ones_b1"], W=["P%d" % (g // 2)])
                    S.op("pe", lambda e, g=g, sq=sq: e.matmul(P[g // 2][:, (g % 2) * 256:(g % 2 + 1) * 256], lhsT=CmT[:, g, :],
                                                              rhs=sTb[:, g * 256:(g + 1) * 256], start=False,
                                                              stop=(sq == NSEQ_S - 1 and g % 2 == 1)), R=["CmT", "sTb"], W=["P%d" % (g // 2)])
                S.op("dve", lambda e, sq=sq: e.tensor_scalar(out=Esq, in0=onesf, scalar1=lastm[:, sq:sq + 1], scalar2=None, op0=ALU.mult),
                     R=["cs", "cs2"], W=["Esq"])
                S.op("dve", lambda e, sq=sq: e.tensor_scalar(out=xdm, in0=xdtd, scalar1=seqm[:, sq:sq + 1], scalar2=None, op0=ALU.mult),
                     R=["xdtd", "cs2"], W=["xdm"])
                S.op("pe", lambda e: e.matmul(P[7][:, 0:32], lhsT=Esq, rhs=sm[:, 1, :], start=True, stop=True), R=["Esq", "acs"], W=["P7"])
                S.op("act", lambda e: e.activation(out=etq, in_=P[7][:, 0:32], func=AF.Exp), R=["P7"], W=["etq"])
                for g in range(8):
                    S.op("pe", lambda e, g=g: e.matmul(P[4 + g // 2][:, (g % 2) * 256:(g % 2 + 1) * 256], lhsT=Btok[:, 0, g * 128:(g + 1) * 128],
                                                       rhs=xdm[:, g * 256:(g + 1) * 256], start=True, stop=True),
                         R=KWB + ["xtok", "xdm"], W=["P%d" % (4 + g // 2)])
                for q in range(4):
                    sv = sT[:, q * 512:(q + 1) * 512].rearrange("p (h d) -> p h d", h=8)
                    S.op("dve", lambda e, sv=sv, q=q: e.tensor_tensor(out=sv, in0=sv, in1=etq[:, q * 8:(q + 1) * 8].unsqueeze(2).to_broadcast([128, 8, 64]),
                                                                    op=ALU.mult), R=["etq", "sT"], W=["sT"])
                    S.op("dve", lambda e, q=q: e.tensor_tensor(out=sT[:, q * 512:(q + 1) * 512], in0=sT[:, q * 512:(q + 1) * 512],
                                                               in1=P[4 + q][:, 0:512], op=ALU.add), R=["P%d" % (4 + q), "sT"], W=["sT"])
                for c4 in range(4):
                    pb, kb = P[4 + c4], "P%d" % (4 + c4)
                    for cc in range(4):
                        c = c4 * 4 + cc
                        S.op("pe", lambda e, c=c, cc=cc, pb=pb: e.transpose(out=pb[:, cc * 128:(cc + 1) * 128], in_=sT[:, c * 128:(c + 1) * 128],
                                                                           identity=ident), R=["sT", "cs"], W=[kb])
                    S.op("dve", lambda e, c4=c4, pb=pb: e.tensor_copy(out=snat[:, c4 * 4:(c4 + 1) * 4, :],
                                                                     in_=pb[:, 0:512].rearrange("p (a b) -> p a b", a=4)), R=[kb], W=["snat"])
                S.dma("sp", ssm_s[sq].rearrange("(c p) n -> p c n", p=128), snat, R=["snat"], W=["out"])
            S.op("dve", lambda e: e.memset(etq, 0.0), W=WAK + ["sT", "sTb", "CmT", "xdm", "Esq", "etq", "snat"])
            for q in range(4):
                S.op("dve", lambda e, q=q: e.tensor_tensor(
                    out=rd[:, 0:512].rearrange("p (h d) -> p h d", h=8), in0=P[q][:, 0:512].rearrange("p (h d) -> p h d", h=8),
                    in1=sm[:, 2, q * 8:(q + 1) * 8].unsqueeze(2).to_broadcast([128, 8, 64]), op=ALU.mult),
                    R=["P%d" % q, "eacs"], W=["rd"])
                S.op("dve", lambda e, q=q: e.tensor_tensor(out=yv[:, q * 512:(q + 1) * 512], in0=yv[:, q * 512:(q + 1) * 512], in1=rd[:, 0:512],
                                                           op=ALU.add), R=["rd", "xin0", "xin1"], W=["xin0", "xin1"])
        for sub in range(nsub):
            csl = slice(sub * 128, (sub + 1) * 128)
            S.op("dve", lambda e, sub=sub: e.tensor_tensor(out=sm[:, 0, :], in0=dts[:, sub, :], in1=prm[:, 1, :], op=ALU.mult),
                 R=["dts", "prm"], W=["adt"])
            S.op("pe", lambda e: e.matmul(P[7][:, 0:32], lhsT=Um, rhs=sm[:, 0, :], start=True, stop=True), R=["adt", "cs", "cs2"], W=["P7"])
            S.op("pe", lambda e: e.matmul(P[7][:, 32:64], lhsT=ATm, rhs=sm[:, 0, :], start=True, stop=True), R=["adt", "cs", "cs2"], W=["P7"])
            S.op("dve", lambda e: e.tensor_copy(out=sm[:, 1, :], in_=P[7][:, 0:32]), R=["P7"], W=["acs"])
            S.op("act", lambda e: e.activation(out=sm[:, 2, :], in_=P[7][:, 0:32], func=AF.Exp), R=["P7"], W=["eacs"])
            S.op("dve", lambda e: e.tensor_tensor(out=sm[:, 3, :], in0=P[7][:, 32:64], in1=sm[:, 1, :], op=ALU.subtract),
                 R=["P7", "acs"], W=["dec"])
            S.op("act", lambda e: e.activation(out=sm[:, 3, :], in_=sm[:, 3, :], func=AF.Exp), R=["dec"], W=["dec"])
            S.op("act", lambda e: e.activation(out=sm[:, 4, :], in_=P[7][:, 32:64], func=AF.Exp), R=["P7"], W=["etot"])
            xv = xtok[:, sub, :].rearrange("p (h d) -> p h d", h=32)
            S.op("dve", lambda e, xv=xv, sub=sub: e.tensor_tensor(
                out=xdt.rearrange("p (h d) -> p h d", h=32), in0=xv, in1=dts[:, sub, :].unsqueeze(2).to_broadcast([128, 32, 64]),
                op=ALU.mult), R=KWB + ["xtok", "dts"], W=HBK + ["xdt"])
            S.op("dve", lambda e: e.tensor_tensor(
                out=xdtd.rearrange("p (h d) -> p h d", h=32), in0=xdt.rearrange("p (h d) -> p h d", h=32),
                in1=sm[:, 3, :].unsqueeze(2).to_broadcast([128, 32, 64]), op=ALU.mult), R=["xdt", "dec"], W=HBK + ["xdtd"])
            for g in range(8):
                S.op("pe", lambda e, g=g, csl=csl: e.matmul(P[g // 4][:, (g % 4) * 128:(g % 4 + 1) * 128], lhsT=BCt[:, g, csl],
                                                            rhs=BCt[:, 8 + g, csl], start=True, stop=True),
                     R=[("BCt", g), ("BCt", 8 + g)], W=["P%d" % (g // 4)])
            for q in range(2):
                S.op("dve", lambda e, q=q: e.tensor_copy(out=CBt[:, q * 4:(q + 1) * 4, :], in_=P[q][:, 0:512].rearrange("p (a b) -> p a b", a=4)),
                     R=["P%d" % q], W=["CBt", ("PTf", 0), ("PTf", 1)])
            for hf in range(2):
                for hh in range(16):
                    h = hf * 16 + hh
                    b = h % 2
                    S.op("dve", lambda e, h=h, b=b: e.tensor_scalar(out=tmp[:, b, 0:128], in0=SLm, scalar1=sm[:, 0, h:h + 1], scalar2=None,
                                                                      op0=ALU.mult), R=["adt", "cs2"], W=[("tmp", b)])
                    S.op("pe", lambda e, b=b: e.matmul(P[4 + b][:, 0:128], lhsT=tmp[:, b, 0:128], rhs=Um, start=True, stop=False),
                         R=[("tmp", b), "cs", "cs2"], W=["P%d" % (4 + b)])
                    S.op("pe", lambda e, b=b: e.matmul(P[4 + b][:, 0:128], lhsT=ident, rhs=NGm, start=False, stop=True),
                         R=["cs", "cs2"], W=["P%d" % (4 + b)])
                    S.op("act", lambda e, b=b: e.activation(out=tmp[:, b, 128:256], in_=P[4 + b][:, 0:128], func=AF.Exp),
                         R=["P%d" % (4 + b)], W=[("tmpL", b)])
                    S.op("dve", lambda e, b=b, h=h: e.tensor_tensor(out=MT[:, b, 0:128], in0=tmp[:, b, 128:256], in1=CBt[:, h // 4, :], op=ALU.mult),
                         R=[("tmpL", b), "CBt"], W=[("MT", b)])
                    S.op("pe", lambda e, b=b, h=h, hh=hh: e.matmul(P[hh // 8][:, (hh % 8) * 64:(hh % 8 + 1) * 64], lhsT=MT[:, b, 0:128],
                                                                 rhs=xdt[:, h * 64:(h + 1) * 64], start=True, stop=True),
                         R=[("MT", b), "xdt"], W=["P%d" % (hh // 8)])
                for q in range(2):
                    S.op("dve", lambda e, q=q, hf=hf: e.tensor_copy(out=yv[:, hf * 1024 + q * 512:hf * 1024 + (q + 1) * 512], in_=P[q][:, 0:512]),
                         R=["P%d" % q], W=["xin%d" % hf])
                if not sample:
                    for gi in range(4):
                        g = hf * 4 + gi
                        S.op("pe", lambda e, g=g, gi=gi, csl=csl: e.matmul(P[2 + gi // 2][:, (gi % 2) * 256:(gi % 2 + 1) * 256], lhsT=BCt[:, 8 + g, csl],
                                                                         rhs=stateTb[:, g * 256:(g + 1) * 256], start=True, stop=True),
                             R=[("BCt", 8 + g), "stateTb"], W=["P%d" % (2 + gi // 2)])
                    for q in range(2):
                        c0 = hf * 1024 + q * 512
                        S.op("dve", lambda e, q=q, hf=hf, c0=c0: e.tensor_tensor(
                            out=rd[:, 0:512].rearrange("p (h d) -> p h d", h=8), in0=P[2 + q][:, 0:512].rearrange("p (h d) -> p h d", h=8),
                            in1=sm[:, 2, hf * 16 + q * 8:hf * 16 + (q + 1) * 8].unsqueeze(2).to_broadcast([128, 8, 64]), op=ALU.mult),
                            R=["P%d" % (2 + q), "eacs"], W=["rd"])
                        S.op("dve", lambda e, c0=c0: e.tensor_tensor(out=yv[:, c0:c0 + 512], in0=yv[:, c0:c0 + 512], in1=rd[:, 0:512], op=ALU.add),
                             R=["rd", "xin%d" % hf], W=["xin%d" % hf])
                    for gi in range(4):
                        g = hf * 4 + gi
                        S.op("pe", lambda e, g=g, gi=gi, sub=sub: e.matmul(P[6 + gi // 2][:, (gi % 2) * 256:(gi % 2 + 1) * 256],
                                                                         lhsT=Btok[:, sub, g * 128:(g + 1) * 128], rhs=xdtd[:, g * 256:(g + 1) * 256],
                                                                         start=True, stop=True), R=KWB + ["xtok", "xdtd"], W=["P%d" % (6 + gi // 2)])
                    for q in range(2):
                        c0 = hf * 1024 + q * 512
                        sv = stateT[:, c0:c0 + 512].rearrange("p (h d) -> p h d", h=8)
                        S.op("dve", lambda e, sv=sv, hf=hf, q=q: e.tensor_tensor(
                            out=sv, in0=sv, in1=sm[:, 4, hf * 16 + q * 8:hf * 16 + (q + 1) * 8].unsqueeze(2).to_broadcast([128, 8, 64]),
                            op=ALU.mult), R=["etot", "stateT"], W=["stateT"])
                        S.op("dve", lambda e, c0=c0, q=q: e.tensor_tensor(out=stateT[:, c0:c0 + 512], in0=stateT[:, c0:c0 + 512],
                                                                        in1=P[6 + q][:, 0:512], op=ALU.add), R=["P%d" % (6 + q), "stateT"], W=["stateT"])
                        S.op("act", lambda e, c0=c0: e.activation(out=stateTb[:, c0:c0 + 512], in_=stateT[:, c0:c0 + 512], func=AF.Identity),
                             R=["stateT"], W=["stateTb"])
            if sample:
                ssd_sample_states(csl)
            xv = xtok[:, sub, :].rearrange("p (h d) -> p h d", h=32)
            S.op("dve", lambda e, xv=xv: e.tensor_tensor(out=ysq.rearrange("p (h d) -> p h d", h=32), in0=xv,
                                                         in1=prm[:, 2, :].unsqueeze(2).to_broadcast([128, 32, 64]), op=ALU.mult),
                 R=KWB + ["xtok", "prm"], W=HBK + ["xdt", "xdtd", "ysq"])
            S.op("dve", lambda e: e.tensor_tensor(out=yv, in0=yv, in1=ysq, op=ALU.add), R=["ysq", "xin0", "xin1"], W=["xin0", "xin1"])
            S.op("dve", lambda e, sub=sub: e.tensor_tensor(out=yv, in0=yv, in1=zs[:, sub, :], op=ALU.mult), R=KWB + ["zs", "xin0", "xin1"],
                 W=["xin0", "xin1"])
            S.op("dve", lambda e: e.tensor_tensor(out=ysq, in0=yv, in1=yv, op=ALU.mult), R=["xin0", "xin1"], W=["ysq"])
            S.op("dve", lambda e: e.tensor_reduce(out=sm[:, 7, 0:8], in_=ysq.rearrange("p (g d) -> p g d", g=8), axis=AX.X, op=ALU.add),
                 R=["ysq"], W=["sm7"])
            S.op("act", lambda e: e.activation(out=sm[:, 7, 0:8], in_=sm[:, 7, 0:8], func=AF.Ln, bias=epsb[:, 0:1], scale=1.0 / 256),
                 R=["sm7", "epsb"], W=["sm7"])
            S.op("act", lambda e: e.activation(out=sm[:, 7, 0:8], in_=sm[:, 7, 0:8], func=AF.Exp, scale=-0.5), R=["sm7"], W=["sm7"])
            S.op("dve", lambda e: e.tensor_tensor(out=yv.rearrange("p (g d) -> p g d", g=8), in0=yv.rearrange("p (g d) -> p g d", g=8),
                                                  in1=sm[:, 7, 0:8].unsqueeze(2).to_broadcast([128, 8, 256]), op=ALU.mult),
                 R=["sm7", "xin0", "xin1"], W=["xin0", "xin1"])
            S.op("dve", lambda e: e.tensor_tensor(out=yv, in0=yv, in1=ngv, op=ALU.mult), R=["mean", "xin0", "xin1"], W=["xin0", "xin1"])
            for c4 in range(4):
                pb, kb = P[c4 % 2], "P%d" % (c4 % 2)
                for cc in range(4):
                    c = c4 * 4 + cc
                    S.op("pe", lambda e, c=c, cc=cc, pb=pb: e.transpose(out=pb[:, cc * 128:(cc + 1) * 128], in_=yv[:, c * 128:(c + 1) * 128],
                                                                       identity=ident), R=["xin0", "xin1", "cs"], W=[kb])
                S.op("dve", lambda e, c4=c4, pb=pb, sub=sub: e.tensor_copy(
                    out=ynT[:, c4 * 4:(c4 + 1) * 4, sub * 128:(sub + 1) * 128], in_=pb[:, 0:512].rearrange("p (a b) -> p a b", a=4)),
                    R=[kb], W=[("aT", j) for j in range(16)])
        if last and not sample:
            for c4 in range(4):
                pb, kb = P[c4 % 2], "P%d" % (c4 % 2)
                for cc in range(4):
                    c = c4 * 4 + cc
                    S.op("pe", lambda e, c=c, cc=cc, pb=pb: e.transpose(out=pb[:, cc * 128:(cc + 1) * 128], in_=stateT[:, c * 128:(c + 1) * 128],
                                                                       identity=ident), R=["stateT", "cs"], W=[kb])
                S.op("dve", lambda e, pb=pb: e.tensor_copy(out=yv[:, 0:512], in_=pb[:, 0:512]), R=[kb], W=["xin0"])
                S.dma("sp", ssm_p[c4 * 512:(c4 + 1) * 512, :].rearrange("(a p) n -> p a n", p=128),
                      yv[:, 0:512].rearrange("p (a n) -> p a n", a=4), R=["xin0"], W=["out"])
        rko = [load_w(wA[:, q], ssm_w_out[q * 1024:(q + 1) * 1024, :], KWA[q], "so%d" % q) for q in range(2)]
        for c in range(8):
            po, ko = P[6 + c % 2], "P%d" % (6 + c % 2)
            for k in range(16):
                S.op("pe", lambda e, k=k, c=c, po=po: e.matmul(
                    po[:, 0:T], lhsT=wA[:, k // 8, k % 8, c * 128:(c + 1) * 128], rhs=ynT[:, k, 0:T],
                    start=(k == 0), stop=(k == 15)), R=rko[k // 8] + [("aT", k)], W=[ko])
            S.op("dve", lambda e, c=c, po=po: e.scalar_tensor_tensor(
                out=hT[:, c, 0:T], in0=hT[:, c, 0:T], scalar=ALPHA, in1=po[:, 0:T],
                op0=ALU.mult, op1=ALU.add), R=[ko, ("hT", c)], W=[("hT", c)])
        layer_norm(li, T)

    def run_tile(src, dst, r0, T):
        load_tile(src, r0, T)
        only = os.environ.get("MK_ONLY", "")
        for L in ([int(only)] if only else range(LAYERS)):
            pre = None
            if os.environ.get("MK_FFN", "1") != "0":
                pf = None
                if MIX and os.environ.get("MK_PREF", "1") == "1":
                    pf = (cmlp_prefetch(L // 3), moba_prefetch, ssd_prefetch)[L % 3]
                pre = ffn(L * 2 + 0, L * 3 + 0, T, prefetch=pf)
            if MIX and L % 3 == 0:
                cmlp(L // 3, L * 3 + 1, T, sample=(T == 128), pre=pre)
            if MIX and L % 3 == 1:
                moba(L * 3 + 1, T, r0, sample=(T == 128), pre=pre)
            if MIX and L % 3 == 2:
                ssd(L * 3 + 1, T, r0, sample=(T == 128), last=(r0 + T == NT * 512), pre=pre)
            if os.environ.get("MK_FFN", "1") != "0":
                ffn(L * 2 + 1, L * 3 + 2, T)
        store_tile(dst, r0, T)

    for it in range(NT):
        run_tile(xp, y_p, it * 512, 512)
    if SAMPLE:
        run_tile(xs, y_s, 0, 128)

    S.finish(["out"])
    with nc.Block() as block:
        S.replay(block)
    st.close()
    print("instructions (incl waits):", S.n_ins)
    return nc


def make_consts():
    c = np.zeros((128, 1024), np.float32)
    c[:, 0:128] = np.eye(128, dtype=np.float32)
    i = np.arange(128)
    c[:, 128:256] = (i[:, None] <= i[None, :]).astype(np.float32)
    c[:, 256:384] = (i[None, :] <= i[:, None]).astype(np.float32)
    c[:, 384:512] = ((i[:, None] // 8 == i[None, :] // 8) & (i[None, :] % 8 <= i[:, None] % 8)).astype(np.float32)
    c[:, 512:640] = np.where((i[:, None] // 8 == i[None, :] // 8) & (i[:, None] % 8 <= i[None, :] % 8), 0.0, -30000.0)
    c[:, 640] = i.astype(np.float32)
    c[:, 641:769] = 1.0
    return c


def make_cstb():
    c = np.zeros((128, 6144), np.float32)
    for n in range(32):
        c[n, n * 128:(n + 1) * 128] = 1.0
    key = np.arange(128)[:, None]
    q = np.arange(512)[None, :]
    for cc in range(4):
        kp = cc * 128 + key
        same = (kp // 256) == (q // 256)
        c[:, 4096 + cc * 512:4096 + (cc + 1) * 512] = np.where(same & (kp > q), -30000.0, 0.0)
    return c


def make_consts2():
    c = np.zeros((128, 768), np.float32)
    i = np.arange(128)
    same = (i[:, None] // 8) == (i[None, :] // 8)
    c[:, 0:128] = (i[:, None] > i[None, :])
    c[:, 128:256] = np.where(i[:, None] <= i[None, :], 0.0, -30000.0)
    c[:, 256:384] = same & (i[:, None] <= i[None, :])
    c[:, 384:512] = same & (i[:, None] > i[None, :])
    c[:, 512:640] = same
    c[:, 640:656] = (i[:, None] // 8) == np.arange(16)[None, :]
    c[:, 656:672] = i[:, None] == (np.arange(16)[None, :] * 8 + 7)
    return c


def make_rope():
    pos = np.concatenate([np.arange(SEQ), 2048 + (np.arange(128) % 8)]).astype(np.float32)
    inv = (10000.0 ** (-np.arange(64, dtype=np.float32) / 64)).astype(np.float32)
    ang = (pos[:, None] * inv[None, :]).astype(np.float32)
    return np.concatenate([np.cos(ang), np.sin(ang)], axis=1).astype(np.float32)


def kernel(**inp):
    NT = int(os.environ.get("MK_NT", "16"))
    LAYERS = int(os.environ.get("MK_LAYERS", "4"))
    nc = build(NT=NT, LAYERS=LAYERS, SAMPLE=os.environ.get("MK_SAMPLE", "1") == "1")
    cst = make_consts()
    cstb = make_cstb()
    ropec = make_rope()
    cst2 = make_consts2()
    sel8d = np.zeros((8, 1024), np.float32)
    for n in range(8):
        sel8d[n, n * 128:(n + 1) * 128] = 1.0
    ck = np.ascontiguousarray(np.asarray(inp["cache_k"])[0]).reshape(2560 * 128, D)
    cvv = np.ascontiguousarray(np.asarray(inp["cache_v"])[0]).reshape(2560 * 128, D)
    f = lambda a: np.ascontiguousarray(np.asarray(a))
    in_maps = []
    for c in range(8):
        m = {
            "xp": f(inp["x_prompt"][c % 2]),
            "xs": f(inp["x_sample"][c * 16:(c + 1) * 16].reshape(128, D)),
            "ln_g": f(inp["ln_g"].reshape(12, D)),
            "ln_b": f(inp["ln_b"].reshape(12, D)),
            "ffn_w_in": f(inp["ffn_w_in"].reshape(8, D, 2 * DFF)),
            "ffn_w_out": f(inp["ffn_w_out"].reshape(8, DFF, D)),
            "cst": cst,
        }
        for k in ("cmlp_w_in", "cmlp_ln_g", "cmlp_ln_b", "cmlp_w_s", "cmlp_b_s", "cmlp_w_out"):
            m[k] = f(inp[k])
        m["moba_w_qkv"] = f(inp["moba_w_qkv"][0])
        m["moba_w_out"] = f(inp["moba_w_out"][0])
        m["ropec"] = ropec
        m["cstb"] = cstb
        m["sel8d"] = sel8d
        m["cache_k"] = ck
        m["cache_v"] = cvv
        m["page_table"] = f(inp["page_table"][c * 16:(c + 1) * 16]).astype(np.int32)
        m["ssm_w_in"] = f(inp["ssm_w_in"][0])
        m["ssm_w_conv"] = f(inp["ssm_w_conv"][0])
        m["ssm_b_conv"] = f(inp["ssm_b_conv"][0]).reshape(1, 4096)
        m["ssm_dt_bias"] = f(inp["ssm_dt_bias"][0]).reshape(1, 32)
        m["ssm_a_log"] = f(inp["ssm_a_log"][0]).reshape(1, 32)
        m["ssm_d"] = f(inp["ssm_d"][0]).reshape(1, 32)
        m["ssm_norm_g"] = f(inp["ssm_norm_g"][0]).reshape(1, 2048)
        m["ssm_w_out"] = f(inp["ssm_w_out"][0])
        m["state_conv"] = f(inp["state_conv"][0, c * 16:(c + 1) * 16]).reshape(48, 4096)
        m["state_ssm"] = f(inp["state_ssm"][0, c * 16:(c + 1) * 16]).reshape(16, 2048, 128)
        m["cst2"] = cst2
        in_maps.append(m)
    res = run_bass_kernel_spmd(nc, in_maps, core_ids=list(range(8)))
    R = res.results
    y_prompt = np.stack([R[0]["y_p"], R[1]["y_p"]]).astype(np.float32)
    y_sample = np.concatenate([np.asarray(R[c]["y_s"]).reshape(16, 8, D) for c in range(8)], axis=0).astype(np.float32)
    cv = np.stack([np.concatenate([np.asarray(R[c]["cv_s"]).reshape(2, 16, 8, D)[j] for c in range(8)], axis=0)
                   for j in range(2)]).astype(np.float32)
    kp = np.stack([np.asarray(R[c]["k_p"]).reshape(SEQ, 8, 128) for c in range(2)])[None].astype(np.float32)
    vp = np.stack([np.asarray(R[c]["v_p"]).reshape(SEQ, 8, 128) for c in range(2)])[None].astype(np.float32)
    ks = np.concatenate([np.asarray(R[c]["k_s"]).reshape(16, 8, 8, 128) for c in range(8)], axis=0)[None].astype(np.float32)
    vs = np.concatenate([np.asarray(R[c]["v_s"]).reshape(16, 8, 8, 128) for c in range(8)], axis=0)[None].astype(np.float32)
    conv_p = np.stack([np.asarray(R[c]["conv_p"]).reshape(3, 4096) for c in range(2)])[None].astype(np.float32)
    ssm_p = np.stack([np.asarray(R[c]["ssm_p"]).reshape(32, 64, 128) for c in range(2)])[None].astype(np.float32)
    conv_s = np.concatenate([np.asarray(R[c]["conv_s"]).reshape(16, 3, 4096) for c in range(8)], axis=0)[None].astype(np.float32)
    ssm_s = np.concatenate([np.asarray(R[c]["ssm_s"]).reshape(16, 32, 64, 128) for c in range(8)], axis=0)[None].astype(np.float32)
    return (y_prompt, y_sample, cv, kp, vp, ks, vs, conv_p, ssm_p, conv_s, ssm_s)
```

```python
import os
import numpy as np
import concourse.bass as bass
import concourse.mybir as mybir
from concourse.bass_utils import run_bass_kernel_spmd

F32 = mybir.dt.float32
BF16 = mybir.dt.bfloat16
I32 = mybir.dt.int32
ALU = mybir.AluOpType
AF = mybir.ActivationFunctionType
AX = mybir.AxisListType

D = 1024
DFF = 2816
NFC = DFF // 128
DEPTH = 4
ALPHA = (2 * DEPTH) ** 0.25
LN_EPS = 1e-5
SEQ = 8192
NSEQ_S = 16
TS = 8


class Sched:
    def __init__(self, nc, stack):
        self.nc = nc
        self.eng = {}
        for name, e in (("pe", nc.tensor), ("act", nc.scalar), ("dve", nc.vector),
                        ("pool", nc.gpsimd), ("sp", nc.sync)):
            sem = stack.enter_context(nc.semaphore("sem_" + name))
            self.eng[name] = dict(name=name, e=e, sem=sem, count=0, waited={}, ops=[])
        self.NDS = 32
        self.dsem = [stack.enter_context(nc.semaphore("dsem%d" % i)) for i in range(self.NDS)]
        self.dval = [0] * self.NDS
        self.dnext2 = [0, 0]
        self.res = {}
        self.n_ins = 0

    def _deps(self, R, W):
        toks = []
        for k in R:
            r = self.res.get(k)
            if r is not None and r["w"] is not None:
                toks.append(r["w"])
        for k in W:
            r = self.res.get(k)
            if r is not None:
                if r["w"] is not None:
                    toks.append(r["w"])
                toks.extend(r["r"].values())
        return toks

    def _wait_list(self, E, toks):
        waits = []
        for (sem, val, owner) in toks:
            if owner == E["name"] and owner == "pe":
                continue
            if E["waited"].get(owner, 0) < val:
                E["waited"][owner] = val
                waits.append((sem, val))
        return waits

    def _mark(self, R, W, tok):
        for k in R:
            r = self.res.setdefault(k, dict(w=None, r={}))
            r["r"][tok[2]] = tok
        for k in W:
            self.res[k] = dict(w=tok, r={})

    def op(self, engname, fn, R=(), W=()):
        E = self.eng[engname]
        if engname in ("act", "dve"):
            W = list(W) + [k for k in R if isinstance(k, str) and len(k) == 2 and k[0] == "P" and k[1].isdigit()]
        waits = self._wait_list(E, self._deps(R, W))
        E["count"] += 1
        tok = (E["sem"], E["count"], engname)
        sem = E["sem"]

        def run(e, waits=waits, fn=fn, sem=sem):
            for (s, v) in waits:
                e.wait_ge(s, v)
            fn(e).then_inc(sem, 1)
        E["ops"].append(run)
        self._mark(R, W, tok)
        self.n_ins += 1 + len(waits)

    def dma(self, qname, out, in_, R=(), W=(), **kw):
        E = self.eng[qname]
        half = self.NDS // 2
        qi = 0 if qname == "sp" else 1
        i = qi * half + self.dnext2[qi]
        self.dnext2[qi] = (self.dnext2[qi] + 1) % half
        toks = self._deps(R, W)
        if self.dval[i] > 0:
            toks.append((self.dsem[i], self.dval[i], "d%d" % i))
        waits = self._wait_list(E, toks)
        self.dval[i] += 16
        tok = (self.dsem[i], self.dval[i], "d%d" % i)
        ds = self.dsem[i]

        def run(e, waits=waits, ds=ds, out=out, in_=in_, kw=kw):
            for (s, v) in waits:
                e.wait_ge(s, v)
            e.dma_start(out=out, in_=in_, **kw).then_inc(ds, 16)
        E["ops"].append(run)
        self._mark(R, W, tok)
        self.n_ins += 1 + len(waits)

    def idma(self, out, in_, off_ap, R=(), W=()):
        E = self.eng["pool"]
        half = self.NDS // 2
        i = half + self.dnext2[1]
        self.dnext2[1] = (self.dnext2[1] + 1) % half
        toks = self._deps(R, W)
        if self.dval[i] > 0:
            toks.append((self.dsem[i], self.dval[i], "d%d" % i))
        waits = self._wait_list(E, toks)
        self.dval[i] += 16
        tok = (self.dsem[i], self.dval[i], "d%d" % i)
        ds = self.dsem[i]

        def run(e, waits=waits, ds=ds):
            for (s_, v) in waits:
                e.wait_ge(s_, v)
            e.indirect_dma_start(out=out, out_offset=None, in_=in_,
                                 in_offset=bass.IndirectOffsetOnAxis(ap=off_ap, axis=0)).then_inc(ds, 16)
        E["ops"].append(run)
        self._mark(R, W, tok)
        self.n_ins += 1 + len(waits)

    def finish(self, final_keys):
        toks = [(self.dsem[i], self.dval[i], "d%d" % i) for i in range(self.NDS) if self.dval[i] > 0]
        E = self.eng["sp"]
        waits = self._wait_list(E, toks)

        def run(e, waits=waits):
            for (s, v) in waits:
                e.wait_ge(s, v)
        E["ops"].append(run)

    def replay(self, block):
        def mk(name):
            ops = self.eng[name]["ops"]

            def f(e):
                for o in ops:
                    o(e)
            return f
        block.tensor(mk("pe"))
        block.scalar(mk("act"))
        block.vector(mk("dve"))
        block.gpsimd(mk("pool"))
        block.sync(mk("sp"))


def build(NT=16, LAYERS=4, SAMPLE=True):
    from contextlib import ExitStack
    nc = bass.Bass("TRN2", target_bir_lowering=False)
    st = ExitStack()

    def din(name, shape, dt=F32):
        return nc.dram_tensor(name, list(shape), dt, kind="ExternalInput").ap()

    def dout(name, shape, dt=F32):
        return nc.dram_tensor(name, list(shape), dt, kind="ExternalOutput").ap()

    xp = din("xp", [SEQ, D])
    xs = din("xs", [128, D])
    ln_g = din("ln_g", [12, D])
    ln_b = din("ln_b", [12, D])
    ffn_w_in = din("ffn_w_in", [8, D, 2 * DFF])
    ffn_w_out = din("ffn_w_out", [8, DFF, D])
    cst = din("cst", [128, 1024])
    cmlp_w_in = din("cmlp_w_in", [2, D, 2 * D])
    cmlp_ln_g = din("cmlp_ln_g", [2, D])
    cmlp_ln_b = din("cmlp_ln_b", [2, D])
    cmlp_w_s = din("cmlp_w_s", [2, 8, 128, 128])
    cmlp_b_s = din("cmlp_b_s", [2, 8, 128])
    cmlp_w_out = din("cmlp_w_out", [2, D, D])
    cv_s = dout("cv_s", [2, 128, D])
    moba_w_qkv = din("moba_w_qkv", [D, 3 * D])
    moba_w_out = din("moba_w_out", [D, D])
    ropec = din("ropec", [SEQ + 128, 128])
    cstb = din("cstb", [128, 6144])
    k_p = dout("k_p", [SEQ, D])
    v_p = dout("v_p", [SEQ, D])
    k_s = dout("k_s", [128, D])
    v_s = dout("v_s", [128, D])
    cache_k = din("cache_k", [2560 * 128, D])
    cache_v = din("cache_v", [2560 * 128, D])
    page_table = din("page_table", [16, 16], I32)
    sel8d = din("sel8d", [8, 1024])
    ssm_w_in = din("ssm_w_in", [D, 6176])
    ssm_w_conv = din("ssm_w_conv", [4, 4096])
    ssm_b_conv = din("ssm_b_conv", [1, 4096])
    ssm_dt_bias = din("ssm_dt_bias", [1, 32])
    ssm_a_log = din("ssm_a_log", [1, 32])
    ssm_d = din("ssm_d", [1, 32])
    ssm_norm_g = din("ssm_norm_g", [1, 2048])
    ssm_w_out = din("ssm_w_out", [2048, D])
    state_conv = din("state_conv", [48, 4096])
    state_ssm = din("state_ssm", [16, 2048, 128])
    cst2 = din("cst2", [128, 768])
    conv_p = dout("conv_p", [3, 4096])
    ssm_p = dout("ssm_p", [2048, 128])
    conv_s = dout("conv_s", [48, 4096])
    ssm_s = dout("ssm_s", [16, 2048, 128])
    KT_hist = nc.dram_tensor("KT_hist", [8, 128, SEQ], BF16).ap()
    V_hist = nc.dram_tensor("V_hist", [SEQ, D], BF16).ap()
    y_p = dout("y_p", [SEQ, D])
    y_s = dout("y_s", [128, D])

    def sb(name, shape, dt):
        return st.enter_context(nc.sbuf_tensor(name, list(shape), dt))

    def ps(name, shape, dt=F32):
        return st.enter_context(nc.psum_tensor(name, list(shape), dt))

    S = Sched(nc, st)
    DBG = os.environ.get("MK_DBG", "")
    MIX = os.environ.get("MK_MIX", "1") == "1"

    hT = sb("hT", [128, 8, 512], F32)
    hb = sb("hb", [128, 8, 512], BF16)
    aT = sb("aT", [128, NFC, 512], BF16)
    wA = sb("wA", [128, 2, 8, 1024], BF16)
    wB = sb("wB", [128, NFC, 1024], BF16)
    xin = sb("xin", [128, 2, D], F32)
    tmp = sb("tmp", [128, 2, 512], F32)
    zb = sb("zb", [128, 2, 512], BF16)
    stat = sb("stat", [128, 4, 512], F32)
    cs = sb("cs", [128, 1024], F32)
    ident_b = sb("ident_b", [128, 128], BF16)
    onesb = sb("onesb", [128, 128], BF16)
    lng = sb("lng", [128, 96], F32)
    lnb = sb("lnb", [128, 96], F32)
    P = [ps("ps%d" % i, [128, 512]) for i in range(8)]

    ident = cs[:, 0:128]

    wkeys = {}

    def WKEY(name, idx=None):
        return wkeys.get((name, idx), [])

    if os.environ.get("MK_PRECAST", "1") == "1":
        def precast(name, src, rows_per=128):
            shp = list(src.shape)
            dst = nc.dram_tensor(name + "_b", shp, BF16).ap()
            if len(shp) == 2:
                pairs = [(None, dst, src)]
            else:
                pairs = [(i, dst[i], src[i]) for i in range(shp[0])]
            for (idx, d2, s2) in pairs:
                nr = d2.shape[0]
                for r in range(0, nr, rows_per):
                    S.dma("pool", d2[r:min(nr, r + rows_per), :], s2[r:min(nr, r + rows_per), :], W=[("wc", name, idx, r)])
                    wkeys.setdefault((name, idx), []).append(("wc", name, idx, r))
            return dst
        ffn_w_in = precast("ffn_w_in", ffn_w_in)
        ffn_w_out = precast("ffn_w_out", ffn_w_out)
        if MIX:
            cmlp_w_in = precast("cmlp_w_in", cmlp_w_in)
            cmlp_w_out = precast("cmlp_w_out", cmlp_w_out)
            moba_w_qkv = precast("moba_w_qkv", moba_w_qkv)
            moba_w_out = precast("moba_w_out", moba_w_out)
            ssm_w_in = precast("ssm_w_in", ssm_w_in)
            ssm_w_out = precast("ssm_w_out", ssm_w_out)

    S.dma("sp", cs[:, :], cst[:, :], W=["cs"])
    epsb = sb("epsb", [128, 2], F32)
    if "B" not in DBG:
        S.op("dve", lambda e: e.tensor_copy(out=ident_b[:, :], in_=cs[:, 0:128]), R=["cs"], W=["ident_b"])
        S.op("dve", lambda e: e.memset(onesb[:, :], 1.0 / D), W=["onesb"])
        S.op("dve", lambda e: e.memset(epsb[:, :], LN_EPS), W=["epsb"])

    def load_cols(dst, src_rows, nrows, key):
        S.dma("sp", xin[0:nrows, 0, 0:128], src_rows, R=[], W=["xin0"])
        S.op("pe", lambda e: e.transpose(out=P[7][:, 0:nrows], in_=xin[0:nrows, 0, 0:128],
                                         identity=ident[0:nrows, 0:nrows]),
             R=["xin0", "cs"], W=["P7"])
        S.op("dve", lambda e: e.tensor_copy(out=dst, in_=P[7][:, 0:nrows]), R=["P7"], W=[key])

    if os.environ.get("MK_COLS", "1") == "1":
        load_cols(lng[:, :], ln_g.rearrange("l (c p) -> (l c) p", p=128), 96, "lng")
        load_cols(lnb[:, :], ln_b.rearrange("l (c p) -> (l c) p", p=128), 96, "lnb")

    WTp = sb("WTp", [128, 2, 8, 128], BF16)
    WTs = sb("WTs", [128, 2, 8, 128], BF16)
    statf = stat[:, :, :].rearrange("p a b -> p (a b)")
    cGv = statf[:, 0:D]
    cBv = statf[:, D:2 * D]
    bSs = sb("bSs", [128, 2, 8, 8], F32)
    arena = sb("arena", [128, 9216], BF16)
    vnb = arena[:, 0:4 * D].rearrange("p (a b) -> p a b", a=4)
    st6 = sb("st6", [128, 16], F32)
    Rw = sb("Rw", [128, 8, 8], F32)
    wBf = wB[:, :, :].rearrange("p a b -> p (a b)")
    wBv = wBf[:, 0:8 * 2048].rearrange("p (k n) -> p k n", k=8)
    if MIX:
        for j in range(2):
            S.dma("sp", bSs[:, j, :, :], bass.AP(tensor=cmlp_b_s.tensor, offset=j * 1024, ap=[[0, 128], [128, 8], [1, 8]]),
                  W=["bSs"])
            xv = xin[:, 0, :].rearrange("p (g s) -> p g s", g=8)
            S.dma("sp", xv, cmlp_w_s[j].rearrange("g t s -> t g s"), W=["xin0"])
            for g in range(8):
                S.op("dve", lambda e, g=g: e.tensor_tensor(out=xin[:, 0, g * 128:(g + 1) * 128],
                                                           in0=xin[:, 0, g * 128:(g + 1) * 128], in1=cs[:, 256:384], op=ALU.mult),
                     R=["cs", "xin0"], W=["xin0"])
                pb = P[g % 2]
                S.op("pe", lambda e, g=g, pb=pb: e.transpose(out=pb[:, 0:128], in_=xin[:, 0, g * 128:(g + 1) * 128], identity=ident),
                     R=["xin0", "cs"], W=["P%d" % (g % 2)])
                S.op("dve", lambda e, g=g, pb=pb, j=j: e.tensor_copy(out=WTp[:, j, g, :], in_=pb[:, 0:128]),
                     R=["P%d" % (g % 2)], W=["WTp"])
            for q in range(16):
                S.dma("sp", Rw[q * 8:(q + 1) * 8, :, :], cmlp_w_s[j, :, 0:8, 0:8].rearrange("g t s -> t g s"), W=["Rw"])
            for g in range(8):
                S.op("dve", lambda e, g=g: e.tensor_tensor(
                    out=xin[:, 1, g * 128:(g + 1) * 128].rearrange("p (a b) -> p a b", a=16),
                    in0=Rw[:, g, :].unsqueeze(1).to_broadcast([128, 16, 8]),
                    in1=cs[:, 384:512].rearrange("p (a b) -> p a b", a=16), op=ALU.mult),
                    R=["Rw", "cs"], W=["xin1"])
                pb = P[2 + g % 2]
                S.op("pe", lambda e, g=g, pb=pb: e.transpose(out=pb[:, 0:128], in_=xin[:, 1, g * 128:(g + 1) * 128], identity=ident),
                     R=["xin1", "cs"], W=["P%d" % (2 + g % 2)])
                S.op("dve", lambda e, g=g, pb=pb, j=j: e.tensor_copy(out=WTs[:, j, g, :], in_=pb[:, 0:128]),
                     R=["P%d" % (2 + g % 2)], W=["WTs"])

    SEL = wBf[0:32, 16384:20480]
    CMt = wBf[:, 20480:22528].rearrange("p (a b) -> p a b", a=4)
    ones_b1 = sb("ones_b1", [128, 128], BF16)
    kmT = sb("kmT", [128, 8, 32], F32)
    kmTb = sb("kmTb", [128, 8, 32], BF16)
    rc = sb("rc", [128, 128], F32)
    xpad = sb("xpad", [128, 2, 520], F32)
    PTf = xpad[:, :, 0:512]
    rd = sb("rd", [128, 512], F32)
    gsb = sb("gsb", [128, 8, 32], F32)
    biasq = sb("biasq", [128, 8, 32], F32)
    mx = sb("mx", [128, 8, 8], F32)
    OT = arena[:, 0:4096].rearrange("p (a b) -> p a b", a=8)
    ktt = arena[:, 4096:8192].rearrange("p (a b) -> p a b", a=8)
    PT = arena[:, 8192:9216].rearrange("p (a b) -> p a b", a=2)
    QT = aT[:, 0:8, :]
    biasT = aT[:, 8:16, :]
    wAf = wA[:, :, :, :].rearrange("p a b c -> p (a b c)")
    wBq = wBf[:, 0:8192].rearrange("p (k n) -> p k n", k=8)
    KWA = [[("wA", 0, 0), ("wA", 0, 1)], [("wA", 1, 0), ("wA", 1, 1)]]
    KWB = [("wB", 0), ("wB", 1)]
    SCALE = 128 ** -0.5
    arena_f = arena[:, 4096:8192].bitcast(F32)
    arena_i = arena[:, 8192:9216].bitcast(I32)
    sel8 = arena_f[0:8, 0:1024]
    Pown = arena_f[:, 1024:1152]
    PTs = arena_f[:, 1152:1280].rearrange("p (a b) -> p a b", a=2)
    fin = arena_f[:, 1280:1408].rearrange("p (a b) -> p a b", a=2)
    smal = arena_f[:, 1408:1472]
    bT8 = arena_f[0:8, 1472:1536]
    ptf = arena_f[:, 1536:1552]
    ptb = arena_i[:, 0:16]
    pidx = arena_i[:, 16:32]
    aTf = aT[:, :, :].rearrange("p a b -> p (a b)").bitcast(F32)
    QTf = aTf[:, 0:1024].rearrange("p (a b) -> p a b", a=8)
    KTnf = aTf[:, 1024:2048].rearrange("p (a b) -> p a b", a=8)
    Oown = aTf[:, 2048:3072].rearrange("p (a b) -> p a b", a=8)
    dOwn = aTf[:, 3072:4096].rearrange("p (a b) -> p a b", a=8)
    wBf32 = wBf[:, 0:16384].bitcast(F32)
    Kpg = wBf32.rearrange("p (s j d) -> p s j d", s=2, j=4)
    wAf32 = wAf.bitcast(F32)
    KTp = wAf32[:, 0:2048].rearrange("p (s h k) -> p s h k", s=2, h=8)
    STall = wAf32[:, 2048:3072].rearrange("p (j c) -> p j c", j=16)
    onesf = cs[:, 641:769]
    if MIX:
        S.op("dve", lambda e: e.memset(ones_b1[:, :], 1.0), W=["ones_b1"])

    stateT = sb("stateT", [128, 2048], F32)
    stateTb = sb("stateTb", [128, 2048], BF16)
    halo = sb("halo", [128, 32, 3], F32)
    wcvdt = sb("wcvdt", [128, 8, 32], BF16)
    sm = sb("sm", [128, 8, 32], F32)
    dts = sb("dts", [128, 4, 32], F32)
    prm = sb("prm", [128, 3, 32], F32)
    wcv = sb("wcv", [128, 128], F32)
    bcv = sb("bcv", [128, 32], F32)
    cs2 = sb("cs2", [128, 768], F32)
    SLc, NEGM, UB, SLB, SAME = (cs2[:, i * 128:(i + 1) * 128] for i in range(5))
    seqm, lastm = cs2[:, 640:656], cs2[:, 656:672]
    xtok = wBf[:, 0:8192].rearrange("p (a b) -> p a b", a=4)
    Btok = wBf[:, 8192:12288].rearrange("p (a b) -> p a b", a=4)
    zs = wBf[:, 12288:20480].rearrange("p (a b) -> p a b", a=4)
    BCt = arena[:, 0:8192].rearrange("p (a b) -> p a b", a=16)
    MT = arena[:, 8192:9216].rearrange("p (a b) -> p a b", a=2)
    MTf = arena[:, 8192:9216]
    hbf = hb[:, :, :].rearrange("p a b -> p (a b)")
    xdt, xdtd = hbf[:, 0:2048], hbf[:, 2048:4096]
    ysq = hbf.bitcast(F32)
    yv = xin[:, :, :].rearrange("p a b -> p (a b)")
    ynT = aT[:, 0:16, :]
    xpf = xpad[:, :, :].rearrange("p a b -> p (a b)")
    CBt = xpf[:, 0:1024].rearrange("p (g t) -> p g t", g=8)
    CmT = xpf[:, 520:1032].bitcast(BF16).rearrange("p (g t) -> p g t", g=8) if False else None
    if MIX and (LAYERS >= 3 or os.environ.get("MK_ONLY", "") == "2"):
        S.dma("sp", cs2[:, :], cst2[:, :], W=["cs2"])
        S.dma("sp", prm[:, 0, :], bass.AP(tensor=ssm_dt_bias.tensor, offset=0, ap=[[0, 128], [1, 32]]), W=["prm"])
        S.dma("sp", prm[:, 1, :], bass.AP(tensor=ssm_a_log.tensor, offset=0, ap=[[0, 128], [1, 32]]), W=["prm"])
        S.dma("sp", prm[:, 2, :], bass.AP(tensor=ssm_d.tensor, offset=0, ap=[[0, 128], [1, 32]]), W=["prm"])
        S.op("act", lambda e: e.activation(out=prm[:, 1, :], in_=prm[:, 1, :], func=AF.Exp), R=["prm"], W=["prm"])
        S.op("dve", lambda e: e.tensor_scalar(out=prm[:, 1, :], in0=prm[:, 1, :], scalar1=-1.0, scalar2=None, op0=ALU.mult),
             R=["prm"], W=["prm"])
        load_cols(wcv[:, :], ssm_w_conv.rearrange("k (c p) -> (k c) p", p=128), 128, "wcv")
        load_cols(bcv[:, :], ssm_b_conv.rearrange("o (c p) -> (o c) p", p=128), 32, "bcv")
        S.op("dve", lambda e: e.memset(stateT[:, :], 0.0), W=["stateT"])
        S.op("dve", lambda e: e.memset(stateTb[:, :], 0.0), W=["stateTb"])
        S.op("dve", lambda e: e.memset(halo[:, :, :], 0.0), W=["halo"])

    def load_tile(src, r0, T):
        nsub = T // 128
        for m in range(nsub):
            slot = m % 2
            S.dma("sp", xin[:, slot, :], src[r0 + m * 128: r0 + (m + 1) * 128, :], W=["xin%d" % slot])
            for c in range(8):
                pb = P[c % 2]
                S.op("pe", lambda e, pb=pb, slot=slot, c=c: e.transpose(
                    out=pb[:, 0:128], in_=xin[:, slot, c * 128:(c + 1) * 128], identity=ident),
                    R=["xin%d" % slot, "cs"], W=["P%d" % (c % 2)])
                S.op("dve", lambda e, pb=pb, c=c, m=m: e.tensor_copy(
                    out=hT[:, c, m * 128:(m + 1) * 128], in_=pb[:, 0:128]),
                    R=["P%d" % (c % 2)], W=[("hT", c)])
                if "C" in DBG:
                    S.op("act", lambda e, pb=pb, c=c, m=m: e.activation(
                        out=hb[:, c, m * 128:(m + 1) * 128], in_=pb[:, 0:128], func=AF.Identity),
                        R=["P%d" % (c % 2)], W=[("hb", c)])
                elif "E" not in DBG:
                    S.op("act", lambda e, pb=pb, c=c, m=m: e.activation(
                        out=hb[:, c, m * 128:(m + 1) * 128], in_=hT[:, c, m * 128:(m + 1) * 128], func=AF.Identity),
                        R=[("hT", c)], W=[("hb", c)])
                elif "A" not in DBG:
                    S.op("act", lambda e, pb=pb, c=c, m=m: e.copy(
                        out=hb[:, c, m * 128:(m + 1) * 128], in_=pb[:, 0:128]),
                        R=["P%d" % (c % 2)], W=[("hb", c)])

    def store_tile(dst, r0, T):
        nsub = T // 128
        for m in range(nsub):
            slot = m % 2
            for c in range(8):
                pb = P[c % 2]
                S.op("pe", lambda e, pb=pb, c=c, m=m: e.transpose(
                    out=pb[:, 0:128], in_=hT[:, c, m * 128:(m + 1) * 128], identity=ident),
                    R=[("hT", c), "cs"], W=["P%d" % (c % 2)])
                S.op("dve", lambda e, pb=pb, c=c, slot=slot: e.tensor_copy(
                    out=xin[:, slot, c * 128:(c + 1) * 128], in_=pb[:, 0:128]),
                    R=["P%d" % (c % 2)], W=["xin%d" % slot])
            S.dma("sp", dst[r0 + m * 128: r0 + (m + 1) * 128, :], xin[:, slot, :],
                  R=["xin%d" % slot], W=["out"])

    def layer_norm(li, T):
        for c in range(8):
            s = c % 2
            S.op("act", lambda e, c=c, s=s: e.copy(out=zb[:, s, 0:T], in_=hT[:, c, 0:T]),
                 R=[("hT", c)], W=[("zb", s)])
            S.op("pe", lambda e, c=c, s=s: e.matmul(P[4][:, 0:T], lhsT=onesb[:, :], rhs=zb[:, s, 0:T],
                                                  start=(c == 0), stop=(c == 7)),
                 R=[("zb", s), "onesb"], W=["P4"])
        for c in range(8):
            s = c % 2
            S.op("act", lambda e, c=c, s=s: e.activation(out=zb[:, s, 0:T], in_=hT[:, c, 0:T], func=AF.Square),
                 R=[("hT", c)], W=[("zb", s)])
            S.op("pe", lambda e, c=c, s=s: e.matmul(P[5][:, 0:T], lhsT=onesb[:, :], rhs=zb[:, s, 0:T],
                                                  start=(c == 0), stop=(c == 7)),
                 R=[("zb", s), "onesb"], W=["P5"])
        mean, msq, rstd = stat[:, 0, 0:T], stat[:, 1, 0:T], stat[:, 2, 0:T]
        S.op("dve", lambda e: e.tensor_copy(out=mean, in_=P[4][:, 0:T]), R=["P4"], W=["mean", "cG"])
        S.op("dve", lambda e: e.tensor_tensor(out=msq, in0=mean, in1=mean, op=ALU.mult), R=["mean"], W=["msq", "cG"])
        S.op("dve", lambda e: e.tensor_tensor(out=rstd, in0=P[5][:, 0:T], in1=msq, op=ALU.subtract),
             R=["P5", "msq"], W=["rstd", "cB"])
        S.op("act", lambda e: e.activation(out=rstd, in_=rstd, func=AF.Ln, bias=epsb[:, 0:1]), R=["rstd", "epsb"], W=["rstd"])
        S.op("act", lambda e: e.activation(out=rstd, in_=rstd, func=AF.Exp, scale=-0.5), R=["rstd"], W=["rstd"])
        for c in range(8):
            s = c % 2
            t = tmp[:, s, 0:T]
            S.op("dve", lambda e, c=c, t=t: e.tensor_tensor(out=t, in0=hT[:, c, 0:T], in1=mean, op=ALU.subtract),
                 R=[("hT", c), "mean"], W=[("tmp", s)])
            S.op("dve", lambda e, t=t: e.tensor_tensor(out=t, in0=t, in1=rstd, op=ALU.mult),
                 R=[("tmp", s), "rstd"], W=[("tmp", s)])
            col = li * 8 + c
            S.op("dve", lambda e, c=c, t=t, col=col: e.tensor_scalar(
                out=hT[:, c, 0:T], in0=t, scalar1=lng[:, col:col + 1], scalar2=lnb[:, col:col + 1],
                op0=ALU.mult, op1=ALU.add), R=[("tmp", s), "lng", "lnb"], W=[("hT", c)])
            S.op("act", lambda e, c=c: e.copy(out=hb[:, c, 0:T], in_=hT[:, c, 0:T]),
                 R=[("hT", c)], W=[("hb", c)])

    GROUPS = [(0, 4), (4, 4), (8, 4), (12, 4), (16, 4), (20, 2)]

    def ffn(fi, li, T, prefetch=None):
        w_in = ffn_w_in[fi]
        w_out = ffn_w_out[fi]

        def load_group(gi):
            j0, nj = GROUPS[gi]
            slot = gi % 2
            w = nj * 128
            S.dma("pool", wA[:, slot, :, 0:w],
                  w_in[:, j0 * 128: j0 * 128 + w].rearrange("(k p) n -> p k n", p=128),
                  R=WKEY("ffn_w_in", fi), W=[("wA", slot, 0)])
            S.dma("pool", wA[:, slot, :, 512:512 + w],
                  w_in[:, DFF + j0 * 128: DFF + j0 * 128 + w].rearrange("(k p) n -> p k n", p=128),
                  R=WKEY("ffn_w_in", fi), W=[("wA", slot, 1)])

        load_group(0)
        for gi, (j0, nj) in enumerate(GROUPS):
            if gi + 1 < len(GROUPS):
                load_group(gi + 1)
            if gi == 1:
                for q in range(2):
                    S.dma("pool", wB[:, q * 11:(q + 1) * 11, :],
                          w_out[q * 11 * 128:(q + 1) * 11 * 128, :].rearrange("(j p) n -> p j n", p=128),
                          R=WKEY("ffn_w_out", fi), W=[("wB", q)])
            slot = gi % 2
            for jj in range(nj):
                j = j0 + jj
                pg, pu = P[(j % 2) * 2], P[(j % 2) * 2 + 1]
                kg, ku = "P%d" % ((j % 2) * 2), "P%d" % ((j % 2) * 2 + 1)
                for k in range(8):
                    S.op("pe", lambda e, k=k, jj=jj, pg=pg, slot=slot: e.matmul(
                        pg[:, 0:T], lhsT=wA[:, slot, k, jj * 128:(jj + 1) * 128], rhs=hb[:, k, 0:T],
                        start=(k == 0), stop=(k == 7)), R=[("wA", slot, 0), ("hb", k)], W=[kg])
                for k in range(8):
                    S.op("pe", lambda e, k=k, jj=jj, pu=pu, slot=slot: e.matmul(
                        pu[:, 0:T], lhsT=wA[:, slot, k, 512 + jj * 128:512 + (jj + 1) * 128], rhs=hb[:, k, 0:T],
                        start=(k == 0), stop=(k == 7)), R=[("wA", slot, 1), ("hb", k)], W=[ku])
                s = j % 2
                S.op("act", lambda e, pg=pg, s=s: e.activation(out=tmp[:, s, 0:T], in_=pg[:, 0:T], func=AF.Silu),
                     R=[kg], W=[("tmp", s)])
                S.op("dve", lambda e, pu=pu, s=s, j=j: e.scalar_tensor_tensor(
                    out=aT[:, j, 0:T], in0=tmp[:, s, 0:T], scalar=0.5, in1=pu[:, 0:T],
                    op0=ALU.mult, op1=ALU.mult), R=[("tmp", s), ku], W=[("aT", j)])
        pre = prefetch() if prefetch is not None else None
        for c in range(8):
            po = P[6 + c % 2]
            ko = "P%d" % (6 + c % 2)
            for j in range(NFC):
                S.op("pe", lambda e, j=j, c=c, po=po: e.matmul(
                    po[:, 0:T], lhsT=wB[:, j, c * 128:(c + 1) * 128], rhs=aT[:, j, 0:T],
                    start=(j == 0), stop=(j == NFC - 1)), R=[("wB", j // 11), ("aT", j)], W=[ko])
            S.op("dve", lambda e, c=c, po=po: e.scalar_tensor_tensor(
                out=hT[:, c, 0:T], in0=hT[:, c, 0:T], scalar=ALPHA, in1=po[:, 0:T],
                op0=ALU.mult, op1=ALU.add), R=[ko, ("hT", c)], W=[("hT", c)])
        layer_norm(li, T)
        return pre

    def cmlp(j, li, T, sample, pre=None):
        nsub = T // 128
        if pre is None:
            pre = cmlp_prefetch(j)()
        rk_u, rk_vv = pre
        rk_o = load_w(wBq, cmlp_w_out[j], KWB, "co", WKEY("cmlp_w_out", j))
        WK = rk_u
        S.dma("sp", cGv, bass.AP(tensor=cmlp_ln_g.tensor, offset=j * D, ap=[[0, 128], [1, D]]), W=["mean", "msq", "cG"])
        S.dma("sp", cBv, bass.AP(tensor=cmlp_ln_b.tensor, offset=j * D, ap=[[0, 128], [1, D]]), W=["rstd", "cB"])
        for jc in range(8):
            pb, kb = P[jc % 2], "P%d" % (jc % 2)
            for k in range(8):
                S.op("pe", lambda e, k=k, jc=jc, pb=pb: e.matmul(
                    pb[:, 0:T], lhsT=wA[:, 0, k, jc * 128:(jc + 1) * 128], rhs=hb[:, k, 0:T],
                    start=(k == 0), stop=(k == 7)), R=rk_u + [("hb", k)], W=[kb])
            s2 = jc % 2
            S.op("act", lambda e, s2=s2, pb=pb: e.activation(out=tmp[:, s2, 0:T], in_=pb[:, 0:T], func=AF.Gelu),
                 R=[kb], W=[("tmp", s2)])
            S.op("dve", lambda e, jc=jc, s2=s2: e.tensor_copy(out=aT[:, jc, 0:T], in_=tmp[:, s2, 0:T]),
                 R=[("tmp", s2)], W=[("aT", jc)])
        for m in range(nsub):
            for n in range(2):
                pb, kb = P[2 + n], "P%d" % (2 + n)
                for k in range(8):
                    S.op("pe", lambda e, k=k, n=n, m=m, pb=pb: e.matmul(
                        pb[:, 0:512], lhsT=hb[:, k, m * 128:(m + 1) * 128],
                        rhs=wA[:, 1, k, n * 512:(n + 1) * 512],
                        start=(k == 0), stop=(k == 7)), R=rk_vv + [("hb", k)], W=[kb])
                S.op("act", lambda e, n=n, pb=pb: e.activation(out=xin[:, 0, n * 512:(n + 1) * 512], in_=pb[:, 0:512],
                                                              func=AF.Gelu), R=[kb], W=["xin0"])
            for n in range(2):
                S.op("dve", lambda e, n=n: e.bn_stats(out=st6[:, n * 6:(n + 1) * 6], in_=xin[:, 0, n * 512:(n + 1) * 512]),
                     R=["xin0"], W=["st6"])
            S.op("dve", lambda e: e.bn_aggr(out=st6[:, 12:14], in_=st6[:, 0:12]), R=["st6"], W=["st6"])
            S.op("act", lambda e: e.activation(out=st6[:, 14:15], in_=st6[:, 13:14], func=AF.Ln, bias=epsb[:, 0:1]),
                 R=["st6", "epsb"], W=["st6"])
            S.op("act", lambda e: e.activation(out=st6[:, 14:15], in_=st6[:, 14:15], func=AF.Exp, scale=-0.5),
                 R=["st6"], W=["st6"])
            S.op("dve", lambda e: e.tensor_scalar(out=xin[:, 0, :], in0=xin[:, 0, :], scalar1=st6[:, 12:13],
                                                  scalar2=st6[:, 14:15], op0=ALU.subtract, op1=ALU.mult),
                 R=["st6", "xin0"], W=["xin0"])
            S.op("dve", lambda e: e.tensor_tensor(out=xin[:, 0, :], in0=xin[:, 0, :], in1=cGv, op=ALU.mult),
                 R=["cG", "xin0"], W=["xin0"])
            S.op("dve", lambda e: e.tensor_tensor(out=xin[:, 0, :], in0=xin[:, 0, :], in1=cBv, op=ALU.add),
                 R=["cB", "xin0"], W=["xin0"])
            if sample:
                S.dma("sp", cv_s[j], xin[:, 0, :], R=["xin0"], W=["out"])
            S.op("act", lambda e, m=m: e.activation(out=vnb[:, m, :], in_=xin[:, 0, :], func=AF.Identity),
                 R=["xin0"], W=[("vnb", m)])
        WT = WTs if sample else WTp
        bSp = xpad[:, :, :].rearrange("p a b -> p (a b)")[:, 0:1024].rearrange("p (g t) -> p g t", g=8)
        if not sample:
            S.dma("sp", bSp, bass.AP(tensor=cmlp_b_s.tensor, offset=j * 1024, ap=[[0, 128], [128, 8], [1, 128]]),
                  W=["bSp", ("PTf", 0), ("PTf", 1)])
        for g in range(8):
            pb, kb = P[4 + g % 2], "P%d" % (4 + g % 2)
            for m in range(nsub):
                S.op("pe", lambda e, g=g, m=m, pb=pb: e.matmul(
                    pb[:, m * 128:(m + 1) * 128], lhsT=vnb[:, m, g * 128:(g + 1) * 128], rhs=WT[:, j, g, :],
                    start=True, stop=True), R=[("vnb", m), "WTp", "WTs"], W=[kb])
            s2 = g % 2
            bw = 8 if sample else 128
            S.op("dve", lambda e, g=g, pb=pb, s2=s2, bw=bw: e.tensor_tensor(
                out=tmp[:, s2, 0:T].rearrange("p (a b) -> p a b", b=bw),
                in0=pb[:, 0:T].rearrange("p (a b) -> p a b", b=bw),
                in1=(bSs[:, j, g, :] if sample else bSp[:, g, :]).unsqueeze(1).to_broadcast([128, T // bw, bw]), op=ALU.add),
                R=[kb, "bSp", "bSs"], W=[("tmp", s2)])
            S.op("dve", lambda e, g=g, s2=s2: e.tensor_tensor(out=aT[:, 8 + g, 0:T], in0=tmp[:, s2, 0:T], in1=aT[:, g, 0:T],
                                                           op=ALU.mult), R=[("tmp", s2), ("aT", g)], W=[("aT", 8 + g)])
        WK2 = rk_o
        for c in range(8):
            po, ko = P[6 + c % 2], "P%d" % (6 + c % 2)
            for k in range(8):
                S.op("pe", lambda e, k=k, c=c, po=po: e.matmul(
                    po[:, 0:T], lhsT=wBq[:, k, c * 128:(c + 1) * 128], rhs=aT[:, 8 + k, 0:T],
                    start=(k == 0), stop=(k == 7)), R=WK2 + [("aT", 8 + k)], W=[ko])
            S.op("dve", lambda e, c=c, po=po: e.scalar_tensor_tensor(
                out=hT[:, c, 0:T], in0=hT[:, c, 0:T], scalar=ALPHA, in1=po[:, 0:T],
                op0=ALU.mult, op1=ALU.add), R=[ko, ("hT", c)], W=[("hT", c)])
        layer_norm(li, T)

    def load_w(dst, src, canon, tag, wk=()):
        keys = list(canon)
        for q in range(2):
            kq = ("x", tag, q)
            S.dma("pool", dst[:, q * 4:(q + 1) * 4, :], src[q * 512:(q + 1) * 512, :].rearrange("(k p) n -> p k n", p=128),
                  R=list(wk), W=(list(canon) if q == 0 else []) + [kq])
            keys.append(kq)
        return keys

    def cmlp_prefetch(j):
        def f():
            return (load_w(wA[:, 0], cmlp_w_in[j][:, 0:D], KWA[0], "cu", WKEY("cmlp_w_in", j)), load_w(wA[:, 1], cmlp_w_in[j][:, D:2 * D], KWA[1], "cv", WKEY("cmlp_w_in", j)))
        return f

    def moba_prefetch():
        return (load_w(wA[:, 0], moba_w_qkv[:, 0:D], KWA[0], "mq", WKEY("moba_w_qkv")), load_w(wA[:, 1], moba_w_qkv[:, D:2 * D], KWA[1], "mk", WKEY("moba_w_qkv")))

    def ssd_prefetch():
        return [load_w(wA[:, g], ssm_w_in[:, 2048 + g * 1024:2048 + (g + 1) * 1024], KWA[g], "sx%d" % g, WKEY("ssm_w_in")) for g in range(2)]

    def out_proj(w_src, rhs_chunks, rhs_keys, li, T):
        rk = load_w(wA[:, 0], w_src, KWA[0], "wo", WKEY("moba_w_out"))
        for c in range(8):
            po, ko = P[6 + c % 2], "P%d" % (6 + c % 2)
            nk = len(rhs_chunks)
            for k in range(nk):
                S.op("pe", lambda e, k=k, c=c, po=po: e.matmul(
                    po[:, 0:T], lhsT=wA[:, 0, k, c * 128:(c + 1) * 128], rhs=rhs_chunks[k],
                    start=(k == 0), stop=(k == nk - 1)), R=rk + [rhs_keys[k]], W=[ko])
            S.op("dve", lambda e, c=c, po=po: e.scalar_tensor_tensor(
                out=hT[:, c, 0:T], in0=hT[:, c, 0:T], scalar=ALPHA, in1=po[:, 0:T],
                op0=ALU.mult, op1=ALU.add), R=[ko, ("hT", c)], W=[("hT", c)])
        layer_norm(li, T)

    def moba(li, T, r0, sample, pre=None):
        nsub = T // 128
        rk_q, rk_k = pre if pre is not None else moba_prefetch()
        rk_v = load_w(wBq, moba_w_qkv[:, 2 * D:3 * D], KWB, "mv", WKEY("moba_w_qkv"))
        qst, kst = xin[:, 0, :], xin[:, 1, :]
        vst = statf[:, 0:D]
        rsc = statf[:, D:D + 256].rearrange("p (h d) -> p h d", h=4)
        VK = ["mean", "msq", "cG"]
        RK = ["rstd", "cB"]
        kout, vout = (k_s, v_s) if sample else (k_p, v_p)
        ro = 0 if sample else r0
        blk0 = r0 // 256
        for m in range(nsub):
            S.dma("sp", rc[:, :], ropec[(SEQ if sample else r0) + m * 128:(SEQ if sample else r0) + (m + 1) * 128, :], W=["rc"])
            cosb = rc[:, 0:64].unsqueeze(1).to_broadcast([128, 4, 64])
            sinb = rc[:, 64:128].unsqueeze(1).to_broadcast([128, 4, 64])
            for (wv, rk, stage, skey) in ((wA[:, 0], rk_q, qst, "xin0"), (wA[:, 1], rk_k, kst, "xin1")):
                for n in range(2):
                    pb, kb = P[n], "P%d" % n
                    for k in range(8):
                        S.op("pe", lambda e, k=k, n=n, m=m, pb=pb, wv=wv: e.matmul(
                            pb[:, 0:512], lhsT=hb[:, k, m * 128:(m + 1) * 128], rhs=wv[:, k, n * 512:(n + 1) * 512],
                            start=(k == 0), stop=(k == 7)), R=rk + [("hb", k)], W=[kb])
                    psv = pb[:, 0:512].rearrange("p (h t d) -> p h t d", h=4, t=2)
                    dv = stage[:, n * 512:(n + 1) * 512].rearrange("p (h t d) -> p h t d", h=4, t=2)
                    x1, x2, d1, d2 = psv[:, :, 0, :], psv[:, :, 1, :], dv[:, :, 0, :], dv[:, :, 1, :]
                    S.op("dve", lambda e, d1=d1, x1=x1: e.tensor_tensor(out=d1, in0=x1, in1=cosb, op=ALU.mult), R=[kb, "rc"], W=[skey])
                    S.op("dve", lambda e, x2=x2: e.tensor_tensor(out=rsc, in0=x2, in1=sinb, op=ALU.mult), R=[kb, "rc"], W=RK)
                    S.op("dve", lambda e, d1=d1: e.tensor_tensor(out=d1, in0=d1, in1=rsc, op=ALU.subtract), R=RK + [skey], W=[skey])
                    S.op("dve", lambda e, d2=d2, x2=x2: e.tensor_tensor(out=d2, in0=x2, in1=cosb, op=ALU.mult), R=[kb, "rc"], W=[skey])
                    S.op("dve", lambda e, x1=x1: e.tensor_tensor(out=rsc, in0=x1, in1=sinb, op=ALU.mult), R=[kb, "rc"], W=RK)
                    S.op("dve", lambda e, d2=d2: e.tensor_tensor(out=d2, in0=d2, in1=rsc, op=ALU.add), R=RK + [skey], W=[skey])
            for n in range(2):
                pb, kb = P[n], "P%d" % n
                for k in range(8):
                    S.op("pe", lambda e, k=k, n=n, m=m, pb=pb: e.matmul(
                        pb[:, 0:512], lhsT=hb[:, k, m * 128:(m + 1) * 128], rhs=wBq[:, k, n * 512:(n + 1) * 512],
                        start=(k == 0), stop=(k == 7)), R=rk_v + [("hb", k)], W=[kb])
                S.op("act", lambda e, n=n, pb=pb: e.activation(out=vst[:, n * 512:(n + 1) * 512], in_=pb[:, 0:512], func=AF.Identity),
                     R=[kb], W=VK)
            S.dma("sp", kout[ro + m * 128:ro + (m + 1) * 128, :], kst, R=["xin1"], W=["out"])
            S.dma("sp", vout[ro + m * 128:ro + (m + 1) * 128, :], vst, R=VK, W=["out"])
            if not sample:
                S.dma("pool", V_hist[r0 + m * 128:r0 + (m + 1) * 128, :], vst, R=VK, W=[("vhist", m)])
            for (stage, skey, dstT, dkey, scl) in ((qst, "xin0", QTf if sample else QT, "QT", SCALE),
                                                   (kst, "xin1", KTnf if sample else ktt, "ktt", 1.0)):
                for g in range(2):
                    pb, kb = P[2 + g], "P%d" % (2 + g)
                    for hh in range(4):
                        h = g * 4 + hh
                        S.op("pe", lambda e, h=h, hh=hh, pb=pb, stage=stage: e.transpose(
                            out=pb[:, hh * 128:(hh + 1) * 128], in_=stage[:, h * 128:(h + 1) * 128], identity=ident),
                            R=[skey, "cs"], W=[kb])
                    S.op("dve", lambda e, g=g, pb=pb, dstT=dstT, scl=scl, m=m: e.tensor_scalar(
                        out=dstT[:, g * 4:(g + 1) * 4, m * 128:(m + 1) * 128],
                        in0=pb[:, 0:512].rearrange("p (a b) -> p a b", a=4), scalar1=scl, scalar2=None, op0=ALU.mult),
                        R=[kb], W=[dkey])
        if sample:
            S.dma("sp", sel8, sel8d[:, :], R=["ktt"], W=["sel8", "ktt"])
            for h in range(8):
                S.op("pe", lambda e, h=h: e.matmul(P[0][:, 0:128], lhsT=KTnf[:, h, :], rhs=QTf[:, h, :], start=True, stop=True),
                     R=["QT", "ktt"], W=["P0"])
                S.op("dve", lambda e: e.tensor_tensor(out=Pown, in0=P[0][:, 0:128], in1=cs[:, 512:640], op=ALU.add),
                     R=["P0", "cs"], W=["Pown"])
                S.op("act", lambda e: e.activation(out=Pown, in_=Pown, func=AF.Exp), R=["Pown"], W=["Pown"])
                S.op("pe", lambda e, h=h: e.matmul(P[1][:, 0:128], lhsT=vst[:, h * 128:(h + 1) * 128], rhs=Pown,
                                                   start=True, stop=True), R=VK + ["Pown"], W=["P1"])
                S.op("pe", lambda e, h=h: e.matmul(P[2][:, 0:128], lhsT=onesf, rhs=Pown, start=True, stop=True),
                     R=["cs", "Pown"], W=["P2"])
                S.op("dve", lambda e, h=h: e.tensor_copy(out=Oown[:, h, :], in_=P[1][:, 0:128]), R=["P1"], W=["Oown"])
                S.op("dve", lambda e, h=h: e.tensor_copy(out=dOwn[:, h, :], in_=P[2][:, 0:128]), R=["P2"], W=["dOwn"])
            for sq in range(NSEQ_S):
                S.dma("sp", ptb, bass.AP(tensor=page_table.tensor, offset=sq * 16, ap=[[0, 128], [1, 16]]), W=["ptb"])
                S.op("dve", lambda e: e.tensor_copy(out=ptf, in_=ptb), R=["ptb"], W=["ptf"])
                S.op("dve", lambda e: e.tensor_scalar(out=ptf, in0=ptf, scalar1=128.0, scalar2=cs[:, 640:641],
                                                      op0=ALU.mult, op1=ALU.add), R=["ptf", "cs"], W=["ptf"])
                S.op("dve", lambda e: e.tensor_copy(out=pidx, in_=ptf), R=["ptf"], W=["pidx"])
                for j in range(16):
                    sl, jj = (j // 4) % 2, j % 4
                    if jj == 0:
                        for j2 in range(4):
                            first = (sq == 0 and j < 8)
                            S.idma(Kpg[:, sl, j2, :], cache_k[:, :], pidx[:, j + j2:j + j2 + 1], R=["pidx"],
                                   W=(KWB if first else []) + [("Kpg", sl, j2)])
                    ks = j % 2
                    for g in range(2):
                        pb, kb = P[2 + g], "P%d" % (2 + g)
                        for hh in range(4):
                            h = g * 4 + hh
                            S.op("pe", lambda e, h=h, hh=hh, pb=pb, sl=sl, jj=jj: e.transpose(
                                out=pb[:, hh * 128:(hh + 1) * 128], in_=Kpg[:, sl, jj, h * 128:(h + 1) * 128], identity=ident),
                                R=KWB + [("Kpg", sl, jj), "cs"], W=[kb])
                        S.op("dve" if g == 0 else "act", (lambda e, g=g, pb=pb, ks=ks: e.tensor_copy(
                            out=KTp[:, ks, g * 4:(g + 1) * 4, :], in_=pb[:, 0:512].rearrange("p (a b) -> p a b", a=4))) if g == 0 else
                            (lambda e, g=g, pb=pb, ks=ks: e.activation(
                                out=KTp[:, ks, g * 4:(g + 1) * 4, :], in_=pb[:, 0:512].rearrange("p (a b) -> p a b", a=4),
                                func=AF.Identity)), R=[kb], W=KWA[0] + [("KTp", ks, g)] if (sq == 0 and j < 2) else [("KTp", ks, g)])
                    pb, kb = P[j % 2], "P%d" % (j % 2)
                    for h in range(8):
                        S.op("pe", lambda e, h=h, pb=pb, ks=ks, sq=sq: e.matmul(
                            pb[:, h * 8:(h + 1) * 8], lhsT=KTp[:, ks, h, :], rhs=QTf[:, h, sq * 8:(sq + 1) * 8],
                            start=True, stop=True), R=KWA[0] + [("KTp", ks, h // 4), "QT"], W=[kb])
                    S.op("dve", lambda e, j=j, pb=pb: e.tensor_copy(out=STall[:, j, :], in_=pb[:, 0:64]), R=[kb],
                         W=(KWA[0] if (sq == 0 and j == 0) else []) + [("ST", j)])
                for n in range(8):
                    for t2 in range(2):
                        j = 2 * n + t2
                        S.op("pe", lambda e, n=n, j=j, t2=t2: e.matmul(P[6][0:64, n:n + 1], lhsT=STall[:, j, :], rhs=onesf[:, 0:1],
                                                                     start=(t2 == 0), stop=(t2 == 1)),
                             R=KWA[0] + [("ST", j), "cs"], W=["P6"])
                S.op("dve", lambda e: e.tensor_copy(out=smal[0:64, 0:8], in_=P[6][0:64, 0:8]), R=["P6"], W=["smal"])
                S.op("dve", lambda e: e.max(out=smal[0:64, 8:16], in_=smal[0:64, 0:8]), R=["smal"], W=["smal"])
                S.op("dve", lambda e: e.tensor_tensor(out=smal[0:64, 16:24], in0=smal[0:64, 0:8],
                                                      in1=smal[0:64, 10:11].to_broadcast([64, 8]), op=ALU.is_ge),
                     R=["smal"], W=["smal"])
                S.op("dve", lambda e: e.tensor_scalar(out=smal[0:64, 16:24], in0=smal[0:64, 16:24], scalar1=-1.0, scalar2=30000.0,
                                                      op0=ALU.add, op1=ALU.mult), R=["smal"], W=["smal"])
                S.op("pe", lambda e: e.transpose(out=P[7][0:8, 0:64], in_=smal[0:64, 16:24], identity=ident[0:64, 0:64]),
                     R=["smal", "cs"], W=["P7"])
                S.op("dve", lambda e: e.tensor_copy(out=bT8, in_=P[7][0:8, 0:64]), R=["P7"], W=["bT8"])
                for j in range(16):
                    sl, jj = (j // 4) % 2, j % 4
                    n = j // 2
                    if jj == 0:
                        for j2 in range(4):
                            S.idma(Kpg[:, sl, j2, :], cache_v[:, :], pidx[:, j + j2:j + j2 + 1], R=["pidx"], W=[("Kpg", sl, j2)])
                    b2 = j % 2
                    pb, kb = P[b2], "P%d" % b2
                    S.op("pe", lambda e, n=n, pb=pb: e.matmul(pb[:, 0:64], lhsT=sel8[0:8, n * 128:(n + 1) * 128], rhs=bT8[0:8, 0:64],
                                                              start=True, stop=False), R=["sel8", "bT8"], W=[kb])
                    S.op("pe", lambda e, j=j, pb=pb: e.matmul(pb[:, 0:64], lhsT=ident, rhs=STall[:, j, :], start=False, stop=True),
                         R=KWA[0] + [("ST", j), "cs"], W=[kb])
                    S.op("act", lambda e, pb=pb, b2=b2: e.activation(out=PTs[:, b2, :], in_=pb[:, 0:64], func=AF.Exp),
                         R=[kb], W=[("PTs", b2)])
                    if j == 0:
                        S.op("pe", lambda e: e.matmul(P[4][:, 0:64], lhsT=onesf, rhs=cs[:, 769:833], start=True, stop=False),
                             R=["cs"], W=["P4"])
                    for h in range(8):
                        S.op("pe", lambda e, h=h, j=j, b2=b2, sl=sl, jj=jj: e.matmul(
                            P[4][:, h * 8:(h + 1) * 8], lhsT=Kpg[:, sl, jj, h * 128:(h + 1) * 128], rhs=PTs[:, b2, h * 8:(h + 1) * 8],
                            start=False, stop=(j == 15 and h == 7)), R=KWB + [("Kpg", sl, jj), ("PTs", b2)], W=["P4"])
                    S.op("pe", lambda e, j=j, b2=b2: e.matmul(P[5][:, 0:64], lhsT=onesf, rhs=PTs[:, b2, :],
                                                              start=(j == 0), stop=(j == 15)), R=["cs", ("PTs", b2)], W=["P5"])
                S.op("dve", lambda e, sq=sq: e.tensor_tensor(
                    out=fin[:, 0, :].rearrange("p (a b) -> p a b", a=8), in0=P[4][:, 0:64].rearrange("p (a b) -> p a b", a=8),
                    in1=Oown[:, :, sq * 8:(sq + 1) * 8], op=ALU.add), R=["P4", "Oown"], W=["fin0"])
                S.op("dve", lambda e, sq=sq: e.tensor_tensor(
                    out=fin[:, 1, :].rearrange("p (a b) -> p a b", a=8), in0=P[5][:, 0:64].rearrange("p (a b) -> p a b", a=8),
                    in1=dOwn[:, :, sq * 8:(sq + 1) * 8], op=ALU.add), R=["P5", "dOwn"], W=["fin1"])
                S.op("dve", lambda e: e.reciprocal(out=fin[:, 1, :], in_=fin[:, 1, :]), R=["fin1"], W=["fin1"])
                S.op("dve", lambda e, sq=sq: e.tensor_tensor(
                    out=OT[:, :, sq * 8:(sq + 1) * 8], in0=fin[:, 0, :].rearrange("p (a b) -> p a b", a=8),
                    in1=fin[:, 1, :].rearrange("p (a b) -> p a b", a=8), op=ALU.mult),
                    R=["fin0", "fin1"], W=[("OT", h) for h in range(8)])
        else:
            NC = (r0 + T) // 128
            S.dma("pool", SEL, cstb[0:32, 0:4096], W=KWB + ["SEL"])
            S.dma("pool", CMt, cstb[:, 4096:6144].rearrange("p (a b) -> p a b", a=4), W=["CMt"])
            S.dma("sp", KT_hist[:, :, r0:r0 + T].rearrange("h p t -> p h t"), ktt[:, :, 0:T], R=["ktt"], W=["kthist"])
            S.op("dve", lambda e: e.tensor_reduce(out=kmT[:, :, blk0:blk0 + 2],
                                                  in_=ktt[:, :, 0:T].rearrange("p h (b t) -> p h b t", b=2),
                                                  axis=AX.X, op=ALU.add), R=["ktt"], W=["kmT"])
            S.op("dve", lambda e: e.tensor_scalar(out=kmTb[:, :, blk0:blk0 + 2], in0=kmT[:, :, blk0:blk0 + 2],
                                                  scalar1=1.0 / 256, scalar2=None, op0=ALU.mult), R=["kmT"], W=["kmTb"])
            for m in range(nsub):
                own = blk0 + m // 2
                S.op("dve", lambda e: e.memset(biasq[:, :, :], -30000.0), W=["biasq"])
                if own > 0:
                    if own <= 3:
                        S.op("dve", lambda e, own=own: e.memset(biasq[:, :, 0:own], 0.0), W=["biasq"])
                    else:
                        S.op("dve", lambda e: e.memset(gsb[:, :, :], -1e30), W=["gsb"])
                        for h in range(8):
                            S.op("pe", lambda e, h=h, m=m, own=own: e.matmul(
                                P[6][:, h * 32:h * 32 + own], lhsT=QT[:, h, m * 128:(m + 1) * 128], rhs=kmTb[:, h, 0:own],
                                start=True, stop=True), R=["QT", "kmTb"], W=["P6"])
                        S.op("dve", lambda e, own=own: e.tensor_copy(
                            out=gsb[:, :, 0:own], in_=P[6][:, 0:256].rearrange("p (h n) -> p h n", h=8)[:, :, 0:own]),
                            R=["P6"], W=["gsb"])
                        for h in range(8):
                            S.op("dve", lambda e, h=h: e.max(out=mx[:, h, :], in_=gsb[:, h, :]), R=["gsb"], W=["mx"])
                        S.op("dve", lambda e, own=own: e.tensor_tensor(
                            out=biasq[:, :, 0:own], in0=gsb[:, :, 0:own], in1=mx[:, :, 2:3].to_broadcast([128, 8, own]),
                            op=ALU.is_ge), R=["gsb", "mx"], W=["biasq"])
                        S.op("dve", lambda e, own=own: e.tensor_scalar(
                            out=biasq[:, :, 0:own], in0=biasq[:, :, 0:own], scalar1=-1.0, scalar2=30000.0,
                            op0=ALU.add, op1=ALU.mult), R=["biasq"], W=["biasq"])
                S.op("dve", lambda e, own=own: e.memset(biasq[:, :, own:own + 1], 0.0), W=["biasq"])
                for g in range(2):
                    for hh in range(4):
                        h = g * 4 + hh
                        S.op("pe", lambda e, h=h, hh=hh: e.transpose(
                            out=P[7][0:32, hh * 128:(hh + 1) * 128], in_=biasq[:, h, :], identity=ident),
                            R=["biasq", "cs"], W=["P7"])
                    S.op("dve", lambda e, g=g, m=m: e.tensor_copy(
                        out=biasT[0:32, g * 4:(g + 1) * 4, m * 128:(m + 1) * 128],
                        in_=P[7][0:32, 0:512].rearrange("p (a b) -> p a b", a=4)), R=["P7"], W=["biasT"])
            for h in range(8):
                s2 = h % 2
                KTh = wAf[:, s2 * 8192:s2 * 8192 + NC * 128]
                Vh = wBf[:, s2 * 8192:s2 * 8192 + NC * 128].rearrange("p (c d) -> p c d", d=128)
                first = False
                S.dma("sp", KTh, KT_hist[h, :, 0:NC * 128], R=["kthist"], W=(KWA[s2] if h < 2 else []) + [("KTh", s2)])
                S.dma("sp", Vh, V_hist[0:NC * 128, h * 128:(h + 1) * 128].rearrange("(c p) d -> p c d", p=128),
                      R=[("vhist", mm) for mm in range(nsub)], W=(KWB if h < 2 else []) + [("Vh", s2)])
                RKT = KWA[s2] + [("KTh", s2)]
                RV = KWB + [("Vh", s2)]
                for c in range(NC):
                    n = c // 2
                    b2 = c % 2
                    pb, kb = P[c % 4], "P%d" % (c % 4)
                    own_tile = c >= NC - 4
                    S.op("pe", lambda e, c=c, pb=pb, h=h, KTh=KTh: e.matmul(
                        pb[:, 0:T], lhsT=KTh[:, c * 128:(c + 1) * 128], rhs=QT[:, h, 0:T], start=True, stop=False),
                        R=RKT + ["QT"], W=[kb])
                    S.op("pe", lambda e, n=n, pb=pb, h=h, own_tile=own_tile: e.matmul(
                        pb[:, 0:T], lhsT=SEL[0:32, n * 128:(n + 1) * 128], rhs=biasT[0:32, h, 0:T],
                        start=False, stop=(not own_tile)), R=["SEL", "biasT"], W=[kb])
                    if own_tile:
                        cc = c - (NC - 4)
                        S.op("pe", lambda e, cc=cc, pb=pb: e.matmul(
                            pb[:, 0:T], lhsT=ident_b[:, :], rhs=CMt[:, cc, 0:T], start=False, stop=True),
                            R=["ident_b", "CMt"], W=[kb])
                    S.op("act", lambda e, pb=pb, b2=b2: e.activation(out=PTf[:, b2, 0:T], in_=pb[:, 0:T], func=AF.Exp),
                         R=[kb], W=[("PTf", b2)])
                    S.op("dve", lambda e, b2=b2: e.tensor_copy(out=PT[:, b2, 0:T], in_=PTf[:, b2, 0:T]),
                         R=[("PTf", b2)], W=[("PT", b2)])
                    S.op("pe", lambda e, c=c, b2=b2, Vh=Vh: e.matmul(
                        P[4][:, 0:T], lhsT=Vh[:, c, :], rhs=PT[:, b2, 0:T], start=(c == 0), stop=(c == NC - 1)),
                        R=RV + [("PT", b2)], W=["P4"])
                    S.op("pe", lambda e, c=c, b2=b2: e.matmul(
                        P[5][:, 0:T], lhsT=ones_b1[:, :], rhs=PT[:, b2, 0:T], start=(c == 0), stop=(c == NC - 1)),
                        R=[("PT", b2), "ones_b1"], W=["P5"])
                S.op("dve", lambda e: e.reciprocal(out=rd[:, 0:T], in_=P[5][:, 0:T]), R=["P5"], W=["rd"])
                S.op("dve", lambda e, h=h: e.tensor_tensor(out=OT[:, h, 0:T], in0=P[4][:, 0:T], in1=rd[:, 0:T], op=ALU.mult),
                     R=["P4", "rd"], W=[("OT", h)])
        out_proj(moba_w_out, [OT[:, k, 0:T] for k in range(8)], [("OT", k) for k in range(8)], li, T)

    def ssd(li, T, r0, sample, last, pre=None):
        nsub = T // 128
        HK = [("hb", k) for k in range(8)]
        Um, SLm, NGm, ATm = (UB, SLB, cs[:, 512:640], SAME) if sample else (cs[:, 128:256], SLc, NEGM, onesf)
        if sample:
            pass
        for grp in range(4):
            slot = grp % 2
            if pre is not None and grp < 2:
                rk = pre[grp]
            else:
                rk = load_w(wA[:, slot], ssm_w_in[:, 2048 + grp * 1024:2048 + (grp + 1) * 1024], KWA[slot], "sx%d" % slot, WKEY("ssm_w_in"))
            if sample:
                S.dma("sp", xin[0:48, 0, :], state_conv[:, grp * 1024:(grp + 1) * 1024], W=["xin0"])
            for cc in range(8):
                ch = grp * 8 + cc
                b = ch % 2
                pb, kb = P[b], "P%d" % b
                for k in range(8):
                    S.op("pe", lambda e, k=k, cc=cc, pb=pb, slot=slot: e.matmul(
                        pb[:, 0:T], lhsT=wA[:, slot, k, cc * 128:(cc + 1) * 128], rhs=hb[:, k, 0:T],
                        start=(k == 0), stop=(k == 7)), R=rk + [("hb", k)], W=[kb])
                xk = ("PTf", b)
                if sample:
                    xp3 = xpad[:, b, 0:176].rearrange("p (s t) -> p s t", t=11)
                    S.op("pe", lambda e, cc=cc: e.transpose(out=P[2][:, 0:48], in_=xin[0:48, 0, cc * 128:(cc + 1) * 128],
                                                            identity=ident[0:48, 0:48]), R=["xin0", "cs"], W=["P2"])
                    S.op("dve", lambda e, xp3=xp3: e.tensor_copy(out=xp3[:, :, 0:3], in_=P[2][:, 0:48].rearrange("p (s t) -> p s t", t=3)),
                         R=["P2"], W=[xk])
                    S.op("act", lambda e, xp3=xp3, pb=pb: e.activation(out=xp3[:, :, 3:11], in_=pb[:, 0:128].rearrange("p (s t) -> p s t", t=8),
                                                                       func=AF.Identity), R=[kb], W=[xk])
                    views = [xp3[:, :, k:k + 8] for k in range(4)]
                    acc = rd[:, 0:128].rearrange("p (s t) -> p s t", t=8)
                else:
                    S.op("dve", lambda e, b=b, ch=ch: e.tensor_copy(out=xpad[:, b, 0:3], in_=halo[:, ch, :]), R=["halo"], W=[xk])
                    S.op("act", lambda e, b=b, pb=pb: e.activation(out=xpad[:, b, 3:3 + T], in_=pb[:, 0:T], func=AF.Identity),
                         R=[kb], W=[xk])
                    S.op("dve", lambda e, b=b, ch=ch: e.tensor_copy(out=halo[:, ch, :], in_=xpad[:, b, T:T + 3]), R=[xk], W=["halo"])
                    views = [xpad[:, b, k:k + T] for k in range(4)]
                    acc = rd[:, 0:T]
                S.op("dve", lambda e, acc=acc, v0=views[0], ch=ch: e.tensor_scalar(
                    out=acc, in0=v0, scalar1=wcv[:, ch:ch + 1], scalar2=None, op0=ALU.mult), R=[xk, "wcv"], W=["rd"])
                for k in range(1, 4):
                    S.op("dve", lambda e, acc=acc, vk=views[k], ch=ch, k=k: e.scalar_tensor_tensor(
                        out=acc, in0=vk, scalar=wcv[:, k * 32 + ch:k * 32 + ch + 1], in1=acc, op0=ALU.mult, op1=ALU.add),
                        R=[xk, "wcv", "rd"], W=["rd"])
                S.op("act", lambda e, ch=ch: e.activation(out=rd[:, 0:T], in_=rd[:, 0:T], func=AF.Silu, bias=bcv[:, ch:ch + 1]),
                     R=["rd", "bcv"], W=["rd"])
                if ch >= 16:
                    S.op("dve", lambda e, ch=ch: e.tensor_copy(out=BCt[:, ch - 16, 0:T], in_=rd[:, 0:T]), R=["rd"], W=[("BCt", ch - 16)])
                if ch < 24:
                    dst = xtok if ch < 16 else Btok
                    co = ch if ch < 16 else ch - 16
                    pb2, kb2 = P[2 + ch % 2], "P%d" % (2 + ch % 2)
                    for sub in range(nsub):
                        S.op("pe", lambda e, sub=sub, pb2=pb2: e.transpose(out=pb2[:, sub * 128:(sub + 1) * 128],
                                                                          in_=rd[:, sub * 128:(sub + 1) * 128], identity=ident),
                             R=["rd", "cs"], W=[kb2])
                    S.op("dve", lambda e, dst=dst, co=co, pb2=pb2: e.tensor_copy(
                        out=dst[:, 0:nsub, co * 128:(co + 1) * 128], in_=pb2[:, 0:T].rearrange("p (a b) -> p a b", b=128)),
                        R=[kb2], W=KWB + ["xtok"])
        if sample or last:
            sub = nsub - 1
            for grp in range(4):
                slot = grp % 2
                rk = load_w(wA[:, slot], ssm_w_in[:, 2048 + grp * 1024:2048 + (grp + 1) * 1024], KWA[slot], "sx%d" % slot, WKEY("ssm_w_in"))
                for n in range(2):
                    pb, kb = P[n], "P%d" % n
                    for k in range(8):
                        S.op("pe", lambda e, k=k, n=n, pb=pb, slot=slot, sub=sub: e.matmul(
                            pb[:, 0:512], lhsT=hb[:, k, sub * 128:(sub + 1) * 128], rhs=wA[:, slot, k, n * 512:(n + 1) * 512],
                            start=(k == 0), stop=(k == 7)), R=rk + [("hb", k)], W=[kb])
                    S.op("act", lambda e, n=n, pb=pb: e.activation(out=xin[:, 1, n * 512:(n + 1) * 512], in_=pb[:, 0:512], func=AF.Identity),
                         R=[kb], W=["xin1"])
                if sample:
                    for sq in range(NSEQ_S):
                        S.dma("sp", conv_s[sq * 3:(sq + 1) * 3, grp * 1024:(grp + 1) * 1024], xin[sq * 8 + 5:sq * 8 + 8, 1, :],
                              R=["xin1"], W=["out"])
                else:
                    S.dma("sp", conv_p[:, grp * 1024:(grp + 1) * 1024], xin[125:128, 1, :], R=["xin1"], W=["out"])
        rkz = [load_w(wA[:, q], ssm_w_in[:, q * 1024:(q + 1) * 1024], KWA[q], "sz%d" % q, WKEY("ssm_w_in")) for q in range(2)]
        S.dma("pool", wcvdt[:, :, :], ssm_w_in[:, 6144:6176].rearrange("(k p) n -> p k n", p=128), R=WKEY("ssm_w_in"), W=["wdt"])
        for sub in range(nsub):
            for q in range(2):
                for n in range(2):
                    pb, kb = P[n], "P%d" % n
                    for k in range(8):
                        S.op("pe", lambda e, k=k, n=n, q=q, pb=pb, sub=sub: e.matmul(
                            pb[:, 0:512], lhsT=hb[:, k, sub * 128:(sub + 1) * 128], rhs=wA[:, q, k, n * 512:(n + 1) * 512],
                            start=(k == 0), stop=(k == 7)), R=rkz[q] + [("hb", k)], W=[kb])
                    S.op("act", lambda e, n=n, pb=pb: e.activation(out=tmp[:, n, :], in_=pb[:, 0:512], func=AF.Silu),
                         R=[kb], W=[("tmp", n)])
                    S.op("dve", lambda e, n=n, q=q, sub=sub: e.tensor_copy(
                        out=zs[:, sub, q * 1024 + n * 512:q * 1024 + (n + 1) * 512], in_=tmp[:, n, :]),
                        R=[("tmp", n)], W=KWB + ["zs"])
            for k in range(8):
                S.op("pe", lambda e, k=k, sub=sub: e.matmul(P[7][:, 0:32], lhsT=hb[:, k, sub * 128:(sub + 1) * 128], rhs=wcvdt[:, k, :],
                                                            start=(k == 0), stop=(k == 7)), R=["wdt", ("hb", k)], W=["P7"])
            S.op("dve", lambda e: e.tensor_tensor(out=sm[:, 5, :], in0=P[7][:, 0:32], in1=prm[:, 0, :], op=ALU.add), R=["P7", "prm"], W=["sm5"])
            S.op("dve", lambda e: e.tensor_scalar(out=sm[:, 6, :], in0=sm[:, 5, :], scalar1=-1.0, scalar2=None, op0=ALU.mult),
                 R=["sm5"], W=["sm6"])
            S.op("dve", lambda e: e.tensor_tensor(out=sm[:, 6, :], in0=sm[:, 6, :], in1=sm[:, 5, :], op=ALU.max),
                 R=["sm5", "sm6"], W=["sm6"])
            S.op("act", lambda e: e.activation(out=sm[:, 6, :], in_=sm[:, 6, :], func=AF.Exp, scale=-1.0), R=["sm6"], W=["sm6"])
            S.op("act", lambda e: e.activation(out=sm[:, 6, :], in_=sm[:, 6, :], func=AF.Ln, bias=cs[:, 641:642]), R=["sm6", "cs"], W=["sm6"])
            S.op("dve", lambda e: e.tensor_scalar(out=sm[:, 5, :], in0=sm[:, 5, :], scalar1=0.0, scalar2=None, op0=ALU.max),
                 R=["sm5"], W=["sm5"])
            S.op("dve", lambda e, sub=sub: e.tensor_tensor(out=dts[:, sub, :], in0=sm[:, 5, :], in1=sm[:, 6, :], op=ALU.add),
                 R=["sm5", "sm6"], W=["dts"])
        ngv = statf[:, 0:2048]
        S.dma("sp", ngv, bass.AP(tensor=ssm_norm_g.tensor, offset=0, ap=[[0, 128], [1, 2048]]), W=["mean", "msq", "rstd", "cG", "cB"])
        HBK = HK
        snat = wAf32[:, 0:2048].rearrange("p (c n) -> p c n", c=16)
        sT = wAf32[:, 2048:4096]
        sTb = wAf[:, 8192:10240]
        CmT = wAf[:, 10240:11264].rearrange("p (g t) -> p g t", g=8)
        xdm = wAf[:, 11264:13312]
        Esq = wAf32[:, 6656:6784]
        etq = wAf32[:, 6784:6816]
        WAK = KWA[0] + KWA[1]

        def ssd_sample_states(csl):
            S.op("dve", lambda e: e.memset(zb[:, 0, :], 0.0), W=[("zb", 0)])
            for sq in range(NSEQ_S):
                S.dma("sp", snat, state_ssm[sq].rearrange("(c p) n -> p c n", p=128), W=(WAK if sq == 0 else []) + ["snat"])
                for c4 in range(4):
                    pb, kb = P[4 + c4], "P%d" % (4 + c4)
                    for cc in range(4):
                        c = c4 * 4 + cc
                        S.op("pe", lambda e, c=c, cc=cc, pb=pb: e.transpose(out=pb[:, cc * 128:(cc + 1) * 128], in_=snat[:, c, :], identity=ident),
                             R=WAK + ["snat", "cs"], W=[kb])
                    S.op("dve", lambda e, c4=c4, pb=pb: e.tensor_copy(out=sT[:, c4 * 512:(c4 + 1) * 512], in_=pb[:, 0:512]), R=[kb], W=["sT"])
                    S.op("act", lambda e, c4=c4: e.activation(out=sTb[:, c4 * 512:(c4 + 1) * 512], in_=sT[:, c4 * 512:(c4 + 1) * 512],
                                                              func=AF.Identity), R=["sT"], W=["sTb"])
                S.op("dve", lambda e: e.memset(CmT, 0.0), W=["CmT"])
                S.op("dve", lambda e, sq=sq: e.tensor_copy(out=CmT[:, :, sq * 8:(sq + 1) * 8], in_=BCt[:, 8:16, sq * 8:(sq + 1) * 8]),
                     R=[("BCt", 8 + g) for g in range(8)], W=["CmT"])
                for g in range(8):
                    if sq == 0 and g % 2 == 0:
                        S.op("pe", lambda e, g=g: e.matmul(P[g // 2][:, 0:512], lhsT=ones_b1[:, :], rhs=zb[:, 0, :], start=True, stop=False),
                             R=[("zb", 0), "ones_b1"], W=["P%d" % (g // 2)])
                    S.op("pe", lambda e, g=g, sq=sq: e.matmul(P[g // 2][:, (g % 2) * 256:(g % 2 + 1) * 256], lhsT=CmT[:, g, :],
                                                              rhs=sTb[:, g * 256:(g + 1) * 256], start=False,
                                                              stop=(sq == NSEQ_S - 1 and g % 2 == 1)), R=["CmT", "sTb"], W=["P%d" % (g // 2)])
                S.op("dve", lambda e, sq=sq: e.tensor_scalar(out=Esq, in0=onesf, scalar1=lastm[:, sq:sq + 1], scalar2=None, op0=ALU.mult),
                     R=["cs", "cs2"], W=["Esq"])
                S.op("dve", lambda e, sq=sq: e.tensor_scalar(out=xdm, in0=xdtd, scalar1=seqm[:, sq:sq + 1], scalar2=None, op0=ALU.mult),
                     R=["xdtd", "cs2"], W=["xdm"])
                S.op("pe", lambda e: e.matmul(P[7][:, 0:32], lhsT=Esq, rhs=sm[:, 1, :], start=True, stop=True), R=["Esq", "acs"], W=["P7"])
                S.op("act", lambda e: e.activation(out=etq, in_=P[7][:, 0:32], func=AF.Exp), R=["P7"], W=["etq"])
                for g in range(8):
                    S.op("pe", lambda e, g=g: e.matmul(P[4 + g // 2][:, (g % 2) * 256:(g % 2 + 1) * 256], lhsT=Btok[:, 0, g * 128:(g + 1) * 128],
                                                       rhs=xdm[:, g * 256:(g + 1) * 256], start=True, stop=True),
                         R=KWB + ["xtok", "xdm"], W=["P%d" % (4 + g // 2)])
                for q in range(4):
                    sv = sT[:, q * 512:(q + 1) * 512].rearrange("p (h d) -> p h d", h=8)
                    S.op("dve", lambda e, sv=sv, q=q: e.tensor_tensor(out=sv, in0=sv, in1=etq[:, q * 8:(q + 1) * 8].unsqueeze(2).to_broadcast([128, 8, 64]),
                                                                    op=ALU.mult), R=["etq", "sT"], W=["sT"])
                    S.op("dve", lambda e, q=q: e.tensor_tensor(out=sT[:, q * 512:(q + 1) * 512], in0=sT[:, q * 512:(q + 1) * 512],
                                                               in1=P[4 + q][:, 0:512], op=ALU.add), R=["P%d" % (4 + q), "sT"], W=["sT"])
                for c4 in range(4):
                    pb, kb = P[4 + c4], "P%d" % (4 + c4)
                    for cc in range(4):
                        c = c4 * 4 + cc
                        S.op("pe", lambda e, c=c, cc=cc, pb=pb: e.transpose(out=pb[:, cc * 128:(cc + 1) * 128], in_=sT[:, c * 128:(c + 1) * 128],
                                                                           identity=ident), R=["sT", "cs"], W=[kb])
                    S.op("dve", lambda e, c4=c4, pb=pb: e.tensor_copy(out=snat[:, c4 * 4:(c4 + 1) * 4, :],
                                                                     in_=pb[:, 0:512].rearrange("p (a b) -> p a b", a=4)), R=[kb], W=["snat"])
                S.dma("sp", ssm_s[sq].rearrange("(c p) n -> p c n", p=128), snat, R=["snat"], W=["out"])
            S.op("dve", lambda e: e.memset(etq, 0.0), W=WAK + ["sT", "sTb", "CmT", "xdm", "Esq", "etq", "snat"])
            for q in range(4):
                S.op("dve", lambda e, q=q: e.tensor_tensor(
                    out=rd[:, 0:512].rearrange("p (h d) -> p h d", h=8), in0=P[q][:, 0:512].rearrange("p (h d) -> p h d", h=8),
                    in1=sm[:, 2, q * 8:(q + 1) * 8].unsqueeze(2).to_broadcast([128, 8, 64]), op=ALU.mult),
                    R=["P%d" % q, "eacs"], W=["rd"])
                S.op("dve", lambda e, q=q: e.tensor_tensor(out=yv[:, q * 512:(q + 1) * 512], in0=yv[:, q * 512:(q + 1) * 512], in1=rd[:, 0:512],
                                                           op=ALU.add), R=["rd", "xin0", "xin1"], W=["xin0", "xin1"])
        for sub in range(nsub):
            csl = slice(sub * 128, (sub + 1) * 128)
            S.op("dve", lambda e, sub=sub: e.tensor_tensor(out=sm[:, 0, :], in0=dts[:, sub, :], in1=prm[:, 1, :], op=ALU.mult),
                 R=["dts", "prm"], W=["adt"])
            S.op("pe", lambda e: e.matmul(P[7][:, 0:32], lhsT=Um, rhs=sm[:, 0, :], start=True, stop=True), R=["adt", "cs", "cs2"], W=["P7"])
            S.op("pe", lambda e: e.matmul(P[7][:, 32:64], lhsT=ATm, rhs=sm[:, 0, :], start=True, stop=True), R=["adt", "cs", "cs2"], W=["P7"])
            S.op("dve", lambda e: e.tensor_copy(out=sm[:, 1, :], in_=P[7][:, 0:32]), R=["P7"], W=["acs"])
            S.op("act", lambda e: e.activation(out=sm[:, 2, :], in_=P[7][:, 0:32], func=AF.Exp), R=["P7"], W=["eacs"])
            S.op("dve", lambda e: e.tensor_tensor(out=sm[:, 3, :], in0=P[7][:, 32:64], in1=sm[:, 1, :], op=ALU.subtract),
                 R=["P7", "acs"], W=["dec"])
            S.op("act", lambda e: e.activation(out=sm[:, 3, :], in_=sm[:, 3, :], func=AF.Exp), R=["dec"], W=["dec"])
            S.op("act", lambda e: e.activation(out=sm[:, 4, :], in_=P[7][:, 32:64], func=AF.Exp), R=["P7"], W=["etot"])
            xv = xtok[:, sub, :].rearrange("p (h d) -> p h d", h=32)
            S.op("dve", lambda e, xv=xv, sub=sub: e.tensor_tensor(
                out=xdt.rearrange("p (h d) -> p h d", h=32), in0=xv, in1=dts[:, sub, :].unsqueeze(2).to_broadcast([128, 32, 64]),
                op=ALU.mult), R=KWB + ["xtok", "dts"], W=HBK + ["xdt"])
            S.op("dve", lambda e: e.tensor_tensor(
                out=xdtd.rearrange("p (h d) -> p h d", h=32), in0=xdt.rearrange("p (h d) -> p h d", h=32),
                in1=sm[:, 3, :].unsqueeze(2).to_broadcast([128, 32, 64]), op=ALU.mult), R=["xdt", "dec"], W=HBK + ["xdtd"])
            for g in range(8):
                S.op("pe", lambda e, g=g, csl=csl: e.matmul(P[g // 4][:, (g % 4) * 128:(g % 4 + 1) * 128], lhsT=BCt[:, g, csl],
                                                            rhs=BCt[:, 8 + g, csl], start=True, stop=True),
                     R=[("BCt", g), ("BCt", 8 + g)], W=["P%d" % (g // 4)])
            for q in range(2):
                S.op("dve", lambda e, q=q: e.tensor_copy(out=CBt[:, q * 4:(q + 1) * 4, :], in_=P[q][:, 0:512].rearrange("p (a b) -> p a b", a=4)),
                     R=["P%d" % q], W=["CBt", ("PTf", 0), ("PTf", 1)])
            for hf in range(2):
                for hh in range(16):
                    h = hf * 16 + hh
                    b = h % 4
                    trh = tmp[:, b // 2, (b % 2) * 256:(b % 2) * 256 + 128]
                    tL = tmp[:, b // 2, (b % 2) * 256 + 128:(b % 2) * 256 + 256]
                    mt = MTf[:, b * 128:(b + 1) * 128]
                    S.op("dve", lambda e, h=h, trh=trh: e.tensor_scalar(out=trh, in0=SLm, scalar1=sm[:, 0, h:h + 1], scalar2=None,
                                                                        op0=ALU.mult), R=["adt", "cs2"], W=[("ssdt", b)])
                    S.op("pe", lambda e, b=b, trh=trh: e.matmul(P[4 + b][:, 0:128], lhsT=trh, rhs=Um, start=True, stop=False),
                         R=[("ssdt", b), "cs", "cs2"], W=["P%d" % (4 + b)])
                    S.op("pe", lambda e, b=b: e.matmul(P[4 + b][:, 0:128], lhsT=ident, rhs=NGm, start=False, stop=True),
                         R=["cs", "cs2"], W=["P%d" % (4 + b)])
                    S.op("act", lambda e, b=b, tL=tL: e.activation(out=tL, in_=P[4 + b][:, 0:128], func=AF.Exp),
                         R=["P%d" % (4 + b)], W=[("ssdL", b)])
                    S.op("dve", lambda e, b=b, h=h, tL=tL, mt=mt: e.tensor_tensor(out=mt, in0=tL, in1=CBt[:, h // 4, :], op=ALU.mult),
                         R=[("ssdL", b), "CBt"], W=[("MT", b)])
                    S.op("pe", lambda e, b=b, h=h, hh=hh, mt=mt: e.matmul(P[hh // 8][:, (hh % 8) * 64:(hh % 8 + 1) * 64], lhsT=mt,
                                                                        rhs=xdt[:, h * 64:(h + 1) * 64], start=True, stop=True),
                         R=[("MT", b), "xdt"], W=["P%d" % (hh // 8)])
                for q in range(2):
                    S.op("dve", lambda e, q=q, hf=hf: e.tensor_copy(out=yv[:, hf * 1024 + q * 512:hf * 1024 + (q + 1) * 512], in_=P[q][:, 0:512]),
                         R=["P%d" % q], W=["xin%d" % hf])
                if not sample:
                    for gi in range(4):
                        g = hf * 4 + gi
                        S.op("pe", lambda e, g=g, gi=gi, csl=csl: e.matmul(P[2 + gi // 2][:, (gi % 2) * 256:(gi % 2 + 1) * 256], lhsT=BCt[:, 8 + g, csl],
                                                                         rhs=stateTb[:, g * 256:(g + 1) * 256], start=True, stop=True),
                             R=[("BCt", 8 + g), "stateTb"], W=["P%d" % (2 + gi // 2)])
                    for q in range(2):
                        c0 = hf * 1024 + q * 512
                        S.op("dve", lambda e, q=q, hf=hf, c0=c0: e.tensor_tensor(
                            out=rd[:, 0:512].rearrange("p (h d) -> p h d", h=8), in0=P[2 + q][:, 0:512].rearrange("p (h d) -> p h d", h=8),
                            in1=sm[:, 2, hf * 16 + q * 8:hf * 16 + (q + 1) * 8].unsqueeze(2).to_broadcast([128, 8, 64]), op=ALU.mult),
                            R=["P%d" % (2 + q), "eacs"], W=["rd"])
                        S.op("dve", lambda e, c0=c0: e.tensor_tensor(out=yv[:, c0:c0 + 512], in0=yv[:, c0:c0 + 512], in1=rd[:, 0:512], op=ALU.add),
                             R=["rd", "xin%d" % hf], W=["xin%d" % hf])
                    for gi in range(4):
                        g = hf * 4 + gi
                        S.op("pe", lambda e, g=g, gi=gi, sub=sub: e.matmul(P[6 + gi // 2][:, (gi % 2) * 256:(gi % 2 + 1) * 256],
                                                                         lhsT=Btok[:, sub, g * 128:(g + 1) * 128], rhs=xdtd[:, g * 256:(g + 1) * 256],
                                                                         start=True, stop=True), R=KWB + ["xtok", "xdtd"], W=["P%d" % (6 + gi // 2)])
                    for q in range(2):
                        c0 = hf * 1024 + q * 512
                        sv = stateT[:, c0:c0 + 512].rearrange("p (h d) -> p h d", h=8)
                        S.op("dve", lambda e, sv=sv, hf=hf, q=q: e.tensor_tensor(
                            out=sv, in0=sv, in1=sm[:, 4, hf * 16 + q * 8:hf * 16 + (q + 1) * 8].unsqueeze(2).to_broadcast([128, 8, 64]),
                            op=ALU.mult), R=["etot", "stateT"], W=["stateT"])
                        S.op("dve", lambda e, c0=c0, q=q: e.tensor_tensor(out=stateT[:, c0:c0 + 512], in0=stateT[:, c0:c0 + 512],
                                                                        in1=P[6 + q][:, 0:512], op=ALU.add), R=["P%d" % (6 + q), "stateT"], W=["stateT"])
                        S.op("act", lambda e, c0=c0: e.activation(out=stateTb[:, c0:c0 + 512], in_=stateT[:, c0:c0 + 512], func=AF.Identity),
                             R=["stateT"], W=["stateTb"])
            if sample:
                ssd_sample_states(csl)
            xv = xtok[:, sub, :].rearrange("p (h d) -> p h d", h=32)
            S.op("dve", lambda e, xv=xv: e.tensor_tensor(out=ysq.rearrange("p (h d) -> p h d", h=32), in0=xv,
                                                         in1=prm[:, 2, :].unsqueeze(2).to_broadcast([128, 32, 64]), op=ALU.mult),
                 R=KWB + ["xtok", "prm"], W=HBK + ["xdt", "xdtd", "ysq"])
            S.op("dve", lambda e: e.tensor_tensor(out=yv, in0=yv, in1=ysq, op=ALU.add), R=["ysq", "xin0", "xin1"], W=["xin0", "xin1"])
            S.op("dve", lambda e, sub=sub: e.tensor_tensor(out=yv, in0=yv, in1=zs[:, sub, :], op=ALU.mult), R=KWB + ["zs", "xin0", "xin1"],
                 W=["xin0", "xin1"])
            S.op("dve", lambda e: e.tensor_tensor(out=ysq, in0=yv, in1=yv, op=ALU.mult), R=["xin0", "xin1"], W=["ysq"])
            S.op("dve", lambda e: e.tensor_reduce(out=sm[:, 7, 0:8], in_=ysq.rearrange("p (g d) -> p g d", g=8), axis=AX.X, op=ALU.add),
                 R=["ysq"], W=["sm7"])
            S.op("act", lambda e: e.activation(out=sm[:, 7, 0:8], in_=sm[:, 7, 0:8], func=AF.Ln, bias=epsb[:, 0:1], scale=1.0 / 256),
                 R=["sm7", "epsb"], W=["sm7"])
            S.op("act", lambda e: e.activation(out=sm[:, 7, 0:8], in_=sm[:, 7, 0:8], func=AF.Exp, scale=-0.5), R=["sm7"], W=["sm7"])
            S.op("dve", lambda e: e.tensor_tensor(out=yv.rearrange("p (g d) -> p g d", g=8), in0=yv.rearrange("p (g d) -> p g d", g=8),
                                                  in1=sm[:, 7, 0:8].unsqueeze(2).to_broadcast([128, 8, 256]), op=ALU.mult),
                 R=["sm7", "xin0", "xin1"], W=["xin0", "xin1"])
            S.op("dve", lambda e: e.tensor_tensor(out=yv, in0=yv, in1=ngv, op=ALU.mult), R=["mean", "xin0", "xin1"], W=["xin0", "xin1"])
            for c4 in range(4):
                pb, kb = P[c4 % 2], "P%d" % (c4 % 2)
                for cc in range(4):
                    c = c4 * 4 + cc
                    S.op("pe", lambda e, c=c, cc=cc, pb=pb: e.transpose(out=pb[:, cc * 128:(cc + 1) * 128], in_=yv[:, c * 128:(c + 1) * 128],
                                                                       identity=ident), R=["xin0", "xin1", "cs"], W=[kb])
                S.op("dve", lambda e, c4=c4, pb=pb, sub=sub: e.tensor_copy(
                    out=ynT[:, c4 * 4:(c4 + 1) * 4, sub * 128:(sub + 1) * 128], in_=pb[:, 0:512].rearrange("p (a b) -> p a b", a=4)),
                    R=[kb], W=[("aT", j) for j in range(16)])
        if last and not sample:
            for c4 in range(4):
                pb, kb = P[c4 % 2], "P%d" % (c4 % 2)
                for cc in range(4):
                    c = c4 * 4 + cc
                    S.op("pe", lambda e, c=c, cc=cc, pb=pb: e.transpose(out=pb[:, cc * 128:(cc + 1) * 128], in_=stateT[:, c * 128:(c + 1) * 128],
                                                                       identity=ident), R=["stateT", "cs"], W=[kb])
                S.op("dve", lambda e, pb=pb: e.tensor_copy(out=yv[:, 0:512], in_=pb[:, 0:512]), R=[kb], W=["xin0"])
                S.dma("sp", ssm_p[c4 * 512:(c4 + 1) * 512, :].rearrange("(a p) n -> p a n", p=128),
                      yv[:, 0:512].rearrange("p (a n) -> p a n", a=4), R=["xin0"], W=["out"])
        rko = [load_w(wA[:, q], ssm_w_out[q * 1024:(q + 1) * 1024, :], KWA[q], "so%d" % q, WKEY("ssm_w_out")) for q in range(2)]
        for c in range(8):
            po, ko = P[6 + c % 2], "P%d" % (6 + c % 2)
            for k in range(16):
                S.op("pe", lambda e, k=k, c=c, po=po: e.matmul(
                    po[:, 0:T], lhsT=wA[:, k // 8, k % 8, c * 128:(c + 1) * 128], rhs=ynT[:, k, 0:T],
                    start=(k == 0), stop=(k == 15)), R=rko[k // 8] + [("aT", k)], W=[ko])
            S.op("dve", lambda e, c=c, po=po: e.scalar_tensor_tensor(
                out=hT[:, c, 0:T], in0=hT[:, c, 0:T], scalar=ALPHA, in1=po[:, 0:T],
                op0=ALU.mult, op1=ALU.add), R=[ko, ("hT", c)], W=[("hT", c)])
        layer_norm(li, T)

    def run_tile(src, dst, r0, T):
        load_tile(src, r0, T)
        only = os.environ.get("MK_ONLY", "")
        for L in ([int(only)] if only else range(LAYERS)):
            pre = None
            if os.environ.get("MK_FFN", "1") != "0":
                pf = None
                if MIX and os.environ.get("MK_PREF", "1") == "1":
                    pf = (cmlp_prefetch(L // 3), moba_prefetch, ssd_prefetch)[L % 3]
                pre = ffn(L * 2 + 0, L * 3 + 0, T, prefetch=pf)
            if MIX and L % 3 == 0:
                cmlp(L // 3, L * 3 + 1, T, sample=(T == 128), pre=pre)
            if MIX and L % 3 == 1:
                moba(L * 3 + 1, T, r0, sample=(T == 128), pre=pre)
            if MIX and L % 3 == 2:
                ssd(L * 3 + 1, T, r0, sample=(T == 128), last=(r0 + T == NT * 512), pre=pre)
            if os.environ.get("MK_FFN", "1") != "0":
                ffn(L * 2 + 1, L * 3 + 2, T)
        store_tile(dst, r0, T)

    for it in range(NT):
        run_tile(xp, y_p, it * 512, 512)
    if SAMPLE:
        run_tile(xs, y_s, 0, 128)

    S.finish(["out"])
    with nc.Block() as block:
        S.replay(block)
    st.close()
    print("instructions (incl waits):", S.n_ins)
    return nc


def make_consts():
    c = np.zeros((128, 1024), np.float32)
    c[:, 0:128] = np.eye(128, dtype=np.float32)
    i = np.arange(128)
    c[:, 128:256] = (i[:, None] <= i[None, :]).astype(np.float32)
    c[:, 256:384] = (i[None, :] <= i[:, None]).astype(np.float32)
    c[:, 384:512] = ((i[:, None] // 8 == i[None, :] // 8) & (i[None, :] % 8 <= i[:, None] % 8)).astype(np.float32)
    c[:, 512:640] = np.where((i[:, None] // 8 == i[None, :] // 8) & (i[:, None] % 8 <= i[None, :] % 8), 0.0, -30000.0)
    c[:, 640] = i.astype(np.float32)
    c[:, 641:769] = 1.0
    return c


def make_cstb():
    c = np.zeros((128, 6144), np.float32)
    for n in range(32):
        c[n, n * 128:(n + 1) * 128] = 1.0
    key = np.arange(128)[:, None]
    q = np.arange(512)[None, :]
    for cc in range(4):
        kp = cc * 128 + key
        same = (kp // 256) == (q // 256)
        c[:, 4096 + cc * 512:4096 + (cc + 1) * 512] = np.where(same & (kp > q), -30000.0, 0.0)
    return c


def make_consts2():
    c = np.zeros((128, 768), np.float32)
    i = np.arange(128)
    same = (i[:, None] // 8) == (i[None, :] // 8)
    c[:, 0:128] = (i[:, None] > i[None, :])
    c[:, 128:256] = np.where(i[:, None] <= i[None, :], 0.0, -30000.0)
    c[:, 256:384] = same & (i[:, None] <= i[None, :])
    c[:, 384:512] = same & (i[:, None] > i[None, :])
    c[:, 512:640] = same
    c[:, 640:656] = (i[:, None] // 8) == np.arange(16)[None, :]
    c[:, 656:672] = i[:, None] == (np.arange(16)[None, :] * 8 + 7)
    return c


def make_rope():
    pos = np.concatenate([np.arange(SEQ), 2048 + (np.arange(128) % 8)]).astype(np.float32)
    inv = (10000.0 ** (-np.arange(64, dtype=np.float32) / 64)).astype(np.float32)
    ang = (pos[:, None] * inv[None, :]).astype(np.float32)
    return np.concatenate([np.cos(ang), np.sin(ang)], axis=1).astype(np.float32)


def kernel(**inp):
    NT = int(os.environ.get("MK_NT", "16"))
    LAYERS = int(os.environ.get("MK_LAYERS", "4"))
    nc = build(NT=NT, LAYERS=LAYERS, SAMPLE=os.environ.get("MK_SAMPLE", "1") == "1")
    cst = make_consts()
    cstb = make_cstb()
    ropec = make_rope()
    cst2 = make_consts2()
    sel8d = np.zeros((8, 1024), np.float32)
    for n in range(8):
        sel8d[n, n * 128:(n + 1) * 128] = 1.0
    ck = np.ascontiguousarray(np.asarray(inp["cache_k"])[0]).reshape(2560 * 128, D)
    cvv = np.ascontiguousarray(np.asarray(inp["cache_v"])[0]).reshape(2560 * 128, D)
    f = lambda a: np.ascontiguousarray(np.asarray(a))
    in_maps = []
    for c in range(8):
        m = {
            "xp": f(inp["x_prompt"][c % 2]),
            "xs": f(inp["x_sample"][c * 16:(c + 1) * 16].reshape(128, D)),
            "ln_g": f(inp["ln_g"].reshape(12, D)),
            "ln_b": f(inp["ln_b"].reshape(12, D)),
            "ffn_w_in": f(inp["ffn_w_in"].reshape(8, D, 2 * DFF)),
            "ffn_w_out": f(inp["ffn_w_out"].reshape(8, DFF, D)),
            "cst": cst,
        }
        for k in ("cmlp_w_in", "cmlp_ln_g", "cmlp_ln_b", "cmlp_w_s", "cmlp_b_s", "cmlp_w_out"):
            m[k] = f(inp[k])
        m["moba_w_qkv"] = f(inp["moba_w_qkv"][0])
        m["moba_w_out"] = f(inp["moba_w_out"][0])
        m["ropec"] = ropec
        m["cstb"] = cstb
        m["sel8d"] = sel8d
        m["cache_k"] = ck
        m["cache_v"] = cvv
        m["page_table"] = f(inp["page_table"][c * 16:(c + 1) * 16]).astype(np.int32)
        m["ssm_w_in"] = f(inp["ssm_w_in"][0])
        m["ssm_w_conv"] = f(inp["ssm_w_conv"][0])
        m["ssm_b_conv"] = f(inp["ssm_b_conv"][0]).reshape(1, 4096)
        m["ssm_dt_bias"] = f(inp["ssm_dt_bias"][0]).reshape(1, 32)
        m["ssm_a_log"] = f(inp["ssm_a_log"][0]).reshape(1, 32)
        m["ssm_d"] = f(inp["ssm_d"][0]).reshape(1, 32)
        m["ssm_norm_g"] = f(inp["ssm_norm_g"][0]).reshape(1, 2048)
        m["ssm_w_out"] = f(inp["ssm_w_out"][0])
        m["state_conv"] = f(inp["state_conv"][0, c * 16:(c + 1) * 16]).reshape(48, 4096)
        m["state_ssm"] = f(inp["state_ssm"][0, c * 16:(c + 1) * 16]).reshape(16, 2048, 128)
        m["cst2"] = cst2
        in_maps.append(m)
    res = run_bass_kernel_spmd(nc, in_maps, core_ids=list(range(8)))
    R = res.results
    y_prompt = np.stack([R[0]["y_p"], R[1]["y_p"]]).astype(np.float32)
    y_sample = np.concatenate([np.asarray(R[c]["y_s"]).reshape(16, 8, D) for c in range(8)], axis=0).astype(np.float32)
    cv = np.stack([np.concatenate([np.asarray(R[c]["cv_s"]).reshape(2, 16, 8, D)[j] for c in range(8)], axis=0)
                   for j in range(2)]).astype(np.float32)
    kp = np.stack([np.asarray(R[c]["k_p"]).reshape(SEQ, 8, 128) for c in range(2)])[None].astype(np.float32)
    vp = np.stack([np.asarray(R[c]["v_p"]).reshape(SEQ, 8, 128) for c in range(2)])[None].astype(np.float32)
    ks = np.concatenate([np.asarray(R[c]["k_s"]).reshape(16, 8, 8, 128) for c in range(8)], axis=0)[None].astype(np.float32)
    vs = np.concatenate([np.asarray(R[c]["v_s"]).reshape(16, 8, 8, 128) for c in range(8)], axis=0)[None].astype(np.float32)
    conv_p = np.stack([np.asarray(R[c]["conv_p"]).reshape(3, 4096) for c in range(2)])[None].astype(np.float32)
    ssm_p = np.stack([np.asarray(R[c]["ssm_p"]).reshape(32, 64, 128) for c in range(2)])[None].astype(np.float32)
    conv_s = np.concatenate([np.asarray(R[c]["conv_s"]).reshape(16, 3, 4096) for c in range(8)], axis=0)[None].astype(np.float32)
    ssm_s = np.concatenate([np.asarray(R[c]["ssm_s"]).reshape(16, 32, 64, 128) for c in range(8)], axis=0)[None].astype(np.float32)
    return (y_prompt, y_sample, cv, kp, vp, ks, vs, conv_p, ssm_p, conv_s, ssm_s)
```

```python
import os
import numpy as np
import concourse.bass as bass
import concourse.mybir as mybir
from concourse.bass_utils import run_bass_kernel_spmd

F32 = mybir.dt.float32
BF16 = mybir.dt.bfloat16
I32 = mybir.dt.int32
ALU = mybir.AluOpType
AF = mybir.ActivationFunctionType
AX = mybir.AxisListType

D = 1024
DFF = 2816
NFC = DFF // 128
DEPTH = 4
ALPHA = (2 * DEPTH) ** 0.25
LN_EPS = 1e-5
SEQ = 8192
NSEQ_S = 16
TS = 8


class Sched:
    def __init__(self, nc, stack):
        self.nc = nc
        self.eng = {}
        for name, e in (("pe", nc.tensor), ("act", nc.scalar), ("dve", nc.vector),
                        ("pool", nc.gpsimd), ("sp", nc.sync)):
            sem = stack.enter_context(nc.semaphore("sem_" + name))
            self.eng[name] = dict(name=name, e=e, sem=sem, count=0, waited={}, ops=[])
        self.NDS = 32
        self.dsem = [stack.enter_context(nc.semaphore("dsem%d" % i)) for i in range(self.NDS)]
        self.dval = [0] * self.NDS
        self.dnext2 = [0, 0]
        self.res = {}
        self.n_ins = 0

    def _deps(self, R, W):
        toks = []
        for k in R:
            r = self.res.get(k)
            if r is not None and r["w"] is not None:
                toks.append(r["w"])
        for k in W:
            r = self.res.get(k)
            if r is not None:
                if r["w"] is not None:
                    toks.append(r["w"])
                toks.extend(r["r"].values())
        return toks

    def _wait_list(self, E, toks):
        waits = []
        for (sem, val, owner) in toks:
            if owner == E["name"] and owner == "pe":
                continue
            if E["waited"].get(owner, 0) < val:
                E["waited"][owner] = val
                waits.append((sem, val))
        return waits

    def _mark(self, R, W, tok):
        for k in R:
            r = self.res.setdefault(k, dict(w=None, r={}))
            r["r"][tok[2]] = tok
        for k in W:
            self.res[k] = dict(w=tok, r={})

    def op(self, engname, fn, R=(), W=()):
        E = self.eng[engname]
        if engname in ("act", "dve"):
            W = list(W) + [k for k in R if isinstance(k, str) and len(k) == 2 and k[0] == "P" and k[1].isdigit()]
        waits = self._wait_list(E, self._deps(R, W))
        E["count"] += 1
        tok = (E["sem"], E["count"], engname)
        sem = E["sem"]

        def run(e, waits=waits, fn=fn, sem=sem):
            for (s, v) in waits:
                e.wait_ge(s, v)
            fn(e).then_inc(sem, 1)
        E["ops"].append(run)
        self._mark(R, W, tok)
        self.n_ins += 1 + len(waits)

    def dma(self, qname, out, in_, R=(), W=(), **kw):
        E = self.eng[qname]
        half = self.NDS // 2
        qi = 0 if qname == "sp" else 1
        i = qi * half + self.dnext2[qi]
        self.dnext2[qi] = (self.dnext2[qi] + 1) % half
        toks = self._deps(R, W)
        if self.dval[i] > 0:
            toks.append((self.dsem[i], self.dval[i], "d%d" % i))
        waits = self._wait_list(E, toks)
        self.dval[i] += 16
        tok = (self.dsem[i], self.dval[i], "d%d" % i)
        ds = self.dsem[i]

        def run(e, waits=waits, ds=ds, out=out, in_=in_, kw=kw):
            for (s, v) in waits:
                e.wait_ge(s, v)
            e.dma_start(out=out, in_=in_, **kw).then_inc(ds, 16)
        E["ops"].append(run)
        self._mark(R, W, tok)
        self.n_ins += 1 + len(waits)

    def idma(self, out, in_, off_ap, R=(), W=()):
        E = self.eng["pool"]
        half = self.NDS // 2
        i = half + self.dnext2[1]
        self.dnext2[1] = (self.dnext2[1] + 1) % half
        toks = self._deps(R, W)
        if self.dval[i] > 0:
            toks.append((self.dsem[i], self.dval[i], "d%d" % i))
        waits = self._wait_list(E, toks)
        self.dval[i] += 16
        tok = (self.dsem[i], self.dval[i], "d%d" % i)
        ds = self.dsem[i]

        def run(e, waits=waits, ds=ds):
            for (s_, v) in waits:
                e.wait_ge(s_, v)
            e.indirect_dma_start(out=out, out_offset=None, in_=in_,
                                 in_offset=bass.IndirectOffsetOnAxis(ap=off_ap, axis=0)).then_inc(ds, 16)
        E["ops"].append(run)
        self._mark(R, W, tok)
        self.n_ins += 1 + len(waits)

    def finish(self, final_keys):
        toks = [(self.dsem[i], self.dval[i], "d%d" % i) for i in range(self.NDS) if self.dval[i] > 0]
        E = self.eng["sp"]
        waits = self._wait_list(E, toks)

        def run(e, waits=waits):
            for (s, v) in waits:
                e.wait_ge(s, v)
        E["ops"].append(run)

    def replay(self, block):
        def mk(name):
            ops = self.eng[name]["ops"]

            def f(e):
                for o in ops:
                    o(e)
            return f
        block.tensor(mk("pe"))
        block.scalar(mk("act"))
        block.vector(mk("dve"))
        block.gpsimd(mk("pool"))
        block.sync(mk("sp"))


def build(NT=16, LAYERS=4, SAMPLE=True):
    from contextlib import ExitStack
    nc = bass.Bass("TRN2", target_bir_lowering=False)
    st = ExitStack()

    def din(name, shape, dt=F32):
        return nc.dram_tensor(name, list(shape), dt, kind="ExternalInput").ap()

    def dout(name, shape, dt=F32):
        return nc.dram_tensor(name, list(shape), dt, kind="ExternalOutput").ap()

    xp = din("xp", [SEQ, D])
    xs = din("xs", [128, D])
    ln_g = din("ln_g", [12, D])
    ln_b = din("ln_b", [12, D])
    ffn_w_in = din("ffn_w_in", [8, D, 2 * DFF])
    ffn_w_out = din("ffn_w_out", [8, DFF, D])
    cst = din("cst", [128, 1024])
    cmlp_w_in = din("cmlp_w_in", [2, D, 2 * D])
    cmlp_ln_g = din("cmlp_ln_g", [2, D])
    cmlp_ln_b = din("cmlp_ln_b", [2, D])
    cmlp_w_s = din("cmlp_w_s", [2, 8, 128, 128])
    cmlp_b_s = din("cmlp_b_s", [2, 8, 128])
    cmlp_w_out = din("cmlp_w_out", [2, D, D])
    cv_s = dout("cv_s", [2, 128, D])
    moba_w_qkv = din("moba_w_qkv", [D, 3 * D])
    moba_w_out = din("moba_w_out", [D, D])
    ropec = din("ropec", [SEQ + 128, 128])
    cstb = din("cstb", [128, 6144])
    k_p = dout("k_p", [SEQ, D])
    v_p = dout("v_p", [SEQ, D])
    k_s = dout("k_s", [128, D])
    v_s = dout("v_s", [128, D])
    cache_k = din("cache_k", [2560 * 128, D])
    cache_v = din("cache_v", [2560 * 128, D])
    page_table = din("page_table", [16, 16], I32)
    sel8d = din("sel8d", [8, 1024])
    ssm_w_in = din("ssm_w_in", [D, 6176])
    ssm_w_conv = din("ssm_w_conv", [4, 4096])
    ssm_b_conv = din("ssm_b_conv", [1, 4096])
    ssm_dt_bias = din("ssm_dt_bias", [1, 32])
    ssm_a_log = din("ssm_a_log", [1, 32])
    ssm_d = din("ssm_d", [1, 32])
    ssm_norm_g = din("ssm_norm_g", [1, 2048])
    ssm_w_out = din("ssm_w_out", [2048, D])
    state_conv = din("state_conv", [48, 4096])
    state_ssm = din("state_ssm", [16, 2048, 128])
    cst2 = din("cst2", [128, 768])
    conv_p = dout("conv_p", [3, 4096])
    ssm_p = dout("ssm_p", [2048, 128])
    conv_s = dout("conv_s", [48, 4096])
    ssm_s = dout("ssm_s", [16, 2048, 128])
    KT_hist = nc.dram_tensor("KT_hist", [8, 128, SEQ], BF16).ap()
    V_hist = nc.dram_tensor("V_hist", [SEQ, D], BF16).ap()
    y_p = dout("y_p", [SEQ, D])
    y_s = dout("y_s", [128, D])

    def sb(name, shape, dt):
        return st.enter_context(nc.sbuf_tensor(name, list(shape), dt))

    def ps(name, shape, dt=F32):
        return st.enter_context(nc.psum_tensor(name, list(shape), dt))

    S = Sched(nc, st)
    DBG = os.environ.get("MK_DBG", "")
    MIX = os.environ.get("MK_MIX", "1") == "1"

    hT = sb("hT", [128, 8, 512], F32)
    hb = sb("hb", [128, 8, 512], BF16)
    aT = sb("aT", [128, NFC, 512], BF16)
    wA = sb("wA", [128, 2, 8, 1024], BF16)
    wB = sb("wB", [128, NFC, 1024], BF16)
    xin = sb("xin", [128, 2, D], F32)
    tmp = sb("tmp", [128, 2, 512], F32)
    zb = sb("zb", [128, 2, 512], BF16)
    stat = sb("stat", [128, 4, 512], F32)
    cs = sb("cs", [128, 1024], F32)
    ident_b = sb("ident_b", [128, 128], BF16)
    onesb = sb("onesb", [128, 128], BF16)
    lng = sb("lng", [128, 96], F32)
    lnb = sb("lnb", [128, 96], F32)
    P = [ps("ps%d" % i, [128, 512]) for i in range(8)]

    ident = cs[:, 0:128]

    wkeys = {}

    pending = {}

    def WKEY(name, idx=None):
        if (name, idx) in pending:
            d2, s2, rows_per = pending.pop((name, idx))
            nr = d2.shape[0]
            for r in range(0, nr, rows_per):
                S.dma("pool", d2[r:min(nr, r + rows_per), :], s2[r:min(nr, r + rows_per), :], W=[("wc", name, idx, r)])
                wkeys.setdefault((name, idx), []).append(("wc", name, idx, r))
        return wkeys.get((name, idx), [])

    if os.environ.get("MK_PRECAST", "1") == "1":
        def precast(name, src, rows_per=128):
            shp = list(src.shape)
            dst = nc.dram_tensor(name + "_b", shp, BF16).ap()
            if len(shp) == 2:
                pairs = [(None, dst, src)]
            else:
                pairs = [(i, dst[i], src[i]) for i in range(shp[0])]
            for (idx, d2, s2) in pairs:
                pending[(name, idx)] = (d2, s2, rows_per)
            return dst
        ffn_w_in = precast("ffn_w_in", ffn_w_in)
        ffn_w_out = precast("ffn_w_out", ffn_w_out)
        if MIX:
            cmlp_w_in = precast("cmlp_w_in", cmlp_w_in)
            cmlp_w_out = precast("cmlp_w_out", cmlp_w_out)
            moba_w_qkv = precast("moba_w_qkv", moba_w_qkv)
            moba_w_out = precast("moba_w_out", moba_w_out)
            ssm_w_in = precast("ssm_w_in", ssm_w_in)
            ssm_w_out = precast("ssm_w_out", ssm_w_out)

    S.dma("sp", cs[:, :], cst[:, :], W=["cs"])
    epsb = sb("epsb", [128, 2], F32)
    if "B" not in DBG:
        S.op("dve", lambda e: e.tensor_copy(out=ident_b[:, :], in_=cs[:, 0:128]), R=["cs"], W=["ident_b"])
        S.op("dve", lambda e: e.memset(onesb[:, :], 1.0 / D), W=["onesb"])
        S.op("dve", lambda e: e.memset(epsb[:, :], LN_EPS), W=["epsb"])

    def load_cols(dst, src_rows, nrows, key):
        S.dma("sp", xin[0:nrows, 0, 0:128], src_rows, R=[], W=["xin0"])
        S.op("pe", lambda e: e.transpose(out=P[7][:, 0:nrows], in_=xin[0:nrows, 0, 0:128],
                                         identity=ident[0:nrows, 0:nrows]),
             R=["xin0", "cs"], W=["P7"])
        S.op("dve", lambda e: e.tensor_copy(out=dst, in_=P[7][:, 0:nrows]), R=["P7"], W=[key])

    if os.environ.get("MK_COLS", "1") == "1":
        load_cols(lng[:, :], ln_g.rearrange("l (c p) -> (l c) p", p=128), 96, "lng")
        load_cols(lnb[:, :], ln_b.rearrange("l (c p) -> (l c) p", p=128), 96, "lnb")

    WTp = sb("WTp", [128, 2, 8, 128], BF16)
    WTs = sb("WTs", [128, 2, 8, 128], BF16)
    statf = stat[:, :, :].rearrange("p a b -> p (a b)")
    cGv = statf[:, 0:D]
    cBv = statf[:, D:2 * D]
    bSs = sb("bSs", [128, 2, 8, 8], F32)
    arena = sb("arena", [128, 9216], BF16)
    vnb = arena[:, 0:4 * D].rearrange("p (a b) -> p a b", a=4)
    st6 = sb("st6", [128, 16], F32)
    Rw = sb("Rw", [128, 8, 8], F32)
    wBf = wB[:, :, :].rearrange("p a b -> p (a b)")
    wBv = wBf[:, 0:8 * 2048].rearrange("p (k n) -> p k n", k=8)
    if MIX:
        for j in range(2):
            S.dma("sp", bSs[:, j, :, :], bass.AP(tensor=cmlp_b_s.tensor, offset=j * 1024, ap=[[0, 128], [128, 8], [1, 8]]),
                  W=["bSs"])
            xv = xin[:, 0, :].rearrange("p (g s) -> p g s", g=8)
            S.dma("sp", xv, cmlp_w_s[j].rearrange("g t s -> t g s"), W=["xin0"])
            for g in range(8):
                S.op("dve", lambda e, g=g: e.tensor_tensor(out=xin[:, 0, g * 128:(g + 1) * 128],
                                                           in0=xin[:, 0, g * 128:(g + 1) * 128], in1=cs[:, 256:384], op=ALU.mult),
                     R=["cs", "xin0"], W=["xin0"])
                pb = P[g % 2]
                S.op("pe", lambda e, g=g, pb=pb: e.transpose(out=pb[:, 0:128], in_=xin[:, 0, g * 128:(g + 1) * 128], identity=ident),
                     R=["xin0", "cs"], W=["P%d" % (g % 2)])
                S.op("dve", lambda e, g=g, pb=pb, j=j: e.tensor_copy(out=WTp[:, j, g, :], in_=pb[:, 0:128]),
                     R=["P%d" % (g % 2)], W=["WTp"])
            for q in range(16):
                S.dma("sp", Rw[q * 8:(q + 1) * 8, :, :], cmlp_w_s[j, :, 0:8, 0:8].rearrange("g t s -> t g s"), W=["Rw"])
            for g in range(8):
                S.op("dve", lambda e, g=g: e.tensor_tensor(
                    out=xin[:, 1, g * 128:(g + 1) * 128].rearrange("p (a b) -> p a b", a=16),
                    in0=Rw[:, g, :].unsqueeze(1).to_broadcast([128, 16, 8]),
                    in1=cs[:, 384:512].rearrange("p (a b) -> p a b", a=16), op=ALU.mult),
                    R=["Rw", "cs"], W=["xin1"])
                pb = P[2 + g % 2]
                S.op("pe", lambda e, g=g, pb=pb: e.transpose(out=pb[:, 0:128], in_=xin[:, 1, g * 128:(g + 1) * 128], identity=ident),
                     R=["xin1", "cs"], W=["P%d" % (2 + g % 2)])
                S.op("dve", lambda e, g=g, pb=pb, j=j: e.tensor_copy(out=WTs[:, j, g, :], in_=pb[:, 0:128]),
                     R=["P%d" % (2 + g % 2)], W=["WTs"])

    SEL = wBf[0:32, 16384:20480]
    CMt = wBf[:, 20480:22528].rearrange("p (a b) -> p a b", a=4)
    ones_b1 = sb("ones_b1", [128, 128], BF16)
    kmT = sb("kmT", [128, 8, 32], F32)
    kmTb = sb("kmTb", [128, 8, 32], BF16)
    rc = sb("rc", [128, 128], F32)
    xpad = sb("xpad", [128, 2, 520], F32)
    PTf = xpad[:, :, 0:512]
    rd = sb("rd", [128, 512], F32)
    gsb = sb("gsb", [128, 8, 32], F32)
    biasq = sb("biasq", [128, 8, 32], F32)
    mx = sb("mx", [128, 8, 8], F32)
    OT = arena[:, 0:4096].rearrange("p (a b) -> p a b", a=8)
    ktt = arena[:, 4096:8192].rearrange("p (a b) -> p a b", a=8)
    PT = arena[:, 8192:9216].rearrange("p (a b) -> p a b", a=2)
    QT = aT[:, 0:8, :]
    biasT = aT[:, 8:16, :]
    wAf = wA[:, :, :, :].rearrange("p a b c -> p (a b c)")
    wBq = wBf[:, 0:8192].rearrange("p (k n) -> p k n", k=8)
    KWA = [[("wA", 0, 0), ("wA", 0, 1)], [("wA", 1, 0), ("wA", 1, 1)]]
    KWB = [("wB", 0), ("wB", 1)]
    SCALE = 128 ** -0.5
    arena_f = arena[:, 4096:8192].bitcast(F32)
    arena_i = arena[:, 8192:9216].bitcast(I32)
    sel8 = arena_f[0:8, 0:1024]
    Pown = arena_f[:, 1024:1152]
    PTs = arena_f[:, 1152:1280].rearrange("p (a b) -> p a b", a=2)
    fin = arena_f[:, 1280:1408].rearrange("p (a b) -> p a b", a=2)
    smal = arena_f[:, 1408:1472]
    bT8 = arena_f[0:8, 1472:1536]
    ptf = arena_f[:, 1536:1552]
    ptb = arena_i[:, 0:16]
    pidx = arena_i[:, 16:32]
    aTf = aT[:, :, :].rearrange("p a b -> p (a b)").bitcast(F32)
    QTf = aTf[:, 0:1024].rearrange("p (a b) -> p a b", a=8)
    KTnf = aTf[:, 1024:2048].rearrange("p (a b) -> p a b", a=8)
    Oown = aTf[:, 2048:3072].rearrange("p (a b) -> p a b", a=8)
    dOwn = aTf[:, 3072:4096].rearrange("p (a b) -> p a b", a=8)
    wBf32 = wBf[:, 0:16384].bitcast(F32)
    Kpg = wBf32.rearrange("p (s j d) -> p s j d", s=2, j=4)
    wAf32 = wAf.bitcast(F32)
    KTp = wAf32[:, 0:2048].rearrange("p (s h k) -> p s h k", s=2, h=8)
    STall = wAf32[:, 2048:3072].rearrange("p (j c) -> p j c", j=16)
    onesf = cs[:, 641:769]
    if MIX:
        S.op("dve", lambda e: e.memset(ones_b1[:, :], 1.0), W=["ones_b1"])

    stateT = sb("stateT", [128, 2048], F32)
    stateTb = sb("stateTb", [128, 2048], BF16)
    halo = sb("halo", [128, 32, 3], F32)
    wcvdt = sb("wcvdt", [128, 8, 32], BF16)
    sm = sb("sm", [128, 8, 32], F32)
    dts = sb("dts", [128, 4, 32], F32)
    prm = sb("prm", [128, 3, 32], F32)
    wcv = sb("wcv", [128, 128], F32)
    bcv = sb("bcv", [128, 32], F32)
    cs2 = sb("cs2", [128, 768], F32)
    SLc, NEGM, UB, SLB, SAME = (cs2[:, i * 128:(i + 1) * 128] for i in range(5))
    seqm, lastm = cs2[:, 640:656], cs2[:, 656:672]
    xtok = wBf[:, 0:8192].rearrange("p (a b) -> p a b", a=4)
    Btok = wBf[:, 8192:12288].rearrange("p (a b) -> p a b", a=4)
    zs = wBf[:, 12288:20480].rearrange("p (a b) -> p a b", a=4)
    BCt = arena[:, 0:8192].rearrange("p (a b) -> p a b", a=16)
    MT = arena[:, 8192:9216].rearrange("p (a b) -> p a b", a=2)
    MTf = arena[:, 8192:9216]
    hbf = hb[:, :, :].rearrange("p a b -> p (a b)")
    xdt, xdtd = hbf[:, 0:2048], hbf[:, 2048:4096]
    ysq = hbf.bitcast(F32)
    yv = xin[:, :, :].rearrange("p a b -> p (a b)")
    ynT = aT[:, 0:16, :]
    xpf = xpad[:, :, :].rearrange("p a b -> p (a b)")
    CBt = xpf[:, 0:1024].rearrange("p (g t) -> p g t", g=8)
    CmT = xpf[:, 520:1032].bitcast(BF16).rearrange("p (g t) -> p g t", g=8) if False else None
    if MIX and (LAYERS >= 3 or os.environ.get("MK_ONLY", "") == "2"):
        S.dma("sp", cs2[:, :], cst2[:, :], W=["cs2"])
        S.dma("sp", prm[:, 0, :], bass.AP(tensor=ssm_dt_bias.tensor, offset=0, ap=[[0, 128], [1, 32]]), W=["prm"])
        S.dma("sp", prm[:, 1, :], bass.AP(tensor=ssm_a_log.tensor, offset=0, ap=[[0, 128], [1, 32]]), W=["prm"])
        S.dma("sp", prm[:, 2, :], bass.AP(tensor=ssm_d.tensor, offset=0, ap=[[0, 128], [1, 32]]), W=["prm"])
        S.op("act", lambda e: e.activation(out=prm[:, 1, :], in_=prm[:, 1, :], func=AF.Exp), R=["prm"], W=["prm"])
        S.op("dve", lambda e: e.tensor_scalar(out=prm[:, 1, :], in0=prm[:, 1, :], scalar1=-1.0, scalar2=None, op0=ALU.mult),
             R=["prm"], W=["prm"])
        load_cols(wcv[:, :], ssm_w_conv.rearrange("k (c p) -> (k c) p", p=128), 128, "wcv")
        load_cols(bcv[:, :], ssm_b_conv.rearrange("o (c p) -> (o c) p", p=128), 32, "bcv")
        S.op("dve", lambda e: e.memset(stateT[:, :], 0.0), W=["stateT"])
        S.op("dve", lambda e: e.memset(stateTb[:, :], 0.0), W=["stateTb"])
        S.op("dve", lambda e: e.memset(halo[:, :, :], 0.0), W=["halo"])

    def load_tile(src, r0, T):
        nsub = T // 128
        for m in range(nsub):
            slot = m % 2
            S.dma("sp", xin[:, slot, :], src[r0 + m * 128: r0 + (m + 1) * 128, :], W=["xin%d" % slot])
            for c in range(8):
                pb = P[c % 2]
                S.op("pe", lambda e, pb=pb, slot=slot, c=c: e.transpose(
                    out=pb[:, 0:128], in_=xin[:, slot, c * 128:(c + 1) * 128], identity=ident),
                    R=["xin%d" % slot, "cs"], W=["P%d" % (c % 2)])
                S.op("dve", lambda e, pb=pb, c=c, m=m: e.tensor_copy(
                    out=hT[:, c, m * 128:(m + 1) * 128], in_=pb[:, 0:128]),
                    R=["P%d" % (c % 2)], W=[("hT", c)])
                if "C" in DBG:
                    S.op("act", lambda e, pb=pb, c=c, m=m: e.activation(
                        out=hb[:, c, m * 128:(m + 1) * 128], in_=pb[:, 0:128], func=AF.Identity),
                        R=["P%d" % (c % 2)], W=[("hb", c)])
                elif "E" not in DBG:
                    S.op("act", lambda e, pb=pb, c=c, m=m: e.activation(
                        out=hb[:, c, m * 128:(m + 1) * 128], in_=hT[:, c, m * 128:(m + 1) * 128], func=AF.Identity),
                        R=[("hT", c)], W=[("hb", c)])
                elif "A" not in DBG:
                    S.op("act", lambda e, pb=pb, c=c, m=m: e.copy(
                        out=hb[:, c, m * 128:(m + 1) * 128], in_=pb[:, 0:128]),
                        R=["P%d" % (c % 2)], W=[("hb", c)])

    def store_tile(dst, r0, T):
        nsub = T // 128
        for m in range(nsub):
            slot = m % 2
            for c in range(8):
                pb = P[c % 2]
                S.op("pe", lambda e, pb=pb, c=c, m=m: e.transpose(
                    out=pb[:, 0:128], in_=hT[:, c, m * 128:(m + 1) * 128], identity=ident),
                    R=[("hT", c), "cs"], W=["P%d" % (c % 2)])
                S.op("dve", lambda e, pb=pb, c=c, slot=slot: e.tensor_copy(
                    out=xin[:, slot, c * 128:(c + 1) * 128], in_=pb[:, 0:128]),
                    R=["P%d" % (c % 2)], W=["xin%d" % slot])
            S.dma("sp", dst[r0 + m * 128: r0 + (m + 1) * 128, :], xin[:, slot, :],
                  R=["xin%d" % slot], W=["out"])

    def layer_norm(li, T):
        for c in range(8):
            s = c % 2
            S.op("act", lambda e, c=c, s=s: e.copy(out=zb[:, s, 0:T], in_=hT[:, c, 0:T]),
                 R=[("hT", c)], W=[("zb", s)])
            S.op("pe", lambda e, c=c, s=s: e.matmul(P[4][:, 0:T], lhsT=onesb[:, :], rhs=zb[:, s, 0:T],
                                                  start=(c == 0), stop=(c == 7)),
                 R=[("zb", s), "onesb"], W=["P4"])
        for c in range(8):
            s = c % 2
            S.op("act", lambda e, c=c, s=s: e.activation(out=zb[:, s, 0:T], in_=hT[:, c, 0:T], func=AF.Square),
                 R=[("hT", c)], W=[("zb", s)])
            S.op("pe", lambda e, c=c, s=s: e.matmul(P[5][:, 0:T], lhsT=onesb[:, :], rhs=zb[:, s, 0:T],
                                                  start=(c == 0), stop=(c == 7)),
                 R=[("zb", s), "onesb"], W=["P5"])
        mean, msq, rstd = stat[:, 0, 0:T], stat[:, 1, 0:T], stat[:, 2, 0:T]
        S.op("dve", lambda e: e.tensor_copy(out=mean, in_=P[4][:, 0:T]), R=["P4"], W=["mean", "cG"])
        S.op("dve", lambda e: e.tensor_tensor(out=msq, in0=mean, in1=mean, op=ALU.mult), R=["mean"], W=["msq", "cG"])
        S.op("dve", lambda e: e.tensor_tensor(out=rstd, in0=P[5][:, 0:T], in1=msq, op=ALU.subtract),
             R=["P5", "msq"], W=["rstd", "cB"])
        S.op("act", lambda e: e.activation(out=rstd, in_=rstd, func=AF.Ln, bias=epsb[:, 0:1]), R=["rstd", "epsb"], W=["rstd"])
        S.op("act", lambda e: e.activation(out=rstd, in_=rstd, func=AF.Exp, scale=-0.5), R=["rstd"], W=["rstd"])
        for c in range(8):
            s = c % 2
            t = tmp[:, s, 0:T]
            S.op("dve", lambda e, c=c, t=t: e.tensor_tensor(out=t, in0=hT[:, c, 0:T], in1=mean, op=ALU.subtract),
                 R=[("hT", c), "mean"], W=[("tmp", s)])
            S.op("dve", lambda e, t=t: e.tensor_tensor(out=t, in0=t, in1=rstd, op=ALU.mult),
                 R=[("tmp", s), "rstd"], W=[("tmp", s)])
            col = li * 8 + c
            S.op("dve", lambda e, c=c, t=t, col=col: e.tensor_scalar(
                out=hT[:, c, 0:T], in0=t, scalar1=lng[:, col:col + 1], scalar2=lnb[:, col:col + 1],
                op0=ALU.mult, op1=ALU.add), R=[("tmp", s), "lng", "lnb"], W=[("hT", c)])
            S.op("act", lambda e, c=c: e.copy(out=hb[:, c, 0:T], in_=hT[:, c, 0:T]),
                 R=[("hT", c)], W=[("hb", c)])

    GROUPS = [(0, 4), (4, 4), (8, 4), (12, 4), (16, 4), (20, 2)]

    def ffn(fi, li, T, prefetch=None):
        w_in = ffn_w_in[fi]
        w_out = ffn_w_out[fi]

        def load_group(gi):
            j0, nj = GROUPS[gi]
            slot = gi % 2
            w = nj * 128
            S.dma("pool", wA[:, slot, :, 0:w],
                  w_in[:, j0 * 128: j0 * 128 + w].rearrange("(k p) n -> p k n", p=128),
                  R=WKEY("ffn_w_in", fi), W=[("wA", slot, 0)])
            S.dma("pool", wA[:, slot, :, 512:512 + w],
                  w_in[:, DFF + j0 * 128: DFF + j0 * 128 + w].rearrange("(k p) n -> p k n", p=128),
                  R=WKEY("ffn_w_in", fi), W=[("wA", slot, 1)])

        load_group(0)
        for gi, (j0, nj) in enumerate(GROUPS):
            if gi + 1 < len(GROUPS):
                load_group(gi + 1)
            if gi == 1:
                for q in range(2):
                    S.dma("pool", wB[:, q * 11:(q + 1) * 11, :],
                          w_out[q * 11 * 128:(q + 1) * 11 * 128, :].rearrange("(j p) n -> p j n", p=128),
                          R=WKEY("ffn_w_out", fi), W=[("wB", q)])
            slot = gi % 2
            for jj in range(nj):
                j = j0 + jj
                pg, pu = P[(j % 2) * 2], P[(j % 2) * 2 + 1]
                kg, ku = "P%d" % ((j % 2) * 2), "P%d" % ((j % 2) * 2 + 1)
                for k in range(8):
                    S.op("pe", lambda e, k=k, jj=jj, pg=pg, slot=slot: e.matmul(
                        pg[:, 0:T], lhsT=wA[:, slot, k, jj * 128:(jj + 1) * 128], rhs=hb[:, k, 0:T],
                        start=(k == 0), stop=(k == 7)), R=[("wA", slot, 0), ("hb", k)], W=[kg])
                for k in range(8):
                    S.op("pe", lambda e, k=k, jj=jj, pu=pu, slot=slot: e.matmul(
                        pu[:, 0:T], lhsT=wA[:, slot, k, 512 + jj * 128:512 + (jj + 1) * 128], rhs=hb[:, k, 0:T],
                        start=(k == 0), stop=(k == 7)), R=[("wA", slot, 1), ("hb", k)], W=[ku])
                s = j % 2
                S.op("act", lambda e, pg=pg, s=s: e.activation(out=tmp[:, s, 0:T], in_=pg[:, 0:T], func=AF.Silu),
                     R=[kg], W=[("tmp", s)])
                S.op("dve", lambda e, pu=pu, s=s, j=j: e.scalar_tensor_tensor(
                    out=aT[:, j, 0:T], in0=tmp[:, s, 0:T], scalar=0.5, in1=pu[:, 0:T],
                    op0=ALU.mult, op1=ALU.mult), R=[("tmp", s), ku], W=[("aT", j)])
        pre = prefetch() if prefetch is not None else None
        for c in range(8):
            po = P[6 + c % 2]
            ko = "P%d" % (6 + c % 2)
            for j in range(NFC):
                S.op("pe", lambda e, j=j, c=c, po=po: e.matmul(
                    po[:, 0:T], lhsT=wB[:, j, c * 128:(c + 1) * 128], rhs=aT[:, j, 0:T],
                    start=(j == 0), stop=(j == NFC - 1)), R=[("wB", j // 11), ("aT", j)], W=[ko])
            S.op("dve", lambda e, c=c, po=po: e.scalar_tensor_tensor(
                out=hT[:, c, 0:T], in0=hT[:, c, 0:T], scalar=ALPHA, in1=po[:, 0:T],
                op0=ALU.mult, op1=ALU.add), R=[ko, ("hT", c)], W=[("hT", c)])
        layer_norm(li, T)
        return pre

    def cmlp(j, li, T, sample, pre=None):
        nsub = T // 128
        if pre is None:
            pre = cmlp_prefetch(j)()
        rk_u, rk_vv = pre
        rk_o = load_w(wBq, cmlp_w_out[j], KWB, "co", WKEY("cmlp_w_out", j))
        WK = rk_u
        S.dma("sp", cGv, bass.AP(tensor=cmlp_ln_g.tensor, offset=j * D, ap=[[0, 128], [1, D]]), W=["mean", "msq", "cG"])
        S.dma("sp", cBv, bass.AP(tensor=cmlp_ln_b.tensor, offset=j * D, ap=[[0, 128], [1, D]]), W=["rstd", "cB"])
        for jc in range(8):
            pb, kb = P[jc % 2], "P%d" % (jc % 2)
            for k in range(8):
                S.op("pe", lambda e, k=k, jc=jc, pb=pb: e.matmul(
                    pb[:, 0:T], lhsT=wA[:, 0, k, jc * 128:(jc + 1) * 128], rhs=hb[:, k, 0:T],
                    start=(k == 0), stop=(k == 7)), R=rk_u + [("hb", k)], W=[kb])
            s2 = jc % 2
            S.op("act", lambda e, s2=s2, pb=pb: e.activation(out=tmp[:, s2, 0:T], in_=pb[:, 0:T], func=AF.Gelu),
                 R=[kb], W=[("tmp", s2)])
            S.op("dve", lambda e, jc=jc, s2=s2: e.tensor_copy(out=aT[:, jc, 0:T], in_=tmp[:, s2, 0:T]),
                 R=[("tmp", s2)], W=[("aT", jc)])
        for m in range(nsub):
            for n in range(2):
                pb, kb = P[2 + n], "P%d" % (2 + n)
                for k in range(8):
                    S.op("pe", lambda e, k=k, n=n, m=m, pb=pb: e.matmul(
                        pb[:, 0:512], lhsT=hb[:, k, m * 128:(m + 1) * 128],
                        rhs=wA[:, 1, k, n * 512:(n + 1) * 512],
                        start=(k == 0), stop=(k == 7)), R=rk_vv + [("hb", k)], W=[kb])
                S.op("act", lambda e, n=n, pb=pb: e.activation(out=xin[:, 0, n * 512:(n + 1) * 512], in_=pb[:, 0:512],
                                                              func=AF.Gelu), R=[kb], W=["xin0"])
            for n in range(2):
                S.op("dve", lambda e, n=n: e.bn_stats(out=st6[:, n * 6:(n + 1) * 6], in_=xin[:, 0, n * 512:(n + 1) * 512]),
                     R=["xin0"], W=["st6"])
            S.op("dve", lambda e: e.bn_aggr(out=st6[:, 12:14], in_=st6[:, 0:12]), R=["st6"], W=["st6"])
            S.op("act", lambda e: e.activation(out=st6[:, 14:15], in_=st6[:, 13:14], func=AF.Ln, bias=epsb[:, 0:1]),
                 R=["st6", "epsb"], W=["st6"])
            S.op("act", lambda e: e.activation(out=st6[:, 14:15], in_=st6[:, 14:15], func=AF.Exp, scale=-0.5),
                 R=["st6"], W=["st6"])
            S.op("dve", lambda e: e.tensor_scalar(out=xin[:, 0, :], in0=xin[:, 0, :], scalar1=st6[:, 12:13],
                                                  scalar2=st6[:, 14:15], op0=ALU.subtract, op1=ALU.mult),
                 R=["st6", "xin0"], W=["xin0"])
            S.op("dve", lambda e: e.tensor_tensor(out=xin[:, 0, :], in0=xin[:, 0, :], in1=cGv, op=ALU.mult),
                 R=["cG", "xin0"], W=["xin0"])
            S.op("dve", lambda e: e.tensor_tensor(out=xin[:, 0, :], in0=xin[:, 0, :], in1=cBv, op=ALU.add),
                 R=["cB", "xin0"], W=["xin0"])
            if sample:
                S.dma("sp", cv_s[j], xin[:, 0, :], R=["xin0"], W=["out"])
            S.op("act", lambda e, m=m: e.activation(out=vnb[:, m, :], in_=xin[:, 0, :], func=AF.Identity),
                 R=["xin0"], W=[("vnb", m)])
        WT = WTs if sample else WTp
        bSp = xpad[:, :, :].rearrange("p a b -> p (a b)")[:, 0:1024].rearrange("p (g t) -> p g t", g=8)
        if not sample:
            S.dma("sp", bSp, bass.AP(tensor=cmlp_b_s.tensor, offset=j * 1024, ap=[[0, 128], [128, 8], [1, 128]]),
                  W=["bSp", ("PTf", 0), ("PTf", 1)])
        for g in range(8):
            pb, kb = P[4 + g % 2], "P%d" % (4 + g % 2)
            for m in range(nsub):
                S.op("pe", lambda e, g=g, m=m, pb=pb: e.matmul(
                    pb[:, m * 128:(m + 1) * 128], lhsT=vnb[:, m, g * 128:(g + 1) * 128], rhs=WT[:, j, g, :],
                    start=True, stop=True), R=[("vnb", m), "WTp", "WTs"], W=[kb])
            s2 = g % 2
            bw = 8 if sample else 128
            S.op("dve", lambda e, g=g, pb=pb, s2=s2, bw=bw: e.tensor_tensor(
                out=tmp[:, s2, 0:T].rearrange("p (a b) -> p a b", b=bw),
                in0=pb[:, 0:T].rearrange("p (a b) -> p a b", b=bw),
                in1=(bSs[:, j, g, :] if sample else bSp[:, g, :]).unsqueeze(1).to_broadcast([128, T // bw, bw]), op=ALU.add),
                R=[kb, "bSp", "bSs"], W=[("tmp", s2)])
            S.op("dve", lambda e, g=g, s2=s2: e.tensor_tensor(out=aT[:, 8 + g, 0:T], in0=tmp[:, s2, 0:T], in1=aT[:, g, 0:T],
                                                           op=ALU.mult), R=[("tmp", s2), ("aT", g)], W=[("aT", 8 + g)])
        WK2 = rk_o
        for c in range(8):
            po, ko = P[6 + c % 2], "P%d" % (6 + c % 2)
            for k in range(8):
                S.op("pe", lambda e, k=k, c=c, po=po: e.matmul(
                    po[:, 0:T], lhsT=wBq[:, k, c * 128:(c + 1) * 128], rhs=aT[:, 8 + k, 0:T],
                    start=(k == 0), stop=(k == 7)), R=WK2 + [("aT", 8 + k)], W=[ko])
            S.op("dve", lambda e, c=c, po=po: e.scalar_tensor_tensor(
                out=hT[:, c, 0:T], in0=hT[:, c, 0:T], scalar=ALPHA, in1=po[:, 0:T],
                op0=ALU.mult, op1=ALU.add), R=[ko, ("hT", c)], W=[("hT", c)])
        layer_norm(li, T)

    def load_w(dst, src, canon, tag, wk=()):
        keys = list(canon)
        for q in range(2):
            kq = ("x", tag, q)
            S.dma("pool", dst[:, q * 4:(q + 1) * 4, :], src[q * 512:(q + 1) * 512, :].rearrange("(k p) n -> p k n", p=128),
                  R=list(wk), W=(list(canon) if q == 0 else []) + [kq])
            keys.append(kq)
        return keys

    def cmlp_prefetch(j):
        def f():
            return (load_w(wA[:, 0], cmlp_w_in[j][:, 0:D], KWA[0], "cu", WKEY("cmlp_w_in", j)), load_w(wA[:, 1], cmlp_w_in[j][:, D:2 * D], KWA[1], "cv", WKEY("cmlp_w_in", j)))
        return f

    def moba_prefetch():
        return (load_w(wA[:, 0], moba_w_qkv[:, 0:D], KWA[0], "mq", WKEY("moba_w_qkv")), load_w(wA[:, 1], moba_w_qkv[:, D:2 * D], KWA[1], "mk", WKEY("moba_w_qkv")))

    def ssd_prefetch():
        return [load_w(wA[:, g], ssm_w_in[:, 2048 + g * 1024:2048 + (g + 1) * 1024], KWA[g], "sx%d" % g, WKEY("ssm_w_in")) for g in range(2)]

    def out_proj(w_src, rhs_chunks, rhs_keys, li, T):
        rk = load_w(wA[:, 0], w_src, KWA[0], "wo", WKEY("moba_w_out"))
        for c in range(8):
            po, ko = P[6 + c % 2], "P%d" % (6 + c % 2)
            nk = len(rhs_chunks)
            for k in range(nk):
                S.op("pe", lambda e, k=k, c=c, po=po: e.matmul(
                    po[:, 0:T], lhsT=wA[:, 0, k, c * 128:(c + 1) * 128], rhs=rhs_chunks[k],
                    start=(k == 0), stop=(k == nk - 1)), R=rk + [rhs_keys[k]], W=[ko])
            S.op("dve", lambda e, c=c, po=po: e.scalar_tensor_tensor(
                out=hT[:, c, 0:T], in0=hT[:, c, 0:T], scalar=ALPHA, in1=po[:, 0:T],
                op0=ALU.mult, op1=ALU.add), R=[ko, ("hT", c)], W=[("hT", c)])
        layer_norm(li, T)

    def moba(li, T, r0, sample, pre=None):
        nsub = T // 128
        rk_q, rk_k = pre if pre is not None else moba_prefetch()
        rk_v = load_w(wBq, moba_w_qkv[:, 2 * D:3 * D], KWB, "mv", WKEY("moba_w_qkv"))
        qst, kst = xin[:, 0, :], xin[:, 1, :]
        vst = statf[:, 0:D]
        rsc = statf[:, D:D + 256].rearrange("p (h d) -> p h d", h=4)
        VK = ["mean", "msq", "cG"]
        RK = ["rstd", "cB"]
        kout, vout = (k_s, v_s) if sample else (k_p, v_p)
        ro = 0 if sample else r0
        blk0 = r0 // 256
        for m in range(nsub):
            S.dma("sp", rc[:, :], ropec[(SEQ if sample else r0) + m * 128:(SEQ if sample else r0) + (m + 1) * 128, :], W=["rc"])
            cosb = rc[:, 0:64].unsqueeze(1).to_broadcast([128, 4, 64])
            sinb = rc[:, 64:128].unsqueeze(1).to_broadcast([128, 4, 64])
            for (wv, rk, stage, skey) in ((wA[:, 0], rk_q, qst, "xin0"), (wA[:, 1], rk_k, kst, "xin1")):
                for n in range(2):
                    pb, kb = P[n], "P%d" % n
                    for k in range(8):
                        S.op("pe", lambda e, k=k, n=n, m=m, pb=pb, wv=wv: e.matmul(
                            pb[:, 0:512], lhsT=hb[:, k, m * 128:(m + 1) * 128], rhs=wv[:, k, n * 512:(n + 1) * 512],
                            start=(k == 0), stop=(k == 7)), R=rk + [("hb", k)], W=[kb])
                    psv = pb[:, 0:512].rearrange("p (h t d) -> p h t d", h=4, t=2)
                    dv = stage[:, n * 512:(n + 1) * 512].rearrange("p (h t d) -> p h t d", h=4, t=2)
                    x1, x2, d1, d2 = psv[:, :, 0, :], psv[:, :, 1, :], dv[:, :, 0, :], dv[:, :, 1, :]
                    S.op("dve", lambda e, d1=d1, x1=x1: e.tensor_tensor(out=d1, in0=x1, in1=cosb, op=ALU.mult), R=[kb, "rc"], W=[skey])
                    S.op("dve", lambda e, x2=x2: e.tensor_tensor(out=rsc, in0=x2, in1=sinb, op=ALU.mult), R=[kb, "rc"], W=RK)
                    S.op("dve", lambda e, d1=d1: e.tensor_tensor(out=d1, in0=d1, in1=rsc, op=ALU.subtract), R=RK + [skey], W=[skey])
                    S.op("dve", lambda e, d2=d2, x2=x2: e.tensor_tensor(out=d2, in0=x2, in1=cosb, op=ALU.mult), R=[kb, "rc"], W=[skey])
                    S.op("dve", lambda e, x1=x1: e.tensor_tensor(out=rsc, in0=x1, in1=sinb, op=ALU.mult), R=[kb, "rc"], W=RK)
                    S.op("dve", lambda e, d2=d2: e.tensor_tensor(out=d2, in0=d2, in1=rsc, op=ALU.add), R=RK + [skey], W=[skey])
            for n in range(2):
                pb, kb = P[n], "P%d" % n
                for k in range(8):
                    S.op("pe", lambda e, k=k, n=n, m=m, pb=pb: e.matmul(
                        pb[:, 0:512], lhsT=hb[:, k, m * 128:(m + 1) * 128], rhs=wBq[:, k, n * 512:(n + 1) * 512],
                        start=(k == 0), stop=(k == 7)), R=rk_v + [("hb", k)], W=[kb])
                S.op("act", lambda e, n=n, pb=pb: e.activation(out=vst[:, n * 512:(n + 1) * 512], in_=pb[:, 0:512], func=AF.Identity),
                     R=[kb], W=VK)
            S.dma("sp", kout[ro + m * 128:ro + (m + 1) * 128, :], kst, R=["xin1"], W=["out"])
            S.dma("sp", vout[ro + m * 128:ro + (m + 1) * 128, :], vst, R=VK, W=["out"])
            if not sample:
                S.dma("pool", V_hist[r0 + m * 128:r0 + (m + 1) * 128, :], vst, R=VK, W=[("vhist", m)])
            for (stage, skey, dstT, dkey, scl) in ((qst, "xin0", QTf if sample else QT, "QT", SCALE),
                                                   (kst, "xin1", KTnf if sample else ktt, "ktt", 1.0)):
                for g in range(2):
                    pb, kb = P[2 + g], "P%d" % (2 + g)
                    for hh in range(4):
                        h = g * 4 + hh
                        S.op("pe", lambda e, h=h, hh=hh, pb=pb, stage=stage: e.transpose(
                            out=pb[:, hh * 128:(hh + 1) * 128], in_=stage[:, h * 128:(h + 1) * 128], identity=ident),
                            R=[skey, "cs"], W=[kb])
                    S.op("dve", lambda e, g=g, pb=pb, dstT=dstT, scl=scl, m=m: e.tensor_scalar(
                        out=dstT[:, g * 4:(g + 1) * 4, m * 128:(m + 1) * 128],
                        in0=pb[:, 0:512].rearrange("p (a b) -> p a b", a=4), scalar1=scl, scalar2=None, op0=ALU.mult),
                        R=[kb], W=[dkey])
        if sample:
            S.dma("sp", sel8, sel8d[:, :], R=["ktt"], W=["sel8", "ktt"])
            for h in range(8):
                S.op("pe", lambda e, h=h: e.matmul(P[0][:, 0:128], lhsT=KTnf[:, h, :], rhs=QTf[:, h, :], start=True, stop=True),
                     R=["QT", "ktt"], W=["P0"])
                S.op("dve", lambda e: e.tensor_tensor(out=Pown, in0=P[0][:, 0:128], in1=cs[:, 512:640], op=ALU.add),
                     R=["P0", "cs"], W=["Pown"])
                S.op("act", lambda e: e.activation(out=Pown, in_=Pown, func=AF.Exp), R=["Pown"], W=["Pown"])
                S.op("pe", lambda e, h=h: e.matmul(P[1][:, 0:128], lhsT=vst[:, h * 128:(h + 1) * 128], rhs=Pown,
                                                   start=True, stop=True), R=VK + ["Pown"], W=["P1"])
                S.op("pe", lambda e, h=h: e.matmul(P[2][:, 0:128], lhsT=onesf, rhs=Pown, start=True, stop=True),
                     R=["cs", "Pown"], W=["P2"])
                S.op("dve", lambda e, h=h: e.tensor_copy(out=Oown[:, h, :], in_=P[1][:, 0:128]), R=["P1"], W=["Oown"])
                S.op("dve", lambda e, h=h: e.tensor_copy(out=dOwn[:, h, :], in_=P[2][:, 0:128]), R=["P2"], W=["dOwn"])
            for sq in range(NSEQ_S):
                S.dma("sp", ptb, bass.AP(tensor=page_table.tensor, offset=sq * 16, ap=[[0, 128], [1, 16]]), W=["ptb"])
                S.op("dve", lambda e: e.tensor_copy(out=ptf, in_=ptb), R=["ptb"], W=["ptf"])
                S.op("dve", lambda e: e.tensor_scalar(out=ptf, in0=ptf, scalar1=128.0, scalar2=cs[:, 640:641],
                                                      op0=ALU.mult, op1=ALU.add), R=["ptf", "cs"], W=["ptf"])
                S.op("dve", lambda e: e.tensor_copy(out=pidx, in_=ptf), R=["ptf"], W=["pidx"])
                for j in range(16):
                    sl, jj = (j // 4) % 2, j % 4
                    if jj == 0:
                        for j2 in range(4):
                            first = (sq == 0 and j < 8)
                            S.idma(Kpg[:, sl, j2, :], cache_k[:, :], pidx[:, j + j2:j + j2 + 1], R=["pidx"],
                                   W=(KWB if first else []) + [("Kpg", sl, j2)])
                    ks = j % 2
                    for g in range(2):
                        pb, kb = P[2 + g], "P%d" % (2 + g)
                        for hh in range(4):
                            h = g * 4 + hh
                            S.op("pe", lambda e, h=h, hh=hh, pb=pb, sl=sl, jj=jj: e.transpose(
                                out=pb[:, hh * 128:(hh + 1) * 128], in_=Kpg[:, sl, jj, h * 128:(h + 1) * 128], identity=ident),
                                R=KWB + [("Kpg", sl, jj), "cs"], W=[kb])
                        S.op("dve" if g == 0 else "act", (lambda e, g=g, pb=pb, ks=ks: e.tensor_copy(
                            out=KTp[:, ks, g * 4:(g + 1) * 4, :], in_=pb[:, 0:512].rearrange("p (a b) -> p a b", a=4))) if g == 0 else
                            (lambda e, g=g, pb=pb, ks=ks: e.activation(
                                out=KTp[:, ks, g * 4:(g + 1) * 4, :], in_=pb[:, 0:512].rearrange("p (a b) -> p a b", a=4),
                                func=AF.Identity)), R=[kb], W=KWA[0] + [("KTp", ks, g)] if (sq == 0 and j < 2) else [("KTp", ks, g)])
                    pb, kb = P[j % 2], "P%d" % (j % 2)
                    for h in range(8):
                        S.op("pe", lambda e, h=h, pb=pb, ks=ks, sq=sq: e.matmul(
                            pb[:, h * 8:(h + 1) * 8], lhsT=KTp[:, ks, h, :], rhs=QTf[:, h, sq * 8:(sq + 1) * 8],
                            start=True, stop=True), R=KWA[0] + [("KTp", ks, h // 4), "QT"], W=[kb])
                    S.op("dve", lambda e, j=j, pb=pb: e.tensor_copy(out=STall[:, j, :], in_=pb[:, 0:64]), R=[kb],
                         W=(KWA[0] if (sq == 0 and j == 0) else []) + [("ST", j)])
                for n in range(8):
                    for t2 in range(2):
                        j = 2 * n + t2
                        S.op("pe", lambda e, n=n, j=j, t2=t2: e.matmul(P[6][0:64, n:n + 1], lhsT=STall[:, j, :], rhs=onesf[:, 0:1],
                                                                     start=(t2 == 0), stop=(t2 == 1)),
                             R=KWA[0] + [("ST", j), "cs"], W=["P6"])
                S.op("dve", lambda e: e.tensor_copy(out=smal[0:64, 0:8], in_=P[6][0:64, 0:8]), R=["P6"], W=["smal"])
                S.op("dve", lambda e: e.max(out=smal[0:64, 8:16], in_=smal[0:64, 0:8]), R=["smal"], W=["smal"])
                S.op("dve", lambda e: e.tensor_tensor(out=smal[0:64, 16:24], in0=smal[0:64, 0:8],
                                                      in1=smal[0:64, 10:11].to_broadcast([64, 8]), op=ALU.is_ge),
                     R=["smal"], W=["smal"])
                S.op("dve", lambda e: e.tensor_scalar(out=smal[0:64, 16:24], in0=smal[0:64, 16:24], scalar1=-1.0, scalar2=30000.0,
                                                      op0=ALU.add, op1=ALU.mult), R=["smal"], W=["smal"])
                S.op("pe", lambda e: e.transpose(out=P[7][0:8, 0:64], in_=smal[0:64, 16:24], identity=ident[0:64, 0:64]),
                     R=["smal", "cs"], W=["P7"])
                S.op("dve", lambda e: e.tensor_copy(out=bT8, in_=P[7][0:8, 0:64]), R=["P7"], W=["bT8"])
                for j in range(16):
                    sl, jj = (j // 4) % 2, j % 4
                    n = j // 2
                    if jj == 0:
                        for j2 in range(4):
                            S.idma(Kpg[:, sl, j2, :], cache_v[:, :], pidx[:, j + j2:j + j2 + 1], R=["pidx"], W=[("Kpg", sl, j2)])
                    b2 = j % 2
                    pb, kb = P[b2], "P%d" % b2
                    S.op("pe", lambda e, n=n, pb=pb: e.matmul(pb[:, 0:64], lhsT=sel8[0:8, n * 128:(n + 1) * 128], rhs=bT8[0:8, 0:64],
                                                              start=True, stop=False), R=["sel8", "bT8"], W=[kb])
                    S.op("pe", lambda e, j=j, pb=pb: e.matmul(pb[:, 0:64], lhsT=ident, rhs=STall[:, j, :], start=False, stop=True),
                         R=KWA[0] + [("ST", j), "cs"], W=[kb])
                    S.op("act", lambda e, pb=pb, b2=b2: e.activation(out=PTs[:, b2, :], in_=pb[:, 0:64], func=AF.Exp),
                         R=[kb], W=[("PTs", b2)])
                    if j == 0:
                        S.op("pe", lambda e: e.matmul(P[4][:, 0:64], lhsT=onesf, rhs=cs[:, 769:833], start=True, stop=False),
                             R=["cs"], W=["P4"])
                    for h in range(8):
                        S.op("pe", lambda e, h=h, j=j, b2=b2, sl=sl, jj=jj: e.matmul(
                            P[4][:, h * 8:(h + 1) * 8], lhsT=Kpg[:, sl, jj, h * 128:(h + 1) * 128], rhs=PTs[:, b2, h * 8:(h + 1) * 8],
                            start=False, stop=(j == 15 and h == 7)), R=KWB + [("Kpg", sl, jj), ("PTs", b2)], W=["P4"])
                    S.op("pe", lambda e, j=j, b2=b2: e.matmul(P[5][:, 0:64], lhsT=onesf, rhs=PTs[:, b2, :],
                                                              start=(j == 0), stop=(j == 15)), R=["cs", ("PTs", b2)], W=["P5"])
                S.op("dve", lambda e, sq=sq: e.tensor_tensor(
                    out=fin[:, 0, :].rearrange("p (a b) -> p a b", a=8), in0=P[4][:, 0:64].rearrange("p (a b) -> p a b", a=8),
                    in1=Oown[:, :, sq * 8:(sq + 1) * 8], op=ALU.add), R=["P4", "Oown"], W=["fin0"])
                S.op("dve", lambda e, sq=sq: e.tensor_tensor(
                    out=fin[:, 1, :].rearrange("p (a b) -> p a b", a=8), in0=P[5][:, 0:64].rearrange("p (a b) -> p a b", a=8),
                    in1=dOwn[:, :, sq * 8:(sq + 1) * 8], op=ALU.add), R=["P5", "dOwn"], W=["fin1"])
                S.op("dve", lambda e: e.reciprocal(out=fin[:, 1, :], in_=fin[:, 1, :]), R=["fin1"], W=["fin1"])
                S.op("dve", lambda e, sq=sq: e.tensor_tensor(
                    out=OT[:, :, sq * 8:(sq + 1) * 8], in0=fin[:, 0, :].rearrange("p (a b) -> p a b", a=8),
                    in1=fin[:, 1, :].rearrange("p (a b) -> p a b", a=8), op=ALU.mult),
                    R=["fin0", "fin1"], W=[("OT", h) for h in range(8)])
        else:
            NC = (r0 + T) // 128
            S.dma("pool", SEL, cstb[0:32, 0:4096], W=KWB + ["SEL"])
            S.dma("pool", CMt, cstb[:, 4096:6144].rearrange("p (a b) -> p a b", a=4), W=["CMt"])
            S.dma("sp", KT_hist[:, :, r0:r0 + T].rearrange("h p t -> p h t"), ktt[:, :, 0:T], R=["ktt"], W=["kthist"])
            S.op("dve", lambda e: e.tensor_reduce(out=kmT[:, :, blk0:blk0 + 2],
                                                  in_=ktt[:, :, 0:T].rearrange("p h (b t) -> p h b t", b=2),
                                                  axis=AX.X, op=ALU.add), R=["ktt"], W=["kmT"])
            S.op("dve", lambda e: e.tensor_scalar(out=kmTb[:, :, blk0:blk0 + 2], in0=kmT[:, :, blk0:blk0 + 2],
                                                  scalar1=1.0 / 256, scalar2=None, op0=ALU.mult), R=["kmT"], W=["kmTb"])
            for m in range(nsub):
                own = blk0 + m // 2
                S.op("dve", lambda e: e.memset(biasq[:, :, :], -30000.0), W=["biasq"])
                if own > 0:
                    if own <= 3:
                        S.op("dve", lambda e, own=own: e.memset(biasq[:, :, 0:own], 0.0), W=["biasq"])
                    else:
                        S.op("dve", lambda e: e.memset(gsb[:, :, :], -1e30), W=["gsb"])
                        for h in range(8):
                            S.op("pe", lambda e, h=h, m=m, own=own: e.matmul(
                                P[6][:, h * 32:h * 32 + own], lhsT=QT[:, h, m * 128:(m + 1) * 128], rhs=kmTb[:, h, 0:own],
                                start=True, stop=True), R=["QT", "kmTb"], W=["P6"])
                        S.op("dve", lambda e, own=own: e.tensor_copy(
                            out=gsb[:, :, 0:own], in_=P[6][:, 0:256].rearrange("p (h n) -> p h n", h=8)[:, :, 0:own]),
                            R=["P6"], W=["gsb"])
                        for h in range(8):
                            S.op("dve", lambda e, h=h: e.max(out=mx[:, h, :], in_=gsb[:, h, :]), R=["gsb"], W=["mx"])
                        S.op("dve", lambda e, own=own: e.tensor_tensor(
                            out=biasq[:, :, 0:own], in0=gsb[:, :, 0:own], in1=mx[:, :, 2:3].to_broadcast([128, 8, own]),
                            op=ALU.is_ge), R=["gsb", "mx"], W=["biasq"])
                        S.op("dve", lambda e, own=own: e.tensor_scalar(
                            out=biasq[:, :, 0:own], in0=biasq[:, :, 0:own], scalar1=-1.0, scalar2=30000.0,
                            op0=ALU.add, op1=ALU.mult), R=["biasq"], W=["biasq"])
                S.op("dve", lambda e, own=own: e.memset(biasq[:, :, own:own + 1], 0.0), W=["biasq"])
                for g in range(2):
                    for hh in range(4):
                        h = g * 4 + hh
                        S.op("pe", lambda e, h=h, hh=hh: e.transpose(
                            out=P[7][0:32, hh * 128:(hh + 1) * 128], in_=biasq[:, h, :], identity=ident),
                            R=["biasq", "cs"], W=["P7"])
                    S.op("dve", lambda e, g=g, m=m: e.tensor_copy(
                        out=biasT[0:32, g * 4:(g + 1) * 4, m * 128:(m + 1) * 128],
                        in_=P[7][0:32, 0:512].rearrange("p (a b) -> p a b", a=4)), R=["P7"], W=["biasT"])
            for h in range(8):
                s2 = h % 2
                KTh = wAf[:, s2 * 8192:s2 * 8192 + NC * 128]
                Vh = wBf[:, s2 * 8192:s2 * 8192 + NC * 128].rearrange("p (c d) -> p c d", d=128)
                first = False
                S.dma("sp", KTh, KT_hist[h, :, 0:NC * 128], R=["kthist"], W=(KWA[s2] if h < 2 else []) + [("KTh", s2)])
                S.dma("sp", Vh, V_hist[0:NC * 128, h * 128:(h + 1) * 128].rearrange("(c p) d -> p c d", p=128),
                      R=[("vhist", mm) for mm in range(nsub)], W=(KWB if h < 2 else []) + [("Vh", s2)])
                RKT = KWA[s2] + [("KTh", s2)]
                RV = KWB + [("Vh", s2)]
                for c in range(NC):
                    n = c // 2
                    b2 = c % 2
                    pb, kb = P[c % 4], "P%d" % (c % 4)
                    own_tile = c >= NC - 4
                    S.op("pe", lambda e, c=c, pb=pb, h=h, KTh=KTh: e.matmul(
                        pb[:, 0:T], lhsT=KTh[:, c * 128:(c + 1) * 128], rhs=QT[:, h, 0:T], start=True, stop=False),
                        R=RKT + ["QT"], W=[kb])
                    S.op("pe", lambda e, n=n, pb=pb, h=h, own_tile=own_tile: e.matmul(
                        pb[:, 0:T], lhsT=SEL[0:32, n * 128:(n + 1) * 128], rhs=biasT[0:32, h, 0:T],
                        start=False, stop=(not own_tile)), R=["SEL", "biasT"], W=[kb])
                    if own_tile:
                        cc = c - (NC - 4)
                        S.op("pe", lambda e, cc=cc, pb=pb: e.matmul(
                            pb[:, 0:T], lhsT=ident_b[:, :], rhs=CMt[:, cc, 0:T], start=False, stop=True),
                            R=["ident_b", "CMt"], W=[kb])
                    S.op("act", lambda e, pb=pb, b2=b2: e.activation(out=PTf[:, b2, 0:T], in_=pb[:, 0:T], func=AF.Exp),
                         R=[kb], W=[("PTf", b2)])
                    S.op("dve", lambda e, b2=b2: e.tensor_copy(out=PT[:, b2, 0:T], in_=PTf[:, b2, 0:T]),
                         R=[("PTf", b2)], W=[("PT", b2)])
                    S.op("pe", lambda e, c=c, b2=b2, Vh=Vh: e.matmul(
                        P[4][:, 0:T], lhsT=Vh[:, c, :], rhs=PT[:, b2, 0:T], start=(c == 0), stop=(c == NC - 1)),
                        R=RV + [("PT", b2)], W=["P4"])
                    S.op("pe", lambda e, c=c, b2=b2: e.matmul(
                        P[5][:, 0:T], lhsT=ones_b1[:, :], rhs=PT[:, b2, 0:T], start=(c == 0), stop=(c == NC - 1)),
                        R=[("PT", b2), "ones_b1"], W=["P5"])
                S.op("dve", lambda e: e.reciprocal(out=rd[:, 0:T], in_=P[5][:, 0:T]), R=["P5"], W=["rd"])
                S.op("dve", lambda e, h=h: e.tensor_tensor(out=OT[:, h, 0:T], in0=P[4][:, 0:T], in1=rd[:, 0:T], op=ALU.mult),
                     R=["P4", "rd"], W=[("OT", h)])
        out_proj(moba_w_out, [OT[:, k, 0:T] for k in range(8)], [("OT", k) for k in range(8)], li, T)

    def ssd(li, T, r0, sample, last, pre=None):
        nsub = T // 128
        HK = [("hb", k) for k in range(8)]
        Um, SLm, NGm, ATm = (UB, SLB, cs[:, 512:640], SAME) if sample else (cs[:, 128:256], SLc, NEGM, onesf)
        if sample:
            pass
        for grp in range(4):
            slot = grp % 2
            if pre is not None and grp < 2:
                rk = pre[grp]
            else:
                rk = load_w(wA[:, slot], ssm_w_in[:, 2048 + grp * 1024:2048 + (grp + 1) * 1024], KWA[slot], "sx%d" % slot, WKEY("ssm_w_in"))
            if sample:
                S.dma("sp", xin[0:48, 0, :], state_conv[:, grp * 1024:(grp + 1) * 1024], W=["xin0"])
            for cc in range(8):
                ch = grp * 8 + cc
                b = ch % 2
                pb, kb = P[b], "P%d" % b
                for k in range(8):
                    S.op("pe", lambda e, k=k, cc=cc, pb=pb, slot=slot: e.matmul(
                        pb[:, 0:T], lhsT=wA[:, slot, k, cc * 128:(cc + 1) * 128], rhs=hb[:, k, 0:T],
                        start=(k == 0), stop=(k == 7)), R=rk + [("hb", k)], W=[kb])
                xk = ("PTf", b)
                if sample:
                    xp3 = xpad[:, b, 0:176].rearrange("p (s t) -> p s t", t=11)
                    S.op("pe", lambda e, cc=cc: e.transpose(out=P[2][:, 0:48], in_=xin[0:48, 0, cc * 128:(cc + 1) * 128],
                                                            identity=ident[0:48, 0:48]), R=["xin0", "cs"], W=["P2"])
                    S.op("dve", lambda e, xp3=xp3: e.tensor_copy(out=xp3[:, :, 0:3], in_=P[2][:, 0:48].rearrange("p (s t) -> p s t", t=3)),
                         R=["P2"], W=[xk])
                    S.op("act", lambda e, xp3=xp3, pb=pb: e.activation(out=xp3[:, :, 3:11], in_=pb[:, 0:128].rearrange("p (s t) -> p s t", t=8),
                                                                       func=AF.Identity), R=[kb], W=[xk])
                    views = [xp3[:, :, k:k + 8] for k in range(4)]
                    acc = rd[:, 0:128].rearrange("p (s t) -> p s t", t=8)
                else:
                    S.op("dve", lambda e, b=b, ch=ch: e.tensor_copy(out=xpad[:, b, 0:3], in_=halo[:, ch, :]), R=["halo"], W=[xk])
                    S.op("act", lambda e, b=b, pb=pb: e.activation(out=xpad[:, b, 3:3 + T], in_=pb[:, 0:T], func=AF.Identity),
                         R=[kb], W=[xk])
                    S.op("dve", lambda e, b=b, ch=ch: e.tensor_copy(out=halo[:, ch, :], in_=xpad[:, b, T:T + 3]), R=[xk], W=["halo"])
                    views = [xpad[:, b, k:k + T] for k in range(4)]
                    acc = rd[:, 0:T]
                S.op("dve", lambda e, acc=acc, v0=views[0], ch=ch: e.tensor_scalar(
                    out=acc, in0=v0, scalar1=wcv[:, ch:ch + 1], scalar2=None, op0=ALU.mult), R=[xk, "wcv"], W=["rd"])
                for k in range(1, 4):
                    S.op("dve", lambda e, acc=acc, vk=views[k], ch=ch, k=k: e.scalar_tensor_tensor(
                        out=acc, in0=vk, scalar=wcv[:, k * 32 + ch:k * 32 + ch + 1], in1=acc, op0=ALU.mult, op1=ALU.add),
                        R=[xk, "wcv", "rd"], W=["rd"])
                S.op("act", lambda e, ch=ch: e.activation(out=rd[:, 0:T], in_=rd[:, 0:T], func=AF.Silu, bias=bcv[:, ch:ch + 1]),
                     R=["rd", "bcv"], W=["rd"])
                if ch >= 16:
                    S.op("dve", lambda e, ch=ch: e.tensor_copy(out=BCt[:, ch - 16, 0:T], in_=rd[:, 0:T]), R=["rd"], W=[("BCt", ch - 16)])
                if ch < 24:
                    dst = xtok if ch < 16 else Btok
                    co = ch if ch < 16 else ch - 16
                    pb2, kb2 = P[2 + ch % 2], "P%d" % (2 + ch % 2)
                    for sub in range(nsub):
                        S.op("pe", lambda e, sub=sub, pb2=pb2: e.transpose(out=pb2[:, sub * 128:(sub + 1) * 128],
                                                                          in_=rd[:, sub * 128:(sub + 1) * 128], identity=ident),
                             R=["rd", "cs"], W=[kb2])
                    S.op("dve", lambda e, dst=dst, co=co, pb2=pb2: e.tensor_copy(
                        out=dst[:, 0:nsub, co * 128:(co + 1) * 128], in_=pb2[:, 0:T].rearrange("p (a b) -> p a b", b=128)),
                        R=[kb2], W=KWB + ["xtok"])
        if sample or last:
            sub = nsub - 1
            for grp in range(4):
                slot = grp % 2
                rk = load_w(wA[:, slot], ssm_w_in[:, 2048 + grp * 1024:2048 + (grp + 1) * 1024], KWA[slot], "sx%d" % slot, WKEY("ssm_w_in"))
                for n in range(2):
                    pb, kb = P[n], "P%d" % n
                    for k in range(8):
                        S.op("pe", lambda e, k=k, n=n, pb=pb, slot=slot, sub=sub: e.matmul(
                            pb[:, 0:512], lhsT=hb[:, k, sub * 128:(sub + 1) * 128], rhs=wA[:, slot, k, n * 512:(n + 1) * 512],
                            start=(k == 0), stop=(k == 7)), R=rk + [("hb", k)], W=[kb])
                    S.op("act", lambda e, n=n, pb=pb: e.activation(out=xin[:, 1, n * 512:(n + 1) * 512], in_=pb[:, 0:512], func=AF.Identity),
                         R=[kb], W=["xin1"])
                if sample:
                    for sq in range(NSEQ_S):
                        S.dma("sp", conv_s[sq * 3:(sq + 1) * 3, grp * 1024:(grp + 1) * 1024], xin[sq * 8 + 5:sq * 8 + 8, 1, :],
                              R=["xin1"], W=["out"])
                else:
                    S.dma("sp", conv_p[:, grp * 1024:(grp + 1) * 1024], xin[125:128, 1, :], R=["xin1"], W=["out"])
        rkz = [load_w(wA[:, q], ssm_w_in[:, q * 1024:(q + 1) * 1024], KWA[q], "sz%d" % q, WKEY("ssm_w_in")) for q in range(2)]
        S.dma("pool", wcvdt[:, :, :], ssm_w_in[:, 6144:6176].rearrange("(k p) n -> p k n", p=128), R=WKEY("ssm_w_in"), W=["wdt"])
        for sub in range(nsub):
            for q in range(2):
                for n in range(2):
                    pb, kb = P[n], "P%d" % n
                    for k in range(8):
                        S.op("pe", lambda e, k=k, n=n, q=q, pb=pb, sub=sub: e.matmul(
                            pb[:, 0:512], lhsT=hb[:, k, sub * 128:(sub + 1) * 128], rhs=wA[:, q, k, n * 512:(n + 1) * 512],
                            start=(k == 0), stop=(k == 7)), R=rkz[q] + [("hb", k)], W=[kb])
                    S.op("act", lambda e, n=n, pb=pb: e.activation(out=tmp[:, n, :], in_=pb[:, 0:512], func=AF.Silu),
                         R=[kb], W=[("tmp", n)])
                    S.op("dve", lambda e, n=n, q=q, sub=sub: e.tensor_copy(
                        out=zs[:, sub, q * 1024 + n * 512:q * 1024 + (n + 1) * 512], in_=tmp[:, n, :]),
                        R=[("tmp", n)], W=KWB + ["zs"])
            for k in range(8):
                S.op("pe", lambda e, k=k, sub=sub: e.matmul(P[7][:, 0:32], lhsT=hb[:, k, sub * 128:(sub + 1) * 128], rhs=wcvdt[:, k, :],
                                                            start=(k == 0), stop=(k == 7)), R=["wdt", ("hb", k)], W=["P7"])
            S.op("dve", lambda e: e.tensor_tensor(out=sm[:, 5, :], in0=P[7][:, 0:32], in1=prm[:, 0, :], op=ALU.add), R=["P7", "prm"], W=["sm5"])
            S.op("dve", lambda e: e.tensor_scalar(out=sm[:, 6, :], in0=sm[:, 5, :], scalar1=-1.0, scalar2=None, op0=ALU.mult),
                 R=["sm5"], W=["sm6"])
            S.op("dve", lambda e: e.tensor_tensor(out=sm[:, 6, :], in0=sm[:, 6, :], in1=sm[:, 5, :], op=ALU.max),
                 R=["sm5", "sm6"], W=["sm6"])
            S.op("act", lambda e: e.activation(out=sm[:, 6, :], in_=sm[:, 6, :], func=AF.Exp, scale=-1.0), R=["sm6"], W=["sm6"])
            S.op("act", lambda e: e.activation(out=sm[:, 6, :], in_=sm[:, 6, :], func=AF.Ln, bias=cs[:, 641:642]), R=["sm6", "cs"], W=["sm6"])
            S.op("dve", lambda e: e.tensor_scalar(out=sm[:, 5, :], in0=sm[:, 5, :], scalar1=0.0, scalar2=None, op0=ALU.max),
                 R=["sm5"], W=["sm5"])
            S.op("dve", lambda e, sub=sub: e.tensor_tensor(out=dts[:, sub, :], in0=sm[:, 5, :], in1=sm[:, 6, :], op=ALU.add),
                 R=["sm5", "sm6"], W=["dts"])
        ngv = statf[:, 0:2048]
        S.dma("sp", ngv, bass.AP(tensor=ssm_norm_g.tensor, offset=0, ap=[[0, 128], [1, 2048]]), W=["mean", "msq", "rstd", "cG", "cB"])
        HBK = HK
        snat = wAf32[:, 0:2048].rearrange("p (c n) -> p c n", c=16)
        sT = wAf32[:, 2048:4096]
        sTb = wAf[:, 8192:10240]
        CmT = wAf[:, 10240:11264].rearrange("p (g t) -> p g t", g=8)
        xdm = wAf[:, 11264:13312]
        Esq = wAf32[:, 6656:6784]
        etq = wAf32[:, 6784:6816]
        WAK = KWA[0] + KWA[1]

        def ssd_sample_states(csl):
            S.op("dve", lambda e: e.memset(zb[:, 0, :], 0.0), W=[("zb", 0)])
            for sq in range(NSEQ_S):
                S.dma("sp", snat, state_ssm[sq].rearrange("(c p) n -> p c n", p=128), W=(WAK if sq == 0 else []) + ["snat"])
                for c4 in range(4):
                    pb, kb = P[4 + c4], "P%d" % (4 + c4)
                    for cc in range(4):
                        c = c4 * 4 + cc
                        S.op("pe", lambda e, c=c, cc=cc, pb=pb: e.transpose(out=pb[:, cc * 128:(cc + 1) * 128], in_=snat[:, c, :], identity=ident),
                             R=WAK + ["snat", "cs"], W=[kb])
                    S.op("dve", lambda e, c4=c4, pb=pb: e.tensor_copy(out=sT[:, c4 * 512:(c4 + 1) * 512], in_=pb[:, 0:512]), R=[kb], W=["sT"])
                    S.op("act", lambda e, c4=c4: e.activation(out=sTb[:, c4 * 512:(c4 + 1) * 512], in_=sT[:, c4 * 512:(c4 + 1) * 512],
                                                              func=AF.Identity), R=["sT"], W=["sTb"])
                S.op("dve", lambda e: e.memset(CmT, 0.0), W=["CmT"])
                S.op("dve", lambda e, sq=sq: e.tensor_copy(out=CmT[:, :, sq * 8:(sq + 1) * 8], in_=BCt[:, 8:16, sq * 8:(sq + 1) * 8]),
                     R=[("BCt", 8 + g) for g in range(8)], W=["CmT"])
                for g in range(8):
                    if sq == 0 and g % 2 == 0:
                        S.op("pe", lambda e, g=g: e.matmul(P[g // 2][:, 0:512], lhsT=ones_b1[:, :], rhs=zb[:, 0, :], start=True, stop=False),
                             R=[("zb", 0), "ones_b1"], W=["P%d" % (g // 2)])
                    S.op("pe", lambda e, g=g, sq=sq: e.matmul(P[g // 2][:, (g % 2) * 256:(g % 2 + 1) * 256], lhsT=CmT[:, g, :],
                                                              rhs=sTb[:, g * 256:(g + 1) * 256], start=False,
                                                              stop=(sq == NSEQ_S - 1 and g % 2 == 1)), R=["CmT", "sTb"], W=["P%d" % (g // 2)])
                S.op("dve", lambda e, sq=sq: e.tensor_scalar(out=Esq, in0=onesf, scalar1=lastm[:, sq:sq + 1], scalar2=None, op0=ALU.mult),
                     R=["cs", "cs2"], W=["Esq"])
                S.op("dve", lambda e, sq=sq: e.tensor_scalar(out=xdm, in0=xdtd, scalar1=seqm[:, sq:sq + 1], scalar2=None, op0=ALU.mult),
                     R=["xdtd", "cs2"], W=["xdm"])
                S.op("pe", lambda e: e.matmul(P[7][:, 0:32], lhsT=Esq, rhs=sm[:, 1, :], start=True, stop=True), R=["Esq", "acs"], W=["P7"])
                S.op("act", lambda e: e.activation(out=etq, in_=P[7][:, 0:32], func=AF.Exp), R=["P7"], W=["etq"])
                for g in range(8):
                    S.op("pe", lambda e, g=g: e.matmul(P[4 + g // 2][:, (g % 2) * 256:(g % 2 + 1) * 256], lhsT=Btok[:, 0, g * 128:(g + 1) * 128],
                                                       rhs=xdm[:, g * 256:(g + 1) * 256], start=True, stop=True),
                         R=KWB + ["xtok", "xdm"], W=["P%d" % (4 + g // 2)])
                for q in range(4):
                    sv = sT[:, q * 512:(q + 1) * 512].rearrange("p (h d) -> p h d", h=8)
                    S.op("dve", lambda e, sv=sv, q=q: e.tensor_tensor(out=sv, in0=sv, in1=etq[:, q * 8:(q + 1) * 8].unsqueeze(2).to_broadcast([128, 8, 64]),
                                                                    op=ALU.mult), R=["etq", "sT"], W=["sT"])
                    S.op("dve", lambda e, q=q: e.tensor_tensor(out=sT[:, q * 512:(q + 1) * 512], in0=sT[:, q * 512:(q + 1) * 512],
                                                               in1=P[4 + q][:, 0:512], op=ALU.add), R=["P%d" % (4 + q), "sT"], W=["sT"])
                for c4 in range(4):
                    pb, kb = P[4 + c4], "P%d" % (4 + c4)
                    for cc in range(4):
                        c = c4 * 4 + cc
                        S.op("pe", lambda e, c=c, cc=cc, pb=pb: e.transpose(out=pb[:, cc * 128:(cc + 1) * 128], in_=sT[:, c * 128:(c + 1) * 128],
                                                                           identity=ident), R=["sT", "cs"], W=[kb])
                    S.op("dve", lambda e, c4=c4, pb=pb: e.tensor_copy(out=snat[:, c4 * 4:(c4 + 1) * 4, :],
                                                                     in_=pb[:, 0:512].rearrange("p (a b) -> p a b", a=4)), R=[kb], W=["snat"])
                S.dma("sp", ssm_s[sq].rearrange("(c p) n -> p c n", p=128), snat, R=["snat"], W=["out"])
            S.op("dve", lambda e: e.memset(etq, 0.0), W=WAK + ["sT", "sTb", "CmT", "xdm", "Esq", "etq", "snat"])
            for q in range(4):
                S.op("dve", lambda e, q=q: e.tensor_tensor(
                    out=rd[:, 0:512].rearrange("p (h d) -> p h d", h=8), in0=P[q][:, 0:512].rearrange("p (h d) -> p h d", h=8),
                    in1=sm[:, 2, q * 8:(q + 1) * 8].unsqueeze(2).to_broadcast([128, 8, 64]), op=ALU.mult),
                    R=["P%d" % q, "eacs"], W=["rd"])
                S.op("dve", lambda e, q=q: e.tensor_tensor(out=yv[:, q * 512:(q + 1) * 512], in0=yv[:, q * 512:(q + 1) * 512], in1=rd[:, 0:512],
                                                           op=ALU.add), R=["rd", "xin0", "xin1"], W=["xin0", "xin1"])
        for sub in range(nsub):
            csl = slice(sub * 128, (sub + 1) * 128)
            S.op("dve", lambda e, sub=sub: e.tensor_tensor(out=sm[:, 0, :], in0=dts[:, sub, :], in1=prm[:, 1, :], op=ALU.mult),
                 R=["dts", "prm"], W=["adt"])
            S.op("pe", lambda e: e.matmul(P[7][:, 0:32], lhsT=Um, rhs=sm[:, 0, :], start=True, stop=True), R=["adt", "cs", "cs2"], W=["P7"])
            S.op("pe", lambda e: e.matmul(P[7][:, 32:64], lhsT=ATm, rhs=sm[:, 0, :], start=True, stop=True), R=["adt", "cs", "cs2"], W=["P7"])
            S.op("dve", lambda e: e.tensor_copy(out=sm[:, 1, :], in_=P[7][:, 0:32]), R=["P7"], W=["acs"])
            S.op("act", lambda e: e.activation(out=sm[:, 2, :], in_=P[7][:, 0:32], func=AF.Exp), R=["P7"], W=["eacs"])
            S.op("dve", lambda e: e.tensor_tensor(out=sm[:, 3, :], in0=P[7][:, 32:64], in1=sm[:, 1, :], op=ALU.subtract),
                 R=["P7", "acs"], W=["dec"])
            S.op("act", lambda e: e.activation(out=sm[:, 3, :], in_=sm[:, 3, :], func=AF.Exp), R=["dec"], W=["dec"])
            S.op("act", lambda e: e.activation(out=sm[:, 4, :], in_=P[7][:, 32:64], func=AF.Exp), R=["P7"], W=["etot"])
            xv = xtok[:, sub, :].rearrange("p (h d) -> p h d", h=32)
            S.op("dve", lambda e, xv=xv, sub=sub: e.tensor_tensor(
                out=xdt.rearrange("p (h d) -> p h d", h=32), in0=xv, in1=dts[:, sub, :].unsqueeze(2).to_broadcast([128, 32, 64]),
                op=ALU.mult), R=KWB + ["xtok", "dts"], W=HBK + ["xdt"])
            S.op("dve", lambda e: e.tensor_tensor(
                out=xdtd.rearrange("p (h d) -> p h d", h=32), in0=xdt.rearrange("p (h d) -> p h d", h=32),
                in1=sm[:, 3, :].unsqueeze(2).to_broadcast([128, 32, 64]), op=ALU.mult), R=["xdt", "dec"], W=HBK + ["xdtd"])
            for g in range(8):
                S.op("pe", lambda e, g=g, csl=csl: e.matmul(P[g // 4][:, (g % 4) * 128:(g % 4 + 1) * 128], lhsT=BCt[:, g, csl],
                                                            rhs=BCt[:, 8 + g, csl], start=True, stop=True),
                     R=[("BCt", g), ("BCt", 8 + g)], W=["P%d" % (g // 4)])
            for q in range(2):
                S.op("dve", lambda e, q=q: e.tensor_copy(out=CBt[:, q * 4:(q + 1) * 4, :], in_=P[q][:, 0:512].rearrange("p (a b) -> p a b", a=4)),
                     R=["P%d" % q], W=["CBt", ("PTf", 0), ("PTf", 1)])
            for hf in range(2):
                for hh in range(16):
                    h = hf * 16 + hh
                    b = h % 4
                    trh = tmp[:, b // 2, (b % 2) * 256:(b % 2) * 256 + 128]
                    tL = tmp[:, b // 2, (b % 2) * 256 + 128:(b % 2) * 256 + 256]
                    mt = MTf[:, b * 128:(b + 1) * 128]
                    S.op("dve", lambda e, h=h, trh=trh: e.tensor_scalar(out=trh, in0=SLm, scalar1=sm[:, 0, h:h + 1], scalar2=None,
                                                                        op0=ALU.mult), R=["adt", "cs2"], W=[("ssdt", b)])
                    S.op("pe", lambda e, b=b, trh=trh: e.matmul(P[4 + b][:, 0:128], lhsT=trh, rhs=Um, start=True, stop=False),
                         R=[("ssdt", b), "cs", "cs2"], W=["P%d" % (4 + b)])
                    S.op("pe", lambda e, b=b: e.matmul(P[4 + b][:, 0:128], lhsT=ident, rhs=NGm, start=False, stop=True),
                         R=["cs", "cs2"], W=["P%d" % (4 + b)])
                    S.op("act", lambda e, b=b, tL=tL: e.activation(out=tL, in_=P[4 + b][:, 0:128], func=AF.Exp),
                         R=["P%d" % (4 + b)], W=[("ssdL", b)])
                    S.op("dve", lambda e, b=b, h=h, tL=tL, mt=mt: e.tensor_tensor(out=mt, in0=tL, in1=CBt[:, h // 4, :], op=ALU.mult),
                         R=[("ssdL", b), "CBt"], W=[("MT", b)])
                    S.op("pe", lambda e, b=b, h=h, hh=hh, mt=mt: e.matmul(P[hh // 8][:, (hh % 8) * 64:(hh % 8 + 1) * 64], lhsT=mt,
                                                                        rhs=xdt[:, h * 64:(h + 1) * 64], start=True, stop=True),
                         R=[("MT", b), "xdt"], W=["P%d" % (hh // 8)])
                for q in range(2):
                    S.op("dve", lambda e, q=q, hf=hf: e.tensor_copy(out=yv[:, hf * 1024 + q * 512:hf * 1024 + (q + 1) * 512], in_=P[q][:, 0:512]),
                         R=["P%d" % q], W=["xin%d" % hf])
                if not sample:
                    for gi in range(4):
                        g = hf * 4 + gi
                        S.op("pe", lambda e, g=g, gi=gi, csl=csl: e.matmul(P[2 + gi // 2][:, (gi % 2) * 256:(gi % 2 + 1) * 256], lhsT=BCt[:, 8 + g, csl],
                                                                         rhs=stateTb[:, g * 256:(g + 1) * 256], start=True, stop=True),
                             R=[("BCt", 8 + g), "stateTb"], W=["P%d" % (2 + gi // 2)])
                    for q in range(2):
                        c0 = hf * 1024 + q * 512
                        S.op("dve", lambda e, q=q, hf=hf, c0=c0: e.tensor_tensor(
                            out=rd[:, 0:512].rearrange("p (h d) -> p h d", h=8), in0=P[2 + q][:, 0:512].rearrange("p (h d) -> p h d", h=8),
                            in1=sm[:, 2, hf * 16 + q * 8:hf * 16 + (q + 1) * 8].unsqueeze(2).to_broadcast([128, 8, 64]), op=ALU.mult),
                            R=["P%d" % (2 + q), "eacs"], W=["rd"])
                        S.op("dve", lambda e, c0=c0: e.tensor_tensor(out=yv[:, c0:c0 + 512], in0=yv[:, c0:c0 + 512], in1=rd[:, 0:512], op=ALU.add),
                             R=["rd", "xin%d" % hf], W=["xin%d" % hf])
                    for gi in range(4):
                        g = hf * 4 + gi
                        S.op("pe", lambda e, g=g, gi=gi, sub=sub: e.matmul(P[6 + gi // 2][:, (gi % 2) * 256:(gi % 2 + 1) * 256],
                                                                         lhsT=Btok[:, sub, g * 128:(g + 1) * 128], rhs=xdtd[:, g * 256:(g + 1) * 256],
                                                                         start=True, stop=True), R=KWB + ["xtok", "xdtd"], W=["P%d" % (6 + gi // 2)])
                    for q in range(2):
                        c0 = hf * 1024 + q * 512
                        sv = stateT[:, c0:c0 + 512].rearrange("p (h d) -> p h d", h=8)
                        S.op("dve", lambda e, sv=sv, hf=hf, q=q: e.tensor_tensor(
                            out=sv, in0=sv, in1=sm[:, 4, hf * 16 + q * 8:hf * 16 + (q + 1) * 8].unsqueeze(2).to_broadcast([128, 8, 64]),
                            op=ALU.mult), R=["etot", "stateT"], W=["stateT"])
                        S.op("dve", lambda e, c0=c0, q=q: e.tensor_tensor(out=stateT[:, c0:c0 + 512], in0=stateT[:, c0:c0 + 512],
                                                                        in1=P[6 + q][:, 0:512], op=ALU.add), R=["P%d" % (6 + q), "stateT"], W=["stateT"])
                        S.op("act", lambda e, c0=c0: e.activation(out=stateTb[:, c0:c0 + 512], in_=stateT[:, c0:c0 + 512], func=AF.Identity),
                             R=["stateT"], W=["stateTb"])
            if sample:
                ssd_sample_states(csl)
            xv = xtok[:, sub, :].rearrange("p (h d) -> p h d", h=32)
            S.op("dve", lambda e, xv=xv: e.tensor_tensor(out=ysq.rearrange("p (h d) -> p h d", h=32), in0=xv,
                                                         in1=prm[:, 2, :].unsqueeze(2).to_broadcast([128, 32, 64]), op=ALU.mult),
                 R=KWB + ["xtok", "prm"], W=HBK + ["xdt", "xdtd", "ysq"])
            S.op("dve", lambda e: e.tensor_tensor(out=yv, in0=yv, in1=ysq, op=ALU.add), R=["ysq", "xin0", "xin1"], W=["xin0", "xin1"])
            S.op("dve", lambda e, sub=sub: e.tensor_tensor(out=yv, in0=yv, in1=zs[:, sub, :], op=ALU.mult), R=KWB + ["zs", "xin0", "xin1"],
                 W=["xin0", "xin1"])
            S.op("dve", lambda e: e.tensor_tensor(out=ysq, in0=yv, in1=yv, op=ALU.mult), R=["xin0", "xin1"], W=["ysq"])
            S.op("dve", lambda e: e.tensor_reduce(out=sm[:, 7, 0:8], in_=ysq.rearrange("p (g d) -> p g d", g=8), axis=AX.X, op=ALU.add),
                 R=["ysq"], W=["sm7"])
            S.op("act", lambda e: e.activation(out=sm[:, 7, 0:8], in_=sm[:, 7, 0:8], func=AF.Ln, bias=epsb[:, 0:1], scale=1.0 / 256),
                 R=["sm7", "epsb"], W=["sm7"])
            S.op("act", lambda e: e.activation(out=sm[:, 7, 0:8], in_=sm[:, 7, 0:8], func=AF.Exp, scale=-0.5), R=["sm7"], W=["sm7"])
            S.op("dve", lambda e: e.tensor_tensor(out=yv.rearrange("p (g d) -> p g d", g=8), in0=yv.rearrange("p (g d) -> p g d", g=8),
                                                  in1=sm[:, 7, 0:8].unsqueeze(2).to_broadcast([128, 8, 256]), op=ALU.mult),
                 R=["sm7", "xin0", "xin1"], W=["xin0", "xin1"])
            S.op("dve", lambda e: e.tensor_tensor(out=yv, in0=yv, in1=ngv, op=ALU.mult), R=["mean", "xin0", "xin1"], W=["xin0", "xin1"])
            for c4 in range(4):
                pb, kb = P[c4 % 2], "P%d" % (c4 % 2)
                for cc in range(4):
                    c = c4 * 4 + cc
                    S.op("pe", lambda e, c=c, cc=cc, pb=pb: e.transpose(out=pb[:, cc * 128:(cc + 1) * 128], in_=yv[:, c * 128:(c + 1) * 128],
                                                                       identity=ident), R=["xin0", "xin1", "cs"], W=[kb])
                S.op("dve", lambda e, c4=c4, pb=pb, sub=sub: e.tensor_copy(
                    out=ynT[:, c4 * 4:(c4 + 1) * 4, sub * 128:(sub + 1) * 128], in_=pb[:, 0:512].rearrange("p (a b) -> p a b", a=4)),
                    R=[kb], W=[("aT", j) for j in range(16)])
        if last and not sample:
            for c4 in range(4):
                pb, kb = P[c4 % 2], "P%d" % (c4 % 2)
                for cc in range(4):
                    c = c4 * 4 + cc
                    S.op("pe", lambda e, c=c, cc=cc, pb=pb: e.transpose(out=pb[:, cc * 128:(cc + 1) * 128], in_=stateT[:, c * 128:(c + 1) * 128],
                                                                       identity=ident), R=["stateT", "cs"], W=[kb])
                S.op("dve", lambda e, pb=pb: e.tensor_copy(out=yv[:, 0:512], in_=pb[:, 0:512]), R=[kb], W=["xin0"])
                S.dma("sp", ssm_p[c4 * 512:(c4 + 1) * 512, :].rearrange("(a p) n -> p a n", p=128),
                      yv[:, 0:512].rearrange("p (a n) -> p a n", a=4), R=["xin0"], W=["out"])
        rko = [load_w(wA[:, q], ssm_w_out[q * 1024:(q + 1) * 1024, :], KWA[q], "so%d" % q, WKEY("ssm_w_out")) for q in range(2)]
        for c in range(8):
            po, ko = P[6 + c % 2], "P%d" % (6 + c % 2)
            for k in range(16):
                S.op("pe", lambda e, k=k, c=c, po=po: e.matmul(
                    po[:, 0:T], lhsT=wA[:, k // 8, k % 8, c * 128:(c + 1) * 128], rhs=ynT[:, k, 0:T],
                    start=(k == 0), stop=(k == 15)), R=rko[k // 8] + [("aT", k)], W=[ko])
            S.op("dve", lambda e, c=c, po=po: e.scalar_tensor_tensor(
                out=hT[:, c, 0:T], in0=hT[:, c, 0:T], scalar=ALPHA, in1=po[:, 0:T],
                op0=ALU.mult, op1=ALU.add), R=[ko, ("hT", c)], W=[("hT", c)])
        layer_norm(li, T)

    def run_tile(src, dst, r0, T):
        load_tile(src, r0, T)
        only = os.environ.get("MK_ONLY", "")
        for L in ([int(only)] if only else range(LAYERS)):
            pre = None
            if os.environ.get("MK_FFN", "1") != "0":
                pf = None
                if MIX and os.environ.get("MK_PREF", "1") == "1":
                    pf = (cmlp_prefetch(L // 3), moba_prefetch, ssd_prefetch)[L % 3]
                pre = ffn(L * 2 + 0, L * 3 + 0, T, prefetch=pf)
            if MIX and L % 3 == 0:
                cmlp(L // 3, L * 3 + 1, T, sample=(T == 128), pre=pre)
            if MIX and L % 3 == 1:
                moba(L * 3 + 1, T, r0, sample=(T == 128), pre=pre)
            if MIX and L % 3 == 2:
                ssd(L * 3 + 1, T, r0, sample=(T == 128), last=(r0 + T == NT * 512), pre=pre)
            if os.environ.get("MK_FFN", "1") != "0":
                ffn(L * 2 + 1, L * 3 + 2, T)
        store_tile(dst, r0, T)

    for it in range(NT):
        run_tile(xp, y_p, it * 512, 512)
    if SAMPLE:
        run_tile(xs, y_s, 0, 128)

    S.finish(["out"])
    with nc.Block() as block:
        S.replay(block)
    st.close()
    print("instructions (incl waits):", S.n_ins)
    return nc


def make_consts():
    c = np.zeros((128, 1024), np.float32)
    c[:, 0:128] = np.eye(128, dtype=np.float32)
    i = np.arange(128)
    c[:, 128:256] = (i[:, None] <= i[None, :]).astype(np.float32)
    c[:, 256:384] = (i[None, :] <= i[:, None]).astype(np.float32)
    c[:, 384:512] = ((i[:, None] // 8 == i[None, :] // 8) & (i[None, :] % 8 <= i[:, None] % 8)).astype(np.float32)
    c[:, 512:640] = np.where((i[:, None] // 8 == i[None, :] // 8) & (i[:, None] % 8 <= i[None, :] % 8), 0.0, -30000.0)
    c[:, 640] = i.astype(np.float32)
    c[:, 641:769] = 1.0
    return c


def make_cstb():
    c = np.zeros((128, 6144), np.float32)
    for n in range(32):
        c[n, n * 128:(n + 1) * 128] = 1.0
    key = np.arange(128)[:, None]
    q = np.arange(512)[None, :]
    for cc in range(4):
        kp = cc * 128 + key
        same = (kp // 256) == (q // 256)
        c[:, 4096 + cc * 512:4096 + (cc + 1) * 512] = np.where(same & (kp > q), -30000.0, 0.0)
    return c


def make_consts2():
    c = np.zeros((128, 768), np.float32)
    i = np.arange(128)
    same = (i[:, None] // 8) == (i[None, :] // 8)
    c[:, 0:128] = (i[:, None] > i[None, :])
    c[:, 128:256] = np.where(i[:, None] <= i[None, :], 0.0, -30000.0)
    c[:, 256:384] = same & (i[:, None] <= i[None, :])
    c[:, 384:512] = same & (i[:, None] > i[None, :])
    c[:, 512:640] = same
    c[:, 640:656] = (i[:, None] // 8) == np.arange(16)[None, :]
    c[:, 656:672] = i[:, None] == (np.arange(16)[None, :] * 8 + 7)
    return c


def make_rope():
    pos = np.concatenate([np.arange(SEQ), 2048 + (np.arange(128) % 8)]).astype(np.float32)
    inv = (10000.0 ** (-np.arange(64, dtype=np.float32) / 64)).astype(np.float32)
    ang = (pos[:, None] * inv[None, :]).astype(np.float32)
    return np.concatenate([np.cos(ang), np.sin(ang)], axis=1).astype(np.float32)


def kernel(**inp):
    NT = int(os.environ.get("MK_NT", "16"))
    LAYERS = int(os.environ.get("MK_LAYERS", "4"))
    nc = build(NT=NT, LAYERS=LAYERS, SAMPLE=os.environ.get("MK_SAMPLE", "1") == "1")
    cst = make_consts()
    cstb = make_cstb()
    ropec = make_rope()
    cst2 = make_consts2()
    sel8d = np.zeros((8, 1024), np.float32)
    for n in range(8):
        sel8d[n, n * 128:(n + 1) * 128] = 1.0
    ck = np.ascontiguousarray(np.asarray(inp["cache_k"])[0]).reshape(2560 * 128, D)
    cvv = np.ascontiguousarray(np.asarray(inp["cache_v"])[0]).reshape(2560 * 128, D)
    f = lambda a: np.ascontiguousarray(np.asarray(a))
    in_maps = []
    for c in range(8):
        m = {
            "xp": f(inp["x_prompt"][c % 2]),
            "xs": f(inp["x_sample"][c * 16:(c + 1) * 16].reshape(128, D)),
            "ln_g": f(inp["ln_g"].reshape(12, D)),
            "ln_b": f(inp["ln_b"].reshape(12, D)),
            "ffn_w_in": f(inp["ffn_w_in"].reshape(8, D, 2 * DFF)),
            "ffn_w_out": f(inp["ffn_w_out"].reshape(8, DFF, D)),
            "cst": cst,
        }
        for k in ("cmlp_w_in", "cmlp_ln_g", "cmlp_ln_b", "cmlp_w_s", "cmlp_b_s", "cmlp_w_out"):
            m[k] = f(inp[k])
        m["moba_w_qkv"] = f(inp["moba_w_qkv"][0])
        m["moba_w_out"] = f(inp["moba_w_out"][0])
        m["ropec"] = ropec
        m["cstb"] = cstb
        m["sel8d"] = sel8d
        m["cache_k"] = ck
        m["cache_v"] = cvv
        m["page_table"] = f(inp["page_table"][c * 16:(c + 1) * 16]).astype(np.int32)
        m["ssm_w_in"] = f(inp["ssm_w_in"][0])
        m["ssm_w_conv"] = f(inp["ssm_w_conv"][0])
        m["ssm_b_conv"] = f(inp["ssm_b_conv"][0]).reshape(1, 4096)
        m["ssm_dt_bias"] = f(inp["ssm_dt_bias"][0]).reshape(1, 32)
        m["ssm_a_log"] = f(inp["ssm_a_log"][0]).reshape(1, 32)
        m["ssm_d"] = f(inp["ssm_d"][0]).reshape(1, 32)
        m["ssm_norm_g"] = f(inp["ssm_norm_g"][0]).reshape(1, 2048)
        m["ssm_w_out"] = f(inp["ssm_w_out"][0])
        m["state_conv"] = f(inp["state_conv"][0, c * 16:(c + 1) * 16]).reshape(48, 4096)
        m["state_ssm"] = f(inp["state_ssm"][0, c * 16:(c + 1) * 16]).reshape(16, 2048, 128)
        m["cst2"] = cst2
        in_maps.append(m)
    res = run_bass_kernel_spmd(nc, in_maps, core_ids=list(range(8)))
    R = res.results
    y_prompt = np.stack([R[0]["y_p"], R[1]["y_p"]]).astype(np.float32)
    y_sample = np.concatenate([np.asarray(R[c]["y_s"]).reshape(16, 8, D) for c in range(8)], axis=0).astype(np.float32)
    cv = np.stack([np.concatenate([np.asarray(R[c]["cv_s"]).reshape(2, 16, 8, D)[j] for c in range(8)], axis=0)
                   for j in range(2)]).astype(np.float32)
    kp = np.stack([np.asarray(R[c]["k_p"]).reshape(SEQ, 8, 128) for c in range(2)])[None].astype(np.float32)
    vp = np.stack([np.asarray(R[c]["v_p"]).reshape(SEQ, 8, 128) for c in range(2)])[None].astype(np.float32)
    ks = np.concatenate([np.asarray(R[c]["k_s"]).reshape(16, 8, 8, 128) for c in range(8)], axis=0)[None].astype(np.float32)
    vs = np.concatenate([np.asarray(R[c]["v_s"]).reshape(16, 8, 8, 128) for c in range(8)], axis=0)[None].astype(np.float32)
    conv_p = np.stack([np.asarray(R[c]["conv_p"]).reshape(3, 4096) for c in range(2)])[None].astype(np.float32)
    ssm_p = np.stack([np.asarray(R[c]["ssm_p"]).reshape(32, 64, 128) for c in range(2)])[None].astype(np.float32)
    conv_s = np.concatenate([np.asarray(R[c]["conv_s"]).reshape(16, 3, 4096) for c in range(8)], axis=0)[None].astype(np.float32)
    ssm_s = np.concatenate([np.asarray(R[c]["ssm_s"]).reshape(16, 32, 64, 128) for c in range(8)], axis=0)[None].astype(np.float32)
    return (y_prompt, y_sample, cv, kp, vp, ks, vs, conv_p, ssm_p, conv_s, ssm_s)
```
